# Optimizing a Trainium2 kernel written in Bass

```python
import jax, jax.numpy as jnp
from jax import lax
import numpy as np

D_MODEL = 2048
BATCH = 4
SEQ = 2048
DEPTH = 1
DEC_BATCH = 128
DEC_SEQ = 4
PAST_LEN = 16384
PAGE_SIZE = 128

POOL_WINDOWS = (2, 4, 8, 16)
N_POOL_GROUPS = len(POOL_WINDOWS)
D_POOL = D_MODEL // 2
POOL_GROUP = D_POOL // N_POOL_GROUPS
POOL_BUF = max(POOL_WINDOWS) - 1
N_HEADS = 4
HEAD_QK = D_MODEL // (2 * N_HEADS)
HEAD_V = D_MODEL // N_HEADS
MLSTM_CHUNK = 64
D_FF = ((8 * D_MODEL // 3 + 255) // 256) * 256
PLE_DIM = 256
EPS = 1e-6

kernel_name = 'hybrid_pool_mlstm_decoder_step'


def _rmsnorm(x, g):
    xf = x.astype(jnp.float32)
    y = xf * lax.rsqrt(jnp.mean(xf * xf, axis=-1, keepdims=True) + EPS)
    return (y * g.astype(jnp.float32)).astype(x.dtype)


def _pool_mix(u, buf, start_pos, w_grp, s_pool):
    B, S, _ = u.shape
    ext = jnp.concatenate([buf.astype(u.dtype), u], axis=1)
    extf = ext.astype(jnp.float32)
    cs = jnp.concatenate([jnp.zeros((B, 1, D_POOL), jnp.float32), jnp.cumsum(extf, axis=1)], axis=1)
    pos = start_pos + jnp.arange(S)
    hi = cs[:, POOL_BUF + 1:POOL_BUF + 1 + S]
    means = []
    for g, w in enumerate(POOL_WINDOWS):
        sl = slice(g * POOL_GROUP, (g + 1) * POOL_GROUP)
        lo = cs[:, POOL_BUF + 1 - w:POOL_BUF + 1 - w + S, sl]
        cnt = jnp.minimum(pos + 1, w).astype(jnp.float32)
        means.append((hi[:, :, sl] - lo) / cnt[None, :, None])
    z = jnp.concatenate(means, axis=-1) - u.astype(jnp.float32)
    z = z.reshape(B, S, N_POOL_GROUPS, POOL_GROUP)
    y = jnp.einsum('bsgc,gcd->bsgd', z, w_grp.astype(jnp.float32)).reshape(B, S, D_POOL)
    y = (y * s_pool.astype(jnp.float32)).astype(u.dtype)
    return y, ext[:, -POOL_BUF:].astype(buf.dtype)


def _mlstm_chunk_step(carry, xs):
    C, n, m = carry
    q, k, v, ig, lf = xs
    L = q.shape[2]
    b = jnp.cumsum(lf, axis=-1)
    causal = jnp.tril(jnp.ones((L, L), bool))
    Dm = jnp.where(causal, b[..., :, None] - b[..., None, :] + ig[..., None, :], -jnp.inf)
    a = b + m[..., None]
    m_t = jnp.maximum(a, jnp.max(Dm, axis=-1))
    inter = jnp.exp(a - m_t)
    qk = jnp.einsum('bhtd,bhjd->bhtj', q, k) * jnp.exp(Dm - m_t[..., None])
    num = inter[..., None] * jnp.einsum('bhtd,bhde->bhte', q, C) + jnp.einsum('bhtj,bhje->bhte', qk, v)
    den = inter * jnp.einsum('bhtd,bhd->bht', q, n) + jnp.sum(qk, axis=-1)
    h = num / jnp.maximum(jnp.abs(den), jnp.exp(-m_t))[..., None]
    bL = b[..., -1]
    m_new = m_t[..., -1]
    decay = jnp.exp(bL + m - m_new)
    wk = jnp.exp(bL[..., None] - b + ig - m_new[..., None])
    C_new = decay[..., None, None] * C + jnp.einsum('bhj,bhjd,bhje->bhde', wk, k, v)
    n_new = decay[..., None] * n + jnp.einsum('bhj,bhjd->bhd', wk, k)
    return (C_new, n_new, m_new), h


def _mlstm(q, k, v, ig, lf, C0, n0, m0):
    B, S, H, _ = q.shape
    L = MLSTM_CHUNK if S % MLSTM_CHUNK == 0 else S
    NC = S // L

    def to_chunks(t):
        t = t.astype(jnp.float32).reshape((B, NC, L, H) + t.shape[3:])
        return jnp.moveaxis(t, (1, 3), (0, 2))

    xs = (to_chunks(q), to_chunks(k), to_chunks(v), to_chunks(ig), to_chunks(lf))
    init = (C0.astype(jnp.float32), n0.astype(jnp.float32), m0.astype(jnp.float32))
    (C, n, m), hs = lax.scan(_mlstm_chunk_step, init, xs)
    h = jnp.moveaxis(hs, (0, 2), (1, 3)).reshape(B, S, H, HEAD_V)
    return h, C.astype(C0.dtype), n.astype(n0.dtype), m.astype(m0.dtype)


def _layer(x, p, pool_buf, start_pos, C0, n0, m0, g_pre_mix, w_in, b_i, b_f, w_pool_grp, s_pool,
           g_head, w_pa, w_pb, w_out, g_post_mix, g_pre_ffn, w_gate, w_up, w_down, g_post_ffn,
           w_ple, w_ple_gate):
    B, S, _ = x.shape
    widths = [D_POOL, N_HEADS * HEAD_QK, N_HEADS * HEAD_QK, N_HEADS * HEAD_V, N_HEADS * HEAD_V,
              N_HEADS, N_HEADS]
    split_idx = list(np.cumsum(widths))
    h = _rmsnorm(x, g_pre_mix)
    proj = h @ w_in.astype(x.dtype)
    u_pool, q, k, v, o_pre, i_pre, f_pre, gate_pre = jnp.split(proj, split_idx, axis=-1)
    a, new_buf = _pool_mix(u_pool, pool_buf, start_pos, w_pool_grp, s_pool)
    q = q.reshape(B, S, N_HEADS, HEAD_QK)
    k = k.reshape(B, S, N_HEADS, HEAD_QK) * (HEAD_QK ** -0.5)
    v = v.reshape(B, S, N_HEADS, HEAD_V)
    ig = (i_pre.astype(jnp.float32) + b_i.astype(jnp.float32))
    lf = jax.nn.log_sigmoid(f_pre.astype(jnp.float32) + b_f.astype(jnp.float32))
    hm, C, n, m = _mlstm(q, k, v, ig, lf, C0, n0, m0)
    hm = _rmsnorm(hm, g_head).astype(x.dtype)
    bm = jax.nn.sigmoid(o_pre) * hm.reshape(B, S, N_HEADS * HEAD_V)
    gates = jax.nn.sigmoid(gate_pre.astype(jnp.float32)).astype(x.dtype)
    g_a, g_b = jnp.split(gates, 2, axis=-1)
    merged = g_a * (a @ w_pa.astype(x.dtype)) + g_b * (bm @ w_pb.astype(x.dtype))
    x = x + _rmsnorm(merged @ w_out.astype(x.dtype), g_post_mix)
    h2 = _rmsnorm(x, g_pre_ffn)
    f = (jax.nn.silu(h2 @ w_gate.astype(x.dtype)) * (h2 @ w_up.astype(x.dtype))) @ w_down.astype(x.dtype)
    x = x + _rmsnorm(f, g_post_ffn)
    x = x + jax.nn.sigmoid(x @ w_ple_gate.astype(x.dtype)) * (p.astype(x.dtype) @ w_ple.astype(x.dtype))
    return x, new_buf, C, n, m


def setup_inputs(seed: int = 0) -> dict:
    key = jax.random.key(seed)
    ks = jax.random.split(key, 32)

    def nrm(k, shape, scale=1.0):
        return jax.random.normal(k, shape, jnp.float32) * scale

    H, K, V = N_HEADS, HEAD_QK, HEAD_V
    n_in = D_POOL + 2 * H * K + 2 * H * V + 2 * H + 2 * D_MODEL
    return {
        'x_prompt': nrm(ks[0], (BATCH, SEQ, D_MODEL)),
        'x_sample': nrm(ks[1], (DEC_BATCH, DEC_SEQ, D_MODEL)),
        'p_prompt': nrm(ks[2], (DEPTH, BATCH, SEQ, PLE_DIM)),
        'p_sample': nrm(ks[3], (DEPTH, DEC_BATCH, DEC_SEQ, PLE_DIM)),
        'state_pool': nrm(ks[4], (DEPTH, DEC_BATCH, POOL_BUF, D_POOL)),
        'state_C': nrm(ks[5], (DEPTH, DEC_BATCH, H, K, V), 0.5),
        'state_n': nrm(ks[6], (DEPTH, DEC_BATCH, H, K), 0.5),
        'state_m': nrm(ks[7], (DEPTH, DEC_BATCH, H)),
        'g_pre_mix': 1.0 + nrm(ks[8], (DEPTH, D_MODEL), 0.05),
        'w_in': nrm(ks[9], (DEPTH, D_MODEL, n_in), D_MODEL ** -0.5),
        'b_i': nrm(ks[10], (DEPTH, H), 0.1),
        'b_f': jnp.linspace(3.0, 6.0, H, dtype=jnp.float32)[None, :] + nrm(ks[11], (DEPTH, H), 0.1),
        'w_pool_grp': nrm(ks[12], (DEPTH, N_POOL_GROUPS, POOL_GROUP, POOL_GROUP), POOL_GROUP ** -0.5),
        's_pool': 1.0 + nrm(ks[13], (DEPTH, D_POOL), 0.1),
        'g_head': 1.0 + nrm(ks[14], (DEPTH, H, V), 0.05),
        'w_pa': nrm(ks[15], (DEPTH, D_POOL, D_MODEL), D_POOL ** -0.5),
        'w_pb': nrm(ks[16], (DEPTH, H * V, D_MODEL), (H * V) ** -0.5),
        'w_out': nrm(ks[17], (DEPTH, D_MODEL, D_MODEL), D_MODEL ** -0.5),
        'g_post_mix': 1.0 + nrm(ks[18], (DEPTH, D_MODEL), 0.05),
        'g_pre_ffn': 1.0 + nrm(ks[19], (DEPTH, D_MODEL), 0.05),
        'w_gate': nrm(ks[20], (DEPTH, D_MODEL, D_FF), D_MODEL ** -0.5),
        'w_up': nrm(ks[21], (DEPTH, D_MODEL, D_FF), D_MODEL ** -0.5),
        'w_down': nrm(ks[22], (DEPTH, D_FF, D_MODEL), D_FF ** -0.5),
        'g_post_ffn': 1.0 + nrm(ks[23], (DEPTH, D_MODEL), 0.05),
        'w_ple': nrm(ks[24], (DEPTH, PLE_DIM, D_MODEL), PLE_DIM ** -0.5),
        'w_ple_gate': nrm(ks[25], (DEPTH, D_MODEL, D_MODEL), D_MODEL ** -0.5),
    }


def reference(x_prompt, x_sample, p_prompt, p_sample, state_pool, state_C, state_n, state_m,
              g_pre_mix, w_in, b_i, b_f, w_pool_grp, s_pool, g_head, w_pa, w_pb, w_out,
              g_post_mix, g_pre_ffn, w_gate, w_up, w_down, g_post_ffn, w_ple, w_ple_gate):
    B = x_prompt.shape[0]
    yp, ys = x_prompt, x_sample
    pool_p, C_p, n_p, m_p = [], [], [], []
    pool_s, C_s, n_s, m_s = [], [], [], []
    for i in range(DEPTH):
        w = (g_pre_mix[i], w_in[i], b_i[i], b_f[i], w_pool_grp[i], s_pool[i], g_head[i], w_pa[i],
             w_pb[i], w_out[i], g_post_mix[i], g_pre_ffn[i], w_gate[i], w_up[i], w_down[i],
             g_post_ffn[i], w_ple[i], w_ple_gate[i])
        buf0 = jnp.zeros((B, POOL_BUF, D_POOL), state_pool.dtype)
        C0 = jnp.zeros((B, N_HEADS, HEAD_QK, HEAD_V), state_C.dtype)
        n0 = jnp.zeros((B, N_HEADS, HEAD_QK), state_n.dtype)
        m0 = jnp.zeros((B, N_HEADS), state_m.dtype)
        yp, bp, cp, np_, mp = _layer(yp, p_prompt[i], buf0, 0, C0, n0, m0, *w)
        ys, bs, cs, ns, ms = _layer(ys, p_sample[i], state_pool[i], PAST_LEN,
                                    state_C[i], state_n[i], state_m[i], *w)
        pool_p.append(bp); C_p.append(cp); n_p.append(np_); m_p.append(mp)
        pool_s.append(bs); C_s.append(cs); n_s.append(ns); m_s.append(ms)
    return (yp, ys, jnp.stack(pool_p), jnp.stack(C_p), jnp.stack(n_p), jnp.stack(m_p),
            jnp.stack(pool_s), jnp.stack(C_s), jnp.stack(n_s), jnp.stack(m_s))
```

```python
import concourse.bass as bass
import concourse.mybir as mybir

F32 = mybir.dt.float32
BF16 = mybir.dt.bfloat16
ALU = mybir.AluOpType
AF = mybir.ActivationFunctionType
AX = mybir.AxisListType

ENGS = ("pe", "act", "dve", "pool", "sp")
NDMASEM = 12
NDMA_Q = {"pool": 4, "sp": 12, "act": 4, "pe": 4, "dve": 4}
_DSZ = {F32: 4, BF16: 2, mybir.dt.int32: 4, mybir.dt.uint8: 1}


def _dsize(dt):
    return _DSZ[dt]


def ap_range(ap):
    t = ap.tensor
    name = t.name
    esz = _dsize(ap.dtype)
    dims = list(ap.ap)
    space = str(ap.space)
    if "DRAM" in space.upper() or "Dram" in space or "dram" in space:
        lo = ap.offset
        hi = lo + sum((c - 1) * abs(s) for s, c in dims) + 1
        return ("d:" + name, lo * esz, hi * esz, False)
    L = dims[0][0]
    fo = ap.offset % L if L > 0 else ap.offset
    hi = fo + sum((c - 1) * abs(s) for s, c in dims[1:]) + 1
    lo_b, hi_b = fo * esz, hi * esz
    if "PSUM" in space.upper() or "Psum" in space:
        b0, b1 = lo_b // 2048, (hi_b - 1) // 2048
        return ("p:" + name, b0 * 2048, (b1 + 1) * 2048, True)
    return ("s:" + name, lo_b, hi_b, False)


class Op:
    __slots__ = ("eng", "fn", "deps", "is_dma", "sem", "semval", "seq", "signal", "idx", "prev_dma")

    def __init__(self, eng, fn, is_dma):
        self.eng = eng
        self.fn = fn
        self.deps = set()
        self.is_dma = is_dma
        self.sem = None
        self.semval = 0
        self.seq = 0
        self.signal = False
        self.prev_dma = None


class Sched:
    def __init__(self, nc):
        self.nc = nc
        self.streams = {e: [] for e in ENGS}
        self.recs = {}
        self.dma_hist = {e: [] for e in ENGS}
        self.nops = 0

    def _touch(self, op, ap, is_write):
        key, lo, hi, excl = ap_range(ap)
        if excl:
            is_write = True
        lst = self.recs.setdefault(key, [])
        keep = []
        for rec in lst:
            rlo, rhi, rop, rw = rec
            if rlo < hi and lo < rhi:
                if is_write or rw:
                    if rop is not op:
                        op.deps.add(rop)
                if is_write and lo <= rlo and rhi <= hi:
                    continue
            keep.append(rec)
        if not is_write:
            keep = [rc for rc in keep if rc[3] or rc[2].eng != op.eng or rc[2].is_dma or op.is_dma
                    or not (lo <= rc[0] and rc[1] <= hi)]
        keep.append([lo, hi, op, is_write])
        self.recs[key] = keep

    def add(self, eng, fn, w=(), r=(), dma=False):
        op = Op(eng, fn, dma)
        op.idx = self.nops
        self.nops += 1
        for ap in r:
            self._touch(op, ap, False)
        for ap in w:
            self._touch(op, ap, True)
        if dma:
            h = self.dma_hist[eng]
            i = len(h)
            nq = NDMA_Q[eng]
            op.sem = (eng, i % nq)
            op.semval = 16 * (i // nq + 1)
            if i >= nq:
                op.prev_dma = h[i - nq]
            h.append(op)
        self.streams[eng].append(op)
        return op

    def dma(self, q, out, in_, **kw):
        return self.add(q, lambda e: e.dma_start(out=out, in_=in_, **kw), w=[out], r=[in_], dma=True)

    def mm(self, out, lhsT, rhs, start, stop, **kw):
        return self.add("pe", lambda e: e.matmul(out, lhsT, rhs, start=start, stop=stop, **kw),
                        w=[out], r=[lhsT, rhs])

    def tr(self, out, in_, ident):
        return self.add("pe", lambda e: e.transpose(out, in_, ident), w=[out], r=[in_, ident])

    def act(self, out, in_, func, bias=None, scale=None, accum_out=None, eng="act"):
        kw = {}
        r = [in_]
        w = [out]
        if bias is not None:
            kw["bias"] = bias
            if not isinstance(bias, (int, float)):
                r.append(bias)
        if scale is not None:
            kw["scale"] = scale
            if not isinstance(scale, (int, float)):
                r.append(scale)
        if accum_out is not None:
            kw["accum_out"] = accum_out
            w.append(accum_out)
        return self.add(eng, lambda e: e.activation(out, in_, func, **kw), w=w, r=r)

    def tt(self, eng, out, in0, in1, op):
        return self.add(eng, lambda e: e.tensor_tensor(out, in0, in1, op), w=[out], r=[in0, in1])

    def ts(self, eng, out, in0, s1, s2, op0, op1=None, accum_out=None):
        r = [in0]
        if not isinstance(s1, (int, float)):
            r.append(s1)
        if s2 is not None and not isinstance(s2, (int, float)):
            r.append(s2)
        w = [out]
        kw = {}
        if accum_out is not None:
            kw["accum_out"] = accum_out
            w.append(accum_out)
        if op1 is None:
            return self.add(eng, lambda e: e.tensor_scalar(out, in0, s1, None, op0, **kw), w=w, r=r)
        return self.add(eng, lambda e: e.tensor_scalar(out, in0, s1, s2, op0, op1, **kw), w=w, r=r)

    def stt(self, eng, out, in0, scalar, in1, op0, op1):
        r = [in0, in1]
        if not isinstance(scalar, (int, float)):
            r.append(scalar)
        return self.add(eng, lambda e: e.scalar_tensor_tensor(out, in0, scalar, in1, op0, op1), w=[out], r=r)

    def copy(self, eng, out, in_):
        if eng == "act":
            return self.add(eng, lambda e: e.copy(out, in_), w=[out], r=[in_])
        return self.add(eng, lambda e: e.tensor_copy(out, in_), w=[out], r=[in_])

    def memset(self, eng, ap, val):
        return self.add(eng, lambda e: e.memset(ap, val), w=[ap])

    def finalize(self):
        for e in ENGS:
            for op in self.streams[e]:
                for d in op.deps:
                    if d.is_dma:
                        continue
                    if d.eng == op.eng and d.eng == "pe":
                        continue
                    d.signal = True
        for e in ENGS:
            n = 0
            for op in self.streams[e]:
                if op.signal and not op.is_dma:
                    n += 1
                    op.seq = n

    def emit(self, final_waits=True):
        nc = self.nc
        self.finalize()
        import contextlib
        with contextlib.ExitStack() as st:
            esem = {e: st.enter_context(nc.semaphore("es_" + e)) for e in ENGS}
            dsem = {}
            for e in ENGS:
                if self.dma_hist[e]:
                    for i in range(NDMASEM):
                        dsem[(e, i)] = st.enter_context(nc.semaphore("ds_%s_%d" % (e, i)))
            block = st.enter_context(nc.Block())
            streams = self.streams
            dma_hist = self.dma_hist

            def run(ename, eng):
                seen = {}

                def wait(sem_key, sem, val):
                    if seen.get(sem_key, 0) >= val:
                        return
                    seen[sem_key] = val
                    eng.wait_ge(sem, val)

                for op in streams[ename]:
                    if op.prev_dma is not None:
                        p = op.prev_dma
                        wait(("d",) + p.sem, dsem[p.sem], p.semval)
                    for d in sorted(op.deps, key=lambda o: o.idx):
                        if d.is_dma:
                            wait(("d",) + d.sem, dsem[d.sem], d.semval)
                        else:
                            if d.eng == ename and ename == "pe":
                                continue
                            wait(("e", d.eng), esem[d.eng], d.seq)
                    ins = op.fn(eng)
                    if op.is_dma:
                        ins.then_inc(dsem[op.sem], 16)
                    elif op.signal:
                        ins.then_inc(esem[ename], 1)
                if final_waits:
                    h = dma_hist[ename]
                    last = {}
                    for p in h:
                        last[p.sem] = p.semval
                    for k, v in last.items():
                        wait(("d",) + k, dsem[k], v)

            @block.tensor
            def _(eng):
                run("pe", eng)

            @block.scalar
            def _(eng):
                run("act", eng)

            @block.vector
            def _(eng):
                run("dve", eng)

            @block.gpsimd
            def _(eng):
                run("pool", eng)

            @block.sync
            def _(eng):
                run("sp", eng)


class Arena:
    def __init__(self, big_f32, nbytes):
        self.big = big_f32
        self.nbytes = nbytes

    def view(self, off, shape, dt):
        esz = _dsize(dt)
        n = 1
        for s in shape[1:]:
            n *= s
        nb = n * esz
        assert off % 4 == 0 and off + nb <= self.nbytes, (off, nb, self.nbytes)
        nb4 = (nb + 3) // 4
        v = self.big[0:shape[0], off // 4: off // 4 + nb4]
        if dt != F32:
            v = v.bitcast(dt)
            v = v[:, 0:n]
        if len(shape) == 3:
            v = v.rearrange("p (a b) -> p a b", a=shape[1])
        elif len(shape) == 4:
            v = v.rearrange("p (a b c) -> p a b c", a=shape[1], b=shape[2])
        return v


import contextlib
import itertools
import numpy as np
from concourse.bass_utils import run_bass_kernel_spmd

DM = 2048
NPFX = 1024
NM = 1024
NS = 64
T = NM + NS
NSEQ = 16
DFF = 5632
COL_U, COL_Q, COL_K, COL_V, COL_O, COL_IG, COL_GA, COL_GB = 0, 1024, 2048, 3072, 5120, 7168, 7176, 9224
EPS = 1e-6
HALO = 16
SBUF_F32 = 53200


class Bump:
    def __init__(self, arena, start, limit):
        self.A, self.p, self.q = arena, start, limit
        self.peak = 0

    def _nb(self, shape, dt):
        n = 1
        for s in shape[1:]:
            n *= s
        return (n * _dsize(dt) + 31) // 32 * 32

    def alloc(self, shape, dt):
        nb = self._nb(shape, dt)
        v = self.A.view(self.p, shape, dt)
        self.p += nb
        assert self.p <= self.q, ("SBUF overflow", self.p, self.q)
        return v

    def hi(self, shape, dt):
        nb = self._nb(shape, dt)
        self.q -= nb
        assert self.p <= self.q, ("SBUF overflow", self.p, self.q)
        return self.A.view(self.q, shape, dt)


def build(stop_after=None):
    nc = bass.Bass("TRN2", target_bir_lowering=False)

    def din(name, shape):
        return nc.dram_tensor(name, shape, F32, kind="ExternalInput").ap()

    def dout(name, shape):
        return nc.dram_tensor(name, shape, F32, kind="ExternalOutput").ap()

    x = din("x", [NPFX + T, DM])
    p_in = din("p", [T, 256])
    pos = din("pos", [1, T])
    spool = din("spool", [NSEQ, 15, 1024])
    sC = din("sC", [NSEQ, 4, 256, 512])
    sn = din("sn", [NSEQ, 4, 256])
    sm = din("sm", [1, NSEQ * 4])
    g_pre_mix = din("g_pre_mix", [1, DM])
    w_in = din("w_in", [DM, 11272])
    b_i = din("b_i", [1, 4])
    b_f = din("b_f", [1, 4])
    w_pool = din("w_pool_grp", [4, 256, 256])
    s_pool = din("s_pool", [1, 1024])
    g_head = din("g_head", [1, 2048])
    w_pa = din("w_pa", [1024, DM])
    w_pb = din("w_pb", [DM, DM])
    w_out = din("w_out", [DM, DM])
    g_post_mix = din("g_post_mix", [1, DM])
    g_pre_ffn = din("g_pre_ffn", [1, DM])
    w_gate = din("w_gate", [DM, DFF])
    w_up = din("w_up", [DM, DFF])
    w_down = din("w_down", [DFF, DM])
    g_post_ffn = din("g_post_ffn", [1, DM])
    w_ple = din("w_ple", [256, DM])
    w_ple_gate = din("w_ple_gate", [DM, DM])
    cI = din("cI", [128, 128])
    cTri = din("cTri", [128, 128])
    cTriS = din("cTriS", [64, 64])
    cOnesS = din("cOnesS", [64, 64])
    cSel = din("cSel", [64, 16])
    cSelT = din("cSelT", [16, 64])
    cMq = din("cMq", [1, 1024])

    y_out = dout("y", [T, DM])
    poolP = dout("poolP", [15, 1024])
    CP = dout("CP", [4, 256, 512])
    nP = dout("nP", [4, 256])
    mP = dout("mP", [4, 1])
    poolS = dout("poolS", [NSEQ * 15, 1024])
    CS = dout("CS", [NSEQ, 4, 256, 512])
    nS = dout("nS", [NSEQ, 4, 256])
    mS = dout("mS", [NSEQ, 4])

    x1s = nc.dram_tensor("x1s", [T, DM], F32).ap()
    fsc = nc.dram_tensor("fsc", [T, DM], F32).ap()

    with contextlib.ExitStack() as st:
        big = st.enter_context(nc.sbuf_tensor("big", [128, SBUF_F32], F32))
        ps = st.enter_context(nc.psum_tensor("ps", [128, 4096], F32))
        A = Arena(big, SBUF_F32 * 4)
        S = Sched(nc)
        B = Bump(A, 0, SBUF_F32 * 4)

        def bank(i, n=512):
            return ps[:, i * 512: i * 512 + n]

        identF = B.alloc([128, 128], F32)
        identB = B.alloc([128, 128], BF16)
        tri = B.alloc([128, 128], F32)
        onesF = B.alloc([128, 128], F32)
        onesB = B.alloc([128, 8], BF16)
        triS = B.alloc([64, 64], F32)
        onesS = B.alloc([64, 64], F32)
        sel = B.alloc([64, 16], F32)
        selT = B.alloc([16, 64], F32)
        off_maskq = B.p
        maskq = B.alloc([128, 16, 64], BF16)
        gbuf = A.view(off_maskq, [128, 8, 32], F32)
        gT = B.alloc([128, 16], F32)
        spT = B.alloc([128, 8], F32)
        off_bibc = B.p
        bibc = B.alloc([128, 4], F32)
        mhalf = A.view(off_bibc + 16, [128, 1], F32)
        bfbc = B.alloc([128, 4], F32)
        gst = B.alloc([128, 17, 16], F32)
        dmbS = B.alloc([64, 4], F32)
        bLS = B.alloc([64, 4], F32)
        mxrow = B.alloc([1, 64], F32)
        bLrow = B.alloc([1, 64], F32)
        mrun = B.alloc([1, 4], F32)
        emf = B.alloc([128, 4], F32)
        scr = B.alloc([128, 64], F32)
        wg = B.alloc([128, 16, 8], BF16)
        TOP = B.q
        Zst = B.hi([128, 4, 2, 512], F32)
        nZ = B.hi([128, 4, 2], F32)
        _sc = [0]

        def sc(n=1):
            if _sc[0] + n > 64:
                _sc[0] = 0
            v = scr[:, _sc[0]: _sc[0] + n]
            _sc[0] += n
            return v

        S.dma("sp", identF, cI)
        S.dma("sp", tri, cTri)
        S.dma("sp", triS, cTriS)
        S.dma("sp", onesS, cOnesS)
        S.dma("sp", sel, cSel)
        S.dma("sp", selT, cSelT)
        S.dma("sp", gT, g_pre_mix.rearrange("o (k p) -> p (o k)", p=128), allow_slow_non_contiguous=True)
        S.dma("sp", spT, s_pool.rearrange("o (k p) -> p (o k)", p=128), allow_slow_non_contiguous=True)
        S.dma("sp", bibc, b_i.partition_broadcast(128).rearrange("p a b -> p (a b)"))
        S.dma("sp", bfbc, b_f.partition_broadcast(128).rearrange("p a b -> p (a b)"))
        S.copy("dve", identB, identF)
        S.memset("dve", onesF, 1.0)
        S.memset("dve", mhalf, -0.5)
        S.memset("dve", onesB, 1.0)
        S.memset("dve", Zst.rearrange("p a b c -> p (a b c)"), 0.0)
        S.memset("dve", nZ.rearrange("p a b -> p (a b)"), 0.0)
        S.memset("dve", mrun, 0.0)
        S.memset("dve", gst.rearrange("p a b -> p (a b)"), 0.0)

        def interleave(main, side, every, lead):
            cnt = 0
            side_done = side is None
            for _ in main:
                cnt += 1
                if not side_done and cnt >= lead and (cnt - lead) % every == 0:
                    try:
                        next(side)
                    except StopIteration:
                        side_done = True
            if not side_done:
                for _ in side:
                    pass

        evc = itertools.cycle(["act", "dve"])

        def evac(out, in_, mul=None, eng=None):
            eng = eng or next(evc)
            if eng == "act":
                if mul is None:
                    S.add("act", lambda e: e.copy(out, in_), w=[out], r=[in_])
                else:
                    S.add("act", lambda e: e.mul(out, in_, mul), w=[out], r=[in_])
            else:
                if mul is None:
                    S.copy("dve", out, in_)
                else:
                    S.ts("dve", out, in_, mul, None, ALU.mult)

        def rsqrt(ss, scale, n):
            t1, t2, t3 = sc(), sc(), sc()
            S.ts("dve", t1[0:n], ss[0:n], scale, EPS, ALU.mult, ALU.add)
            S.act(t2[0:n], t1[0:n], AF.Ln)
            S.act(t3[0:n], t2[0:n], AF.Exp, scale=-0.5)
            return t3

        def rsqrt_pow(ss, scale, n):
            t1, t3 = sc(), sc()
            S.ts("dve", t1[0:n], ss[0:n], scale, EPS, ALU.mult, ALU.add)
            S.tt("pool", t3[0:n], t1[0:n], mhalf[0:n], ALU.pow)
            return t3

        def sumsq(junk, src, n):
            ss = sc()
            S.memset("dve", ss[0:n], 0.0)
            S.act(junk, src, AF.Square, accum_out=ss[0:n])
            return ss

        def wload(dst, src_rows_cols):
            kc = dst.shape[1]
            half = max(1, kc // 2)
            v = src_rows_cols.rearrange("(k p) n -> p k n", p=128)
            for k0 in range(0, kc, half):
                k1 = min(kc, k0 + half)
                S.dma("pool", dst[:, k0:k1, :], v[:, k0:k1, :])

        m_phase = B.p

        def norm_stats(r0, n, xt, xbf):
            S.dma("sp", xt[0:n, :], x[r0:r0 + n, :])
            ss = sumsq(xbf[0:n, :], xt[0:n, :], n)
            return rsqrt(ss, 1.0 / DM, n)

        def norm_apply(rstd, n, dst3, c0, xt, xbf, psT3):
            S.ts("dve", xbf[0:n, :], xt[0:n, :], rstd[0:n], None, ALU.mult)
            for kc in range(16):
                S.tr(psT3[:, kc, 0:n], xbf[0:n, kc * 128:(kc + 1) * 128], identB[0:n, 0:n])
            S.tt("dve", dst3[:, :, c0:c0 + n], psT3[:, :, 0:n],
                 gT.unsqueeze(2).broadcast_to([128, 16, n]), ALU.mult)

        def norm_gen(tiles, dst3, xs, xb):
            nx, nb = len(xs), len(xb)
            nt = len(tiles)

            def load(i):
                S.dma("sp", xs[i % nx][0:tiles[i][1], :], x[tiles[i][0]:tiles[i][0] + tiles[i][1], :])

            for i in range(min(nx - 1, nt)):
                load(i)
            ss = sumsq(xb[0][0:tiles[0][1], :], xs[0][0:tiles[0][1], :], tiles[0][1])
            r = rsqrt(ss, 1.0 / DM, tiles[0][1])
            for i, (r0, n, c0) in enumerate(tiles):
                if i + nx - 1 < nt:
                    load(i + nx - 1)
                ssn = None
                if i + 1 < nt:
                    n1 = tiles[i + 1][1]
                    ssn = sumsq(xb[(i + 1) % nb][0:n1, :], xs[(i + 1) % nx][0:n1, :], n1)
                xt, xbf, psT3 = xs[i % nx], xb[i % nb], psTs[i % 2]
                S.ts("dve", xbf[0:n, :], xt[0:n, :], r[0:n], None, ALU.mult)
                for kc in range(16):
                    S.tr(psT3[:, kc, 0:n], xbf[0:n, kc * 128:(kc + 1) * 128], identB[0:n, 0:n])
                rn = rsqrt(ssn, 1.0 / DM, tiles[i + 1][1]) if ssn is not None else None
                S.tt("dve", dst3[:, :, c0:c0 + n], psT3[:, :, 0:n],
                     gT.unsqueeze(2).broadcast_to([128, 16, n]), ALU.mult)
                r = rn
                yield

        def norm_tiles(tiles, dst3, xs, xb):
            for _ in norm_gen(tiles, dst3, xs, xb):
                pass

        hT = B.alloc([128, 16, HALO + T], BF16)
        m_hT = B.p
        hTP = B.alloc([128, 16, NPFX], BF16)
        nrm_x = [B.alloc([128, DM], F32) for _ in range(2)]
        nrm_b = [B.alloc([128, DM], BF16) for _ in range(2)]
        m_afterP = B.p
        nrm0_x = [B.alloc([128, DM], F32) for _ in range(3)]
        nrm0_b = [B.alloc([128, DM], BF16) for _ in range(2)]
        B.p = m_afterP
        psTs = [ps[:, i * 1024:(i + 1) * 1024].bitcast(BF16).rearrange("p (a b) -> p a b", a=16) for i in range(2)]
        norm_tiles([(tt * 128, 128, tt * 128) for tt in range(8)], hTP, nrm0_x, nrm0_b)

        wload(wg, w_in[:, COL_IG:COL_IG + 8])

        def gate_group(hT3, c0, slot0):
            gps = bank(7)
            G = gps[:, 0:64].rearrange("p (t g) -> p t g", g=8)
            for i in range(8):
                for kc in range(16):
                    S.mm(gps[:, i * 8:(i + 1) * 8], hT3[:, kc, c0 + i * 128: c0 + (i + 1) * 128], wg[:, kc, :],
                         kc == 0, kc == 15)
            gb = gbuf
            v3 = lambda k: gb[:, k, :].rearrange("p (t h) -> p t h", h=4)
            zf, e, sp, lf, ig, dmb, dl, bLs = (gb[:, k, :] for k in range(8))
            bc = lambda t: t.unsqueeze(1).broadcast_to([128, 8, 4])
            S.tt("dve", v3(0), G[:, :, 4:8], bc(bfbc), ALU.add)
            S.act(e, zf, AF.Exp, scale=-1.0)
            S.act(sp, e, AF.Ln, bias=1.0)
            S.ts("dve", lf, sp, -1.0, None, ALU.mult)
            S.tt("dve", v3(4), G[:, :, 0:4], bc(bibc), ALU.add)
            b_ps, bL_ps = gps[:, 64:96], gps[:, 96:128]
            S.mm(b_ps, tri, lf, True, True)
            S.mm(bL_ps, onesF, lf, True, True)
            S.tt("dve", dmb, ig, b_ps, ALU.subtract)
            gsl = lambda a: gst[:, slot0:slot0 + 8, a:a + 4]
            S.act(gsl(0), v3(5), AF.Exp)
            S.tt("dve", dl, dmb, bL_ps, ALU.add)
            S.act(gsl(4), v3(6), AF.Exp)
            S.act(gsl(8), b_ps.rearrange("p (t h) -> p t h", h=4), AF.Exp, scale=-1.0)
            S.act(gsl(12), bL_ps.rearrange("p (t h) -> p t h", h=4), AF.Exp)
            S.copy("dve", bLs, bL_ps)
            S.copy("dve", bLrow[0:1, slot0 * 4: slot0 * 4 + 32], bLs[0:1, :])
            t1 = gps[0:32, 128:256]
            S.tr(t1, dmb, identF)
            mxc = gb[0:32, 0, 0:1]
            S.add("dve", lambda e_: e_.reduce_max(mxc, t1, AX.X), w=[mxc], r=[t1])
            t2 = gps[0:1, 256:288]
            S.tr(t2, mxc, identF[0:32, 0:32])
            S.copy("dve", mxrow[0:1, slot0 * 4: slot0 * 4 + 32], t2)

        def gate_tile(hT3, c0, n, slot, mcol, sample=False):
            gps = bank(7)
            for kc in range(16):
                S.mm(gps[0:n, 0:8], hT3[:, kc, c0:c0 + n], wg[:, kc, :], kc == 0, kc == 15)
            zf, e, sp, lf, ig = sc(4), sc(4), sc(4), sc(4), sc(4)
            S.tt("dve", zf[0:n], gps[0:n, 4:8], bfbc[0:n], ALU.add)
            S.act(e[0:n], zf[0:n], AF.Exp, scale=-1.0)
            S.act(sp[0:n], e[0:n], AF.Ln, bias=1.0)
            S.ts("dve", lf[0:n], sp[0:n], -1.0, None, ALU.mult)
            S.tt("dve", ig[0:n], gps[0:n, 0:4], bibc[0:n], ALU.add)
            triM, onesM = (triS, onesS) if sample else (tri, onesF)
            b_ps, bL_ps = gps[:, 16:20], gps[:, 32:36]
            S.mm(b_ps[0:n], triM[0:n, 0:n], lf[0:n], True, True)
            S.mm(bL_ps[0:n], onesM[0:n, 0:n], lf[0:n], True, True)
            dmb = dmbS if sample else sc(4)
            bLs = bLS if sample else sc(4)
            dl = sc(4)
            S.tt("dve", dmb[0:n], ig[0:n], b_ps[0:n], ALU.subtract)
            S.act(gst[0:n, slot, 0:4], dmb[0:n], AF.Exp)
            S.tt("dve", dl[0:n], dmb[0:n], bL_ps[0:n], ALU.add)
            S.act(gst[0:n, slot, 4:8], dl[0:n], AF.Exp)
            S.act(gst[0:n, slot, 8:12], b_ps[0:n], AF.Exp, scale=-1.0)
            S.act(gst[0:n, slot, 12:16], bL_ps[0:n], AF.Exp)
            S.copy("dve", bLs[0:n], bL_ps[0:n])

        gate_group(hTP, 0, 0)

        if stop_after == "p0":
            dbg = dout("dbg", [128, 16 * NPFX // 2])
            S.dma("sp", dbg, hTP.rearrange("p a b -> p (a b)").bitcast(F32))
            dbg2 = dout("dbg2", [128, 17 * 16])
            S.dma("sp", dbg2, gst.rearrange("p a b -> p (a b)"))
            S.emit()
            return nc

        q_1c = B.q
        kP = B.hi([128, 4, 8, 256], BF16)
        vP = B.hi([128, 4, 8, 512], BF16)
        kwp = [B.hi([128, 256], BF16) for _ in range(2)] * 2
        q_P = B.q
        wkP = [B.alloc([128, 16, 256], BF16)] * 2
        wvP = [B.alloc([128, 16, 512], BF16) for _ in range(2)]
        pb = itertools.cycle(range(4))
        pbP = itertools.cycle((4, 5, 6))

        def pproj_gen():
            for h in range(4):
                wk = wkP[h % 2]
                wv = wvP[h % 2]
                wload(wk, w_in[:, COL_K + h * 256: COL_K + (h + 1) * 256])
                wload(wv, w_in[:, COL_V + h * 512: COL_V + (h + 1) * 512])
                for tt in range(8):
                    b_ = bank(next(pbP))
                    for kc in range(16):
                        S.mm(b_[:, 0:256], hTP[:, kc, tt * 128:(tt + 1) * 128], wk[:, kc, :], kc == 0, kc == 15)
                    evac(kP[:, h, tt, :], b_[:, 0:256], mul=1.0 / 16)
                    yield
                for tt in range(8):
                    b_ = bank(next(pbP))
                    for kc in range(16):
                        S.mm(b_, hTP[:, kc, tt * 128:(tt + 1) * 128], wv[:, kc, :], kc == 0, kc == 15)
                    evac(vP[:, h, tt, :], b_)
                    yield

        S.copy("dve", hT[:, :, 0:HALO], hTP[:, :, NPFX - HALO:NPFX])
        interleave(pproj_gen(), norm_gen([(NPFX + tt * 128, 128 if tt < 8 else 64, HALO + tt * 128)
                                          for tt in range(9)], hT, nrm_x, nrm_b), every=6, lead=2)
        gate_group(hT, HALO, 8)
        gate_tile(hT, HALO + 8 * 128, 64, 16, 16, sample=True)

        def state_update(h, ktile, vtile, slot, kw, n=128):
            S.ts("dve", kw[0:n], ktile, gst[0:n, slot, 4 + h:5 + h], None, ALU.mult)
            un = bank(2)[:, 2 * h:2 * h + 2]
            for dc in range(2):
                S.mm(un[:, dc:dc + 1], kw[0:n, dc * 128:(dc + 1) * 128], onesB[0:n, 0:1], True, True)
            for dc in range(2):
                U = bank(4 + h)
                S.mm(U, kw[0:n, dc * 128:(dc + 1) * 128], vtile, True, True)
                S.stt("dve", Zst[:, h, dc, :], Zst[:, h, dc, :], gst[:, slot, 12 + h:13 + h], U, ALU.mult, ALU.add)
            S.stt("dve", nZ[:, h, :], nZ[:, h, :], gst[:, slot, 12 + h:13 + h], un, ALU.mult, ALU.add)

        def prec_gen():
            for tt in range(8):
                for h in range(4):
                    state_update(h, kP[:, h, tt, :], vP[:, h, tt, :], tt, kwp[h])
                    yield

        B.p = m_hT
        bmT = B.alloc([128, 16, T], BF16)
        m_h = B.p
        for t_ in range(16):
            tmp = sc(4)
            S.tt("dve", tmp[0:1], mrun, mxrow[0:1, t_ * 4:(t_ + 1) * 4], ALU.max)
            S.tt("dve", mrun, tmp[0:1], bLrow[0:1, t_ * 4:(t_ + 1) * 4], ALU.add)
        mrep = bank(7)[:, 320:324]
        S.mm(mrep, onesF[0:1, :], mrun, True, True)
        S.act(emf, mrep, AF.Exp, scale=-1.0)
        S.dma("sp", mP.rearrange("h o -> o h"), mrun)

        if stop_after == "pP":
            dbg = dout("dbg", [128, 4096])
            S.dma("sp", dbg, Zst.rearrange("p a b c -> p (a b c)"))
            dbg2 = dout("dbg2", [128, 17 * 16])
            S.dma("sp", dbg2, gst.rearrange("p a b -> p (a b)"))
            S.emit()
            return nc

        B.p = m_h
        B.q = q_1c
        S.dma("pool", maskq.rearrange("p a b -> p (a b)"), cMq.partition_broadcast(128).rearrange("p a b -> p (a b)"))
        emR = B.alloc([128, 64], F32)
        S.dma("sp", emR, sm.partition_broadcast(128).rearrange("p a b -> p (a b)"))
        S.act(emR, emR, AF.Exp)
        mprevT = B.alloc([4, 16], F32)
        S.dma("sp", mprevT, sm.rearrange("o (s h) -> (o h) s", h=4), allow_slow_non_contiguous=True)
        wslots = [B.alloc([128, 16, 512], BF16) for _ in range(2)]
        qT = B.alloc([128, 2, T], BF16)
        kT = B.alloc([128, 2, T], BF16)
        ktok = B.alloc([128, 9, 256], BF16)
        vtok = B.alloc([128, 9, 512], BF16)
        assert B.p <= q_P, ("head-0 projection buffers overlap the prefix k/v", B.p, q_P)
        wslots.append(B.alloc([128, 16, 512], BF16))
        ghbc = B.alloc([128, 512], F32)
        Zbf = B.alloc([128, 2, 512], BF16)
        nbf = B.alloc([128, 2], BF16)
        kw = B.alloc([128, 256], BF16)
        Sp = B.alloc([128, 128], BF16)
        so = B.alloc([128, 512], F32)
        bmb = B.alloc([128, 512], BF16)
        junk = bmb
        Cs = [B.alloc([128, 2, 512], F32) for _ in range(3)]
        Cb = [B.alloc([128, 2, 512], BF16) for _ in range(1)]
        Cn = [B.alloc([128, 2, 512], F32) for _ in range(2)]
        qm = B.alloc([128, 2, 16, 64], BF16)
        nsT = B.alloc([128, 2, 16], F32)
        nsb = B.alloc([128, 2, 16], BF16)
        wkm = B.alloc([64, 16], F32)
        wkmb = B.alloc([64, 16], BF16)
        kwm = [B.alloc([64, 256], BF16)] * 2
        decR = B.alloc([128, 64], F32)
        nsq = B.alloc([16, 256], F32)
        small = B.alloc([64, 64], F32)
        m_1c_end = B.p

        mnew_sm = B.alloc([16, 4], F32)
        dec_sm = B.alloc([16, 4], F32)
        wkS = B.alloc([64, 4], F32)
        BD = B.alloc([4, 4, 16], F32)

        def sample_scalars():
            tps = bank(7)
            S.tr(tps[0:4, 128:192], dmbS[0:64, 0:4], identF[0:64, 0:64])
            mxT = small[0:4, 0:16]
            S.add("dve", lambda e_: e_.reduce_max(mxT, tps[0:4, 128:192].rearrange("p (s j) -> p s j", j=4), AX.X),
                  w=[mxT], r=[tps[0:4, 128:192]])
            S.tr(tps[0:4, 256:320], bLS[0:64, 0:4], identF[0:64, 0:64])
            bLT = small[0:4, 16:32]
            S.copy("dve", bLT, tps[0:4, 256:320].rearrange("p (s j) -> p s j", j=4)[:, :, 0])
            mnewT = small[0:4, 32:48]
            S.tt("dve", mnewT, mprevT, mxT, ALU.max)
            S.tt("dve", mnewT, mnewT, bLT, ALU.add)
            S.dma("sp", mS.rearrange("s h -> h s"), mnewT, allow_slow_non_contiguous=True)
            decT = small[0:4, 48:64]
            S.tt("dve", decT, bLT, mprevT, ALU.add)
            S.tt("dve", decT, decT, mnewT, ALU.subtract)
            S.act(decT, decT, AF.Exp)
            S.tr(tps[0:16, 384:388], mnewT, identF[0:4, 0:4])
            S.copy("dve", mnew_sm, tps[0:16, 384:388])
            S.tr(tps[0:16, 400:404], decT, identF[0:4, 0:4])
            S.copy("dve", dec_sm, tps[0:16, 400:404])
            mtok = tps[0:64, 416:420]
            S.mm(mtok, selT, mnew_sm, True, True)
            S.tt("dve", wkS, dmbS, bLS, ALU.add)
            S.tt("dve", wkS, wkS, mtok, ALU.subtract)
            S.act(wkS, wkS, AF.Exp)
            S.tt("dve", BD, decT.unsqueeze(1).broadcast_to([4, 4, 16]),
                 identF[0:4, 0:4].unsqueeze(2).broadcast_to([4, 4, 16]), ALU.mult)
            drep = bank(6)[:, 0:64]
            S.mm(drep, onesF[0:4, :], BD.rearrange("p a b -> p (a b)"), True, True)
            S.copy("dve", decR, drep)

        ghb = [ghbc, ghbc]
        so2 = [so, B.alloc([128, 512], F32)]
        Sp2 = [Sp, B.alloc([128, 128], BF16)]
        kw2 = [kw, B.alloc([128, 256], BF16)]
        bmb2 = [bmb, B.alloc([128, 512], BF16)]
        bmb3 = bmb2 + [B.alloc([128, 512], BF16)]
        qTs = B.alloc([128, 2, 64], BF16)
        ktoks = B.alloc([64, 256], BF16)
        vtoks = B.alloc([64, 512], BF16)
        sgo_s = B.alloc([64, 512], F32)
        Sp_s = B.alloc([64, 64], BF16)

        pb2 = itertools.cycle(range(2))

        def qkv_loads(h):
            wqk, wv = wslots[0], wslots[1]
            wload(wqk[:, :, 0:256], w_in[:, COL_Q + h * 256: COL_Q + (h + 1) * 256])
            wload(wqk[:, :, 256:512], w_in[:, COL_K + h * 256: COL_K + (h + 1) * 256])
            wload(wv, w_in[:, COL_V + h * 512: COL_V + (h + 1) * 512])

        def proj_gen(h):
            wqk, wv, wo = wslots[0], wslots[1], wslots[2]
            if h == 0:
                qkv_loads(0)
            if h > 0:
                wload(wo, w_in[:, COL_O + h * 512: COL_O + (h + 1) * 512])
            for which, dstT, mul in ((0, qT, None), (1, kT, 1.0 / 16)):
                for dc in range(2):
                    for (t0, n) in ((0, 512), (512, 512), (1024, 64)):
                        b_ = bank(next(pb2))
                        for kc in range(16):
                            S.mm(b_[:, 0:n], wqk[:, kc, which * 256 + dc * 128: which * 256 + (dc + 1) * 128],
                                 hT[:, kc, HALO + t0: HALO + t0 + n], kc == 0, kc == 15)
                        evac(dstT[:, dc, t0:t0 + n], b_[:, 0:n], mul=mul)
                        yield
            for tt in range(9):
                n = 128 if tt < 8 else 64
                c0 = HALO + tt * 128
                b_ = bank(next(pb2))
                for kc in range(16):
                    S.mm(b_[0:n, 0:256], hT[:, kc, c0:c0 + n], wqk[:, kc, 256:512], kc == 0, kc == 15)
                evac(ktok[0:n, tt, :], b_[0:n, 0:256], mul=1.0 / 16)
                yield
                b_ = bank(next(pb2))
                for kc in range(16):
                    S.mm(b_[0:n, :], hT[:, kc, c0:c0 + n], wv[:, kc, :], kc == 0, kc == 15)
                evac(vtok[0:n, tt, :], b_[0:n, :])
                yield

        def sample_prep(h):
            wo = wslots[2]
            c0 = HALO + NM
            ops = bank(6)
            for kc in range(16):
                S.mm(ops[0:64, :], hT[:, kc, c0:c0 + 64], wo[:, kc, :], kc == 0, kc == 15)
            S.act(sgo_s, ops[0:64, :], AF.Sigmoid)
            STp = bank(0)[0:64, 0:64]
            for dc in range(2):
                S.mm(STp, kT[:, dc, NM:NM + 64], qT[:, dc, NM:NM + 64], dc == 0, dc == 1)
            S.stt("dve", Sp_s, STp, gst[0:64, 16, h:h + 1], triS, ALU.mult, ALU.mult)
            S.copy("dve", qTs, qT[:, :, NM:NM + 64])
            S.copy("dve", ktoks, ktok[0:64, 8, :])
            S.copy("dve", vtoks, vtok[0:64, 8, :])
            for dc in range(2):
                S.tt("dve", qm[:, dc], qTs[:, dc, :].unsqueeze(1).broadcast_to([128, 16, 64]), maskq, ALU.mult)
            S.dma("sp", nsq, sn[:, h, :])
            for dc in range(2):
                tpn = bank(7)[:, 64 + dc * 16: 80 + dc * 16]
                S.tr(tpn, nsq[0:16, dc * 128:(dc + 1) * 128], identF[0:16, 0:16])
                S.copy("dve", nsT[:, dc, :], tpn)
            emh = emR.rearrange("p (s hh) -> p s hh", hh=4)[:, :, h]
            S.tt("dve", nsb, nsT, emh.unsqueeze(1).broadcast_to([128, 2, 16]), ALU.mult)

        def sample_gen(h):
            Np = bank(2)[0:64, :]
            Dp = bank(3)[0:64, 0:1]

            def kwm_for(s_):
                S.ts("dve", wkm[:, s_:s_ + 1], sel[:, s_:s_ + 1], wkS[:, h:h + 1], None, ALU.mult)
                S.ts("dve", kwm[0], ktoks, wkm[:, s_:s_ + 1], None, ALU.mult)

            kwm_for(0)
            for s_ in range(NSEQ):
                cs = Cs[s_ % 3]
                cb = Cb[0]
                S.dma("sp", cs, sC[s_, h].rearrange("(dc p) e -> p dc e", p=128))
                S.add("act", lambda e_, cb=cb, cs=cs, s_=s_: e_.mul(
                    cb.rearrange("p a b -> p (a b)"), cs.rearrange("p a b -> p (a b)"),
                    emR[:, s_ * 4 + h: s_ * 4 + h + 1]),
                    w=[cb], r=[cs, emR[:, s_ * 4 + h: s_ * 4 + h + 1]])
                yield
                for dc in range(2):
                    S.mm(Np, qm[:, dc, s_, :], cb[:, dc, :], (s_ == 0 and dc == 0), False)
                for dc in range(2):
                    S.mm(Dp, qm[:, dc, s_, :], nsb[:, dc, s_:s_ + 1], (s_ == 0 and dc == 0), False)
                kwm_ = kwm[0]
                cn = Cn[s_ % 2]
                for dc in range(2):
                    U = bank(4 + dc)
                    S.mm(U, kwm_[:, dc * 128:(dc + 1) * 128], vtoks, True, True)
                    S.stt("dve", cn[:, dc, :], cs[:, dc, :], decR[:, h * 16 + s_: h * 16 + s_ + 1], U,
                          ALU.mult, ALU.add)
                S.dma("pool", CS[s_, h].rearrange("(dc p) e -> p dc e", p=128), cn)
                if s_ + 1 < NSEQ:
                    kwm_for(s_ + 1)
                yield

        def finish_tile(h, n, Np, Dp, so_, ebcol, bmb_):
            d2, d3, d4 = sc(), sc(), sc()
            S.act(d2[0:n], Dp, AF.Abs)
            S.tt("dve", d3[0:n], d2[0:n], ebcol, ALU.max)
            S.add("dve", lambda e_, d3=d3, d4=d4, n=n: e_.reciprocal(d4[0:n], d3[0:n]), w=[d4[0:n]], r=[d3[0:n]])
            S.add("act", lambda e_, d4=d4, n=n: e_.mul(Np, Np, d4[0:n]), w=[Np], r=[Np, d4[0:n]])
            ss = sumsq(bmb_[0:n], Np, n)
            rstd = rsqrt_pow(ss, 1.0 / 512, n)
            S.stt("dve", Np, Np, rstd[0:n], ghb[h % 2][0:n, :], ALU.mult, ALU.mult)
            S.tt("dve", bmb_[0:n], Np, so_, ALU.mult)

        def bm_transpose(h, n, t0, bmb_):
            bps = bank(7).bitcast(BF16).rearrange("p (a b) -> p a b", b=128)
            for e4 in range(4):
                S.tr(bps[:, e4, 0:n], bmb_[0:n, e4 * 128:(e4 + 1) * 128], identB[0:n, 0:n])
            evac(bmT[:, h * 4:(h + 1) * 4, t0:t0 + n], bps[:, 0:4, 0:n], eng="act")

        def sample_fin(h):
            Np = bank(2)[0:64, :]
            Dp = bank(3)[0:64, 0:1]
            S.mm(Np, Sp_s, vtoks, False, True)
            S.mm(Dp, Sp_s, onesB[0:64, 0:1], False, True)
            S.copy("dve", wkmb, wkm)
            npn = bank(6)[0:16, 0:256]
            S.mm(npn, wkmb, ktoks, True, True)
            S.stt("dve", nsq, nsq, dec_sm[:, h:h + 1], npn, ALU.mult, ALU.add)
            S.dma("sp", nS[:, h, :], nsq)
            finish_tile(h, 64, Np, Dp, sgo_s, gst[0:64, 16, 8 + h:9 + h], bmb2[0])
            bm_transpose(h, 64, NM, bmb2[0])

        def prompt_loop(h):
            wo = wslots[2]
            S.dma("sp", ghbc, g_head[:, h * 512:(h + 1) * 512].partition_broadcast(128).rearrange("p a b -> p (a b)"))
            S.copy("act", Zbf.rearrange("p a b -> p (a b)"), Zst[:, h].rearrange("p a b -> p (a b)"))
            S.copy("act", nbf, nZ[:, h, :])

            def stA(tt):
                c0 = HALO + tt * 128
                t0 = tt * 128
                slot = 8 + tt
                ops = bank(6)
                for kc in range(16):
                    S.mm(ops, hT[:, kc, c0:c0 + 128], wo[:, kc, :], kc == 0, kc == 15)
                S.act(so2[tt % 2], ops, AF.Sigmoid)
                STp = bank(0)[:, 0:128]
                for dc in range(2):
                    S.mm(STp, kT[:, dc, t0:t0 + 128], qT[:, dc, t0:t0 + 128], dc == 0, dc == 1)
                S.stt("dve", Sp2[tt % 2], STp, gst[:, slot, h:h + 1], tri, ALU.mult, ALU.mult)
                S.ts("dve", kw2[tt % 2], ktok[:, tt, :], gst[:, slot, 4 + h:5 + h], None, ALU.mult)

            def stB(tt):
                t0 = tt * 128
                slot = 8 + tt
                Np = bank(2 + tt % 2)
                Dp = bank(1)[:, 0:1]
                for dc in range(2):
                    S.mm(Np, qT[:, dc, t0:t0 + 128], Zbf[:, dc, :], dc == 0, False)
                S.mm(Np, Sp2[tt % 2], vtok[:, tt, :], False, True)
                for dc in range(2):
                    S.mm(Dp, qT[:, dc, t0:t0 + 128], nbf[:, dc:dc + 1], dc == 0, False)
                S.mm(Dp, Sp2[tt % 2], onesB[:, 0:1], False, True)
                kw_ = kw2[tt % 2]
                un = bank(1)[:, 8:10]
                for dc in range(2):
                    S.mm(un[:, dc:dc + 1], kw_[:, dc * 128:(dc + 1) * 128], onesB[:, 0:1], True, True)
                for dc in range(2):
                    U = bank(4 + dc)
                    S.mm(U, kw_[:, dc * 128:(dc + 1) * 128], vtok[:, tt, :], True, True)
                    S.stt("dve", Zst[:, h, dc, :], Zst[:, h, dc, :], gst[:, slot, 12 + h:13 + h], U, ALU.mult, ALU.add)
                S.stt("dve", nZ[:, h, :], nZ[:, h, :], gst[:, slot, 12 + h:13 + h], un, ALU.mult, ALU.add)
                if tt < 7:
                    S.copy("act", Zbf.rearrange("p a b -> p (a b)"), Zst[:, h].rearrange("p a b -> p (a b)"))
                    S.copy("act", nbf, nZ[:, h, :])
                else:
                    co = Cn[0]
                    S.ts("dve", co.rearrange("p a b -> p (a b)"), Zst[:, h].rearrange("p a b -> p (a b)"),
                         emf[:, h:h + 1], None, ALU.mult)
                    S.dma("sp", CP[h].rearrange("(dc p) e -> p dc e", p=128), co)
                    no = sc(2)
                    S.ts("dve", no, nZ[:, h, :], emf[:, h:h + 1], None, ALU.mult)
                    S.dma("sp", nP[h:h + 1, :].rearrange("o (dc p) -> p (o dc)", p=128), no,
                          allow_slow_non_contiguous=True)

            def stC1(tt):
                slot = 8 + tt
                finish_tile(h, 128, bank(2 + tt % 2), bank(1)[:, 0:1], so2[tt % 2],
                            gst[:, slot, 8 + h:9 + h], bmb3[tt % 3])

            stA(0)
            for tt in range(8):
                if tt + 1 < 8:
                    stA(tt + 1)
                if tt >= 2:
                    bm_transpose(h, 128, (tt - 2) * 128, bmb3[(tt - 2) % 3])
                stB(tt)
                stC1(tt)
            bm_transpose(h, 128, 6 * 128, bmb3[6 % 3])
            bm_transpose(h, 128, 7 * 128, bmb3[7 % 3])

        for h in range(4):
            if h == 0:
                interleave(proj_gen(0), prec_gen(), every=1, lead=1)
                wload(wslots[2], w_in[:, COL_O:COL_O + 512])
                sample_scalars()
            else:
                interleave(proj_gen(h), sample_gen(h - 1), every=1, lead=2)
            if h > 0:
                sample_fin(h - 1)
            if h + 1 < 4:
                qkv_loads(h + 1)
            sample_prep(h)
            prompt_loop(h)
        for _ in sample_gen(3):
            pass
        sample_fin(3)

        if stop_after == "p1c":
            dbg = dout("dbg", [128, 16 * T // 2])
            S.dma("sp", dbg, bmT.rearrange("p a b -> p (a b)").bitcast(F32))
            S.emit()
            return nc

        B.p = m_h
        aT = B.alloc([128, 8, T], BF16)
        m_a = B.p
        uT = B.alloc([128, 8, HALO + T], F32)
        zT = B.alloc([128, 8, T], BF16)
        B.q = TOP
        wpg = B.alloc([128, 8, 256], BF16)
        posb = B.alloc([128, T], F32)
        ext = B.alloc([128, 8, 16, 20], F32)
        exa = B.alloc([128, 16, 20], F32)
        exb = B.alloc([128, 16, 20], F32)
        pso = B.alloc([128, 1024], F32)
        m_ov = B.p
        wslots = [B.alloc([128, 16, 256], BF16) for _ in range(2)]
        B.p = m_ov
        icnts = [B.alloc([128, NM], F32) for _ in range(3)]
        icnts.append(icnts[0])
        sa = B.alloc([128, HALO + NM], F32)
        sb = B.alloc([128, HALO + NM], F32)
        sa_p = B.alloc([128, HALO + NM], F32)
        sb_p = B.alloc([128, HALO + NM], F32)
        exa_p = B.alloc([128, 16, 20], F32)
        exb_p = B.alloc([128, 16, 20], F32)
        pso2 = sa_p[:, 0:1024]
        B.p = max(B.p, m_ov + 2 * 8192)
        S.dma("sp", posb, pos.partition_broadcast(128).rearrange("p a b -> p (a b)"))
        S.dma("pool", wpg.rearrange("p (g k) n -> p g k n", k=2),
              w_pool.rearrange("g (k p) n -> p g k n", p=128))
        for half in range(4):
            wu = wslots[half % 2]
            wload(wu, w_in[:, COL_U + half * 256: COL_U + (half + 1) * 256])
            for c4 in range(2):
                c = half * 2 + c4
                for (t0, n) in ((0, 512), (512, 512), (1024, HALO + T - 1024)):
                    b_ = bank(next(pb))
                    for kc in range(16):
                        S.mm(b_[:, 0:n], wu[:, kc, c4 * 128:(c4 + 1) * 128], hT[:, kc, t0:t0 + n], kc == 0, kc == 15)
                    evac(uT[:, c, t0:t0 + n], b_[:, 0:n])
        spf = spool.rearrange("s r c -> (s r) c")
        bufTc = B.alloc([128, 8, 240], F32)
        for blk, (r0, rn) in enumerate(((0, 128), (128, 112))):
            S.dma("sp", pso[0:rn, :], spf[r0:r0 + rn, :])
            for c in range(8):
                tp = bank(next(pb))
                S.tr(tp[:, 0:rn], pso[0:rn, c * 128:(c + 1) * 128], identF[0:rn, 0:rn])
                evac(bufTc[:, c, r0:r0 + rn], tp[:, 0:rn])
        S.memset("pool", ext.rearrange("p a b c -> p (a b c)"), 0.0)
        def mk_icnt(g, wdw):
            S.ts("dve", icnts[g], posb[:, 0:NM], 1.0, float(wdw), ALU.add, ALU.min)
            S.add("dve", lambda e_, ic=icnts[g]: e_.reciprocal(ic, ic), w=[icnts[g]], r=[icnts[g]])

        for g, wdw in enumerate((2, 4, 8)):
            mk_icnt(g, wdw)
        for g, wdw in enumerate((2, 4, 8, 16)):
            icnt = icnts[g]
            if g == 3:
                mk_icnt(3, 16)
            for c in (2 * g, 2 * g + 1):
                en = "pool" if c < 3 else "dve"
                S.copy(en, ext[:, c, :, 0:15], bufTc[:, c, :].rearrange("p (s r) -> p s r", r=15))
                S.copy(en, ext[:, c, :, 15:19],
                       uT[:, c, HALO + NM: HALO + T].rearrange("p (s j) -> p s j", j=4))
                cur = uT[:, c, 0:HALO + NM]
                step = 1
                bufs = [sa_p, sb_p] if en == "pool" else [sa, sb]
                bi = 0
                while step < wdw:
                    nxt = bufs[bi]
                    bi ^= 1
                    S.copy(en, nxt[:, 0:step], cur[:, 0:step])
                    S.tt(en, nxt[:, step:], cur[:, step:], cur[:, 0:HALO + NM - step], ALU.add)
                    cur = nxt
                    step *= 2
                mean = bufs[bi]
                S.tt(en, mean[:, 0:NM], cur[:, HALO:], icnt, ALU.mult)
                S.tt(en, zT[:, c, 0:NM], mean[:, 0:NM], uT[:, c, HALO:HALO + NM], ALU.subtract)
                cur = ext[:, c]
                step = 1
                bufs = [exa_p, exb_p] if en == "pool" else [exa, exb]
                bi = 0
                while step < wdw:
                    nxt = bufs[bi]
                    bi ^= 1
                    S.copy(en, nxt[:, :, 0:step], cur[:, :, 0:step])
                    S.tt(en, nxt[:, :, step:19], cur[:, :, step:19], cur[:, :, 0:19 - step], ALU.add)
                    cur = nxt
                    step *= 2
                if en == "dve":
                    S.stt("dve", zT[:, c, NM:T].rearrange("p (s j) -> p s j", j=4), cur[:, :, 15:19], 1.0 / wdw,
                          ext[:, c, :, 15:19], ALU.mult, ALU.subtract)
                else:
                    oth = bufs[bi]
                    S.memset("pool", oth[:, :, 0:4], 1.0 / wdw)
                    S.tt("pool", oth[:, :, 4:8], cur[:, :, 15:19], oth[:, :, 0:4], ALU.mult)
                    S.tt("pool", zT[:, c, NM:T].rearrange("p (s j) -> p s j", j=4), oth[:, :, 4:8],
                         ext[:, c, :, 15:19], ALU.subtract)
        for c in range(8):
            g = c // 2
            for (t0, n) in ((0, 512), (512, 512), (1024, 64)):
                b_ = bank(next(pb))
                for k in range(2):
                    S.mm(b_[:, 0:n], wpg[:, 2 * g + k, (c % 2) * 128:(c % 2) * 128 + 128], zT[:, 2 * g + k, t0:t0 + n],
                         k == 0, k == 1)
                S.add("act", lambda e_, c=c, t0=t0, n=n, b_=b_: e_.mul(aT[:, c, t0:t0 + n], b_[:, 0:n], spT[:, c:c + 1]),
                      w=[aT[:, c, t0:t0 + n]], r=[b_[:, 0:n], spT[:, c:c + 1]])
        for c in range(8):
            tp = bank(next(pb))
            S.tr(tp[0:15, 0:128], uT[:, c, HALO + NM - 15: HALO + NM], identF)
            evac(pso[0:15, c * 128:(c + 1) * 128], tp[0:15, 0:128])
        S.dma("sp", poolP, pso[0:15, :])
        extr = bufTc.rearrange("p c (s r) -> p c s r", r=15)
        for c in range(8):
            S.copy("pool", extr[:, c], ext[:, c, :, 4:19])
        for blk, (r0, rn) in enumerate(((0, 128), (128, 112))):
            dst = pso if blk == 0 else pso2
            for c in range(8):
                tp = bank(next(pb))
                S.tr(tp[0:rn, 0:128], extr[:, c].rearrange("p s r -> p (s r)")[:, r0:r0 + rn], identF)
                evac(dst[0:rn, c * 128:(c + 1) * 128], tp[0:rn, 0:128])
            S.dma("sp", poolS[r0:r0 + rn, :], dst[0:rn, :])

        if stop_after == "p1b":
            dbg = dout("dbg", [128, 8 * T // 2])
            S.dma("sp", dbg, aT.rearrange("p a b -> p (a b)").bitcast(F32))
            S.emit()
            return nc

        B.p = m_a
        B.q = TOP
        mgT = B.hi([128, 16, T], BF16)
        wo_ = [B.hi([128, 16, 512], BF16) for _ in range(2)]
        wsl = [[B.alloc([128, 16, 128], BF16), B.alloc([128, 16, 128], BF16), B.alloc([128, 8, 128], BF16),
                B.alloc([128, 16, 128], BF16)] for _ in range(2)]
        sga = [B.alloc([128, 512], F32) for _ in range(2)]
        sgb = [B.alloc([128, 512], F32) for _ in range(2)]
        t1 = [B.alloc([128, 512], F32) for _ in range(2)]
        it = 0
        for c in range(16):
            wga, wgb, wpa_, wpb_ = wsl[c % 2]
            wload(wga, w_in[:, COL_GA + c * 128: COL_GA + (c + 1) * 128])
            wload(wgb, w_in[:, COL_GB + c * 128: COL_GB + (c + 1) * 128])
            wload(wpa_, w_pa[:, c * 128:(c + 1) * 128])
            wload(wpb_, w_pb[:, c * 128:(c + 1) * 128])
            if c == 1:
                for nn in range(2):
                    wload(wo_[nn], w_out[:, nn * 512:(nn + 1) * 512])
            for (t0, n) in ((0, 512), (512, 512), (1024, 64)):
                bb = (it % 2) * 4
                it += 1
                pga, pgb, pya, pyb = bank(bb), bank(bb + 1), bank(bb + 2), bank(bb + 3)
                for kc in range(16):
                    S.mm(pga[:, 0:n], wga[:, kc, :], hT[:, kc, HALO + t0:HALO + t0 + n], kc == 0, kc == 15)
                for kc in range(16):
                    S.mm(pgb[:, 0:n], wgb[:, kc, :], hT[:, kc, HALO + t0:HALO + t0 + n], kc == 0, kc == 15)
                for kc in range(8):
                    S.mm(pya[:, 0:n], wpa_[:, kc, :], aT[:, kc, t0:t0 + n], kc == 0, kc == 7)
                for kc in range(16):
                    S.mm(pyb[:, 0:n], wpb_[:, kc, :], bmT[:, kc, t0:t0 + n], kc == 0, kc == 15)
                j = it % 2
                S.act(sga[j][:, 0:n], pga[:, 0:n], AF.Sigmoid)
                S.act(sgb[j][:, 0:n], pgb[:, 0:n], AF.Sigmoid)
                S.tt("dve", sga[j][:, 0:n], sga[j][:, 0:n], pya[:, 0:n], ALU.mult)
                S.tt("dve", t1[j][:, 0:n], sgb[j][:, 0:n], pyb[:, 0:n], ALU.mult)
                S.tt("dve", mgT[:, c, t0:t0 + n], sga[j][:, 0:n], t1[j][:, 0:n], ALU.add)

        B.p = m_phase
        h2T = B.alloc([128, 16, T], BF16)
        m_h2 = B.p
        wo_ = wo_ + [B.alloc([128, 16, 512], BF16) for _ in range(2)]
        for nn in range(2, 4):
            wload(wo_[nn], w_out[:, nn * 512:(nn + 1) * 512])
        g1 = B.alloc([128, DM], F32)
        g2 = B.alloc([128, DM], F32)
        S.dma("sp", g1, g_post_mix.partition_broadcast(128).rearrange("p a b -> p (a b)"))
        S.dma("sp", g2, g_pre_ffn.partition_broadcast(128).rearrange("p a b -> p (a b)"))
        xt2 = [B.alloc([128, DM], F32) for _ in range(2)]
        tb = B.alloc([128, DM], F32)
        hb = [B.alloc([128, DM], BF16) for _ in range(2)]
        jk = B.alloc([128, 512], BF16)
        def mm2(tt):
            n = 128 if tt < 8 else 64
            t0 = tt * 128
            S.dma("sp", xt2[tt % 2][0:n], x[NPFX + t0: NPFX + t0 + n, :])
            for nn in range(4):
                b_ = bank((tt % 2) * 4 + nn)
                for kc in range(16):
                    S.mm(b_[0:n, :], mgT[:, kc, t0:t0 + n], wo_[nn][:, kc, :], kc == 0, kc == 15)

        def post2(tt):
            n = 128 if tt < 8 else 64
            t0 = tt * 128
            xt = xt2[tt % 2]
            bb = (tt % 2) * 4
            sst = sc(4)
            S.memset("dve", sst[0:n], 0.0)
            for nn in range(4):
                S.act(jk[0:n], bank(bb + nn)[0:n, :], AF.Square, accum_out=sst[0:n, nn:nn + 1])
            ss = sc()
            S.add("dve", lambda e_, ss=ss, sst=sst, n=n: e_.reduce_sum(ss[0:n], sst[0:n], AX.X), w=[ss[0:n]], r=[sst[0:n]])
            rstd = rsqrt(ss, 1.0 / DM, n)
            for nn in range(4):
                S.stt("dve", tb[0:n, nn * 512:(nn + 1) * 512], bank(bb + nn)[0:n, :], rstd[0:n],
                      g1[0:n, nn * 512:(nn + 1) * 512], ALU.mult, ALU.mult)
            S.tt("dve", xt[0:n], xt[0:n], tb[0:n], ALU.add)
            S.dma("pool", x1s[t0:t0 + n, :], xt[0:n])
            ss2 = sumsq(hb[tt % 2][0:n], xt[0:n], n)
            rstd2 = rsqrt(ss2, 1.0 / DM, n)
            S.stt("dve", hb[tt % 2][0:n], xt[0:n], rstd2[0:n], g2[0:n], ALU.mult, ALU.mult)
            pT3 = ps[:, bb * 512: bb * 512 + 1024].bitcast(BF16).rearrange("p (a b) -> p a b", a=16)
            for kc in range(16):
                S.tr(pT3[:, kc, 0:n], hb[tt % 2][0:n, kc * 128:(kc + 1) * 128], identB[0:n, 0:n])
            evac(h2T[:, :, t0:t0 + n], pT3[:, :, 0:n], eng="act")

        mm2(0)
        for tt in range(9):
            if tt + 1 < 9:
                mm2(tt + 1)
            post2(tt)

        B.p = m_h2
        B.q = TOP
        actT = B.hi([128, 44, T], BF16)
        wgu = [[B.alloc([128, 16, 128], BF16), B.alloc([128, 16, 128], BF16)] for _ in range(3)]
        sil = [B.alloc([128, 512], F32) for _ in range(2)]
        it = 0
        for fc in range(44):
            wg_, wu_ = wgu[fc % 3]
            wload(wg_, w_gate[:, fc * 128:(fc + 1) * 128])
            wload(wu_, w_up[:, fc * 128:(fc + 1) * 128])
            for (t0, n) in ((0, 512), (512, 512), (1024, 64)):
                bb = (it % 4) * 2
                it += 1
                pg, pu = bank(bb), bank(bb + 1)
                for kc in range(16):
                    S.mm(pg[:, 0:n], wg_[:, kc, :], h2T[:, kc, t0:t0 + n], kc == 0, kc == 15)
                for kc in range(16):
                    S.mm(pu[:, 0:n], wu_[:, kc, :], h2T[:, kc, t0:t0 + n], kc == 0, kc == 15)
                sl = sil[it % 2]
                S.act(sl[:, 0:n], pg[:, 0:n], AF.Silu)
                S.tt("dve", actT[:, fc, t0:t0 + n], sl[:, 0:n], pu[:, 0:n], ALU.mult)

        B.p = m_phase
        wd = [B.alloc([128, 44, 256], BF16) for _ in range(2)]
        fe = [B.alloc([128, 256], F32) for _ in range(4)]
        jkf = B.alloc([128, 256], BF16)
        ssf = B.alloc([128, 9, 8], F32)
        rstd5 = B.alloc([128, 16], F32)
        S.memset("dve", ssf.rearrange("p a b -> p (a b)"), 0.0)
        m_p4 = B.p
        wpg_ = [B.alloc([128, 16, 512], BF16) for _ in range(3)]
        m_p4e = B.p
        it = 0
        for nn in range(8):
            w_ = wd[nn % 2]
            v = w_down[:, nn * 256:(nn + 1) * 256].rearrange("(k p) n -> p k n", p=128)
            for k0 in range(0, 44, 11):
                S.dma("pool", w_[:, k0:k0 + 11, :], v[:, k0:k0 + 11, :])
            if nn == 1:
                for j in range(3):
                    wload(wpg_[j], w_ple_gate[:, j * 512:(j + 1) * 512])
            for tt in range(9):
                n = 128 if tt < 8 else 64
                t0 = tt * 128
                b_ = bank(it % 8)
                f_ = fe[it % 4]
                it += 1
                for kc in range(44):
                    S.mm(b_[0:n, 0:256], actT[:, kc, t0:t0 + n], w_[:, kc, :], kc == 0, kc == 43)
                S.act(jkf[0:n], b_[0:n, 0:256], AF.Square, accum_out=ssf[0:n, tt, nn:nn + 1])
                evac(f_[0:n], b_[0:n, 0:256], eng="dve")
                S.dma("sp", fsc[t0:t0 + n, nn * 256:(nn + 1) * 256], f_[0:n])

        B.p = m_phase
        B.q = TOP
        ft = [B.alloc([128, DM], F32) for _ in range(3)]
        x1t = [B.alloc([128, DM], F32) for _ in range(3)]
        assert B.p <= m_p4
        B.p = m_p4e
        wpl = B.alloc([128, 2, DM], BF16)
        wload(wpl, w_ple)
        wpg_ = wpg_ + [B.alloc([128, 16, 512], BF16)]
        wload(wpg_[3], w_ple_gate[:, 3 * 512:4 * 512])
        g3 = B.alloc([128, DM], F32)
        S.dma("sp", g3, g_post_ffn.partition_broadcast(128).rearrange("p a b -> p (a b)"))
        x2b = [B.alloc([128, DM], BF16) for _ in range(2)]
        x2T = [B.alloc([128, 16, 128], BF16) for _ in range(2)]
        pt = [B.alloc([128, 256], F32) for _ in range(2)]
        ptb = [B.alloc([128, 256], BF16) for _ in range(2)]
        pT = [B.alloc([128, 2, 128], BF16) for _ in range(2)]
        sg = [B.alloc([128, 1024], F32) for _ in range(2)]
        yt = [B.alloc([128, DM], F32) for _ in range(2)]
        jk5 = B.alloc([128, DM], BF16)

        ss5 = sc(9)
        S.add("dve", lambda e_: e_.reduce_sum(ss5, ssf, AX.X), w=[ss5], r=[ssf])
        t5a, t5b = sc(9), sc(9)
        S.ts("dve", t5a, ss5, 1.0 / DM, EPS, ALU.mult, ALU.add)
        S.act(t5b, t5a, AF.Ln)
        S.act(rstd5[:, 0:9], t5b, AF.Exp, scale=-0.5)

        def pre5(tt):
            n = 128 if tt < 8 else 64
            t0 = tt * 128
            f_, x1_ = ft[tt % 3], x1t[tt % 3]
            S.dma("sp", f_[0:n], fsc[t0:t0 + n, :])
            S.dma("sp", x1_[0:n], x1s[t0:t0 + n, :])
            S.dma("sp", pt[tt % 2][0:n], p_in[t0:t0 + n, :])
            S.stt("dve", f_[0:n], f_[0:n], rstd5[0:n, tt:tt + 1], g3[0:n], ALU.mult, ALU.mult)
            S.tt("dve", x1_[0:n], x1_[0:n], f_[0:n], ALU.add)
            S.copy("pool", x2b[tt % 2][0:n], x1_[0:n])
            S.copy("pool", ptb[tt % 2][0:n], pt[tt % 2][0:n])

        def preT5(tt):
            n = 128 if tt < 8 else 64
            pT3 = ps[:, 0:1024].bitcast(BF16).rearrange("p (a b) -> p a b", a=16)
            for kc in range(16):
                S.tr(pT3[:, kc, 0:n], x2b[tt % 2][0:n, kc * 128:(kc + 1) * 128], identB[0:n, 0:n])
            evac(x2T[tt % 2][:, :, 0:n], pT3[:, :, 0:n], eng="act")
            pP3 = ps[:, 1024:1536].bitcast(BF16).rearrange("p (a b) -> p a b", b=128)
            for k in range(2):
                S.tr(pP3[:, k, 0:n], ptb[tt % 2][0:n, k * 128:(k + 1) * 128], identB[0:n, 0:n])
            evac(pT[tt % 2][:, :, 0:n], pP3[:, 0:2, 0:n], eng="dve")

        def main5(tt, between=None):
            n = 128 if tt < 8 else 64
            t0 = tt * 128
            x1_ = x1t[tt % 3]
            y_ = yt[tt % 2]
            for hh in range(2):
                sg_ = sg[hh]
                for q in range(2):
                    nn = hh * 2 + q
                    bg, bp = bank(4 + q), bank(6 + q)
                    for kc in range(16):
                        S.mm(bg[0:n, :], x2T[tt % 2][:, kc, 0:n], wpg_[nn][:, kc, :], kc == 0, kc == 15)
                    for k in range(2):
                        S.mm(bp[0:n, :], pT[tt % 2][:, k, 0:n], wpl[:, k, nn * 512:(nn + 1) * 512], k == 0, k == 1)
                if hh == 1 and between is not None:
                    between()
                for q in range(2):
                    bg, bp = bank(4 + q), bank(6 + q)
                    S.act(sg_[0:n, q * 512:(q + 1) * 512], bg[0:n, :], AF.Sigmoid)
                    S.tt("dve", sg_[0:n, q * 512:(q + 1) * 512], sg_[0:n, q * 512:(q + 1) * 512], bp[0:n, :], ALU.mult)
                S.tt("dve", y_[0:n, hh * 1024:(hh + 1) * 1024], sg_[0:n], x1_[0:n, hh * 1024:(hh + 1) * 1024], ALU.add)
            S.dma("pool", y_out[t0:t0 + n, :], y_[0:n])

        pre5(0)
        preT5(0)
        for tt in range(9):
            if tt + 1 < 9:
                pre5(tt + 1)
                main5(tt, between=lambda tt=tt: preT5(tt + 1))
            else:
                main5(tt)

        S.emit()
    return nc


_NC_CACHE = {}


def _consts():
    i = np.arange(128)
    tri = (i[:, None] <= i[None, :]).astype(np.float32)
    j = np.arange(64)
    same = (j[:, None] // 4) == (j[None, :] // 4)
    triS = (same & (j[:, None] <= j[None, :])).astype(np.float32)
    onesS = same.astype(np.float32)
    sel = ((j[:, None] // 4) == np.arange(16)[None, :]).astype(np.float32)
    return {
        "cI": np.eye(128, dtype=np.float32), "cTri": tri, "cTriS": triS, "cOnesS": onesS,
        "cSel": sel, "cSelT": np.ascontiguousarray(sel.T),
        "cMq": np.ascontiguousarray(sel.T).reshape(1, 1024),
    }


def kernel(x_prompt, x_sample, p_prompt, p_sample, state_pool, state_C, state_n, state_m,
           g_pre_mix, w_in, b_i, b_f, w_pool_grp, s_pool, g_head, w_pa, w_pb, w_out,
           g_post_mix, g_pre_ffn, w_gate, w_up, w_down, g_post_ffn, w_ple, w_ple_gate):
    f32 = lambda a: np.ascontiguousarray(np.asarray(a, dtype=np.float32))
    x_prompt, x_sample = f32(x_prompt), f32(x_sample)
    p_prompt, p_sample = f32(p_prompt)[0], f32(p_sample)[0]
    state_pool, state_C, state_n, state_m = f32(state_pool)[0], f32(state_C)[0], f32(state_n)[0], f32(state_m)[0]
    shared = {
        "g_pre_mix": f32(g_pre_mix), "w_in": f32(w_in)[0], "b_i": f32(b_i), "b_f": f32(b_f),
        "w_pool_grp": f32(w_pool_grp)[0], "s_pool": f32(s_pool), "g_head": f32(g_head).reshape(1, 2048),
        "w_pa": f32(w_pa)[0], "w_pb": f32(w_pb)[0], "w_out": f32(w_out)[0], "g_post_mix": f32(g_post_mix),
        "g_pre_ffn": f32(g_pre_ffn), "w_gate": f32(w_gate)[0], "w_up": f32(w_up)[0], "w_down": f32(w_down)[0],
        "g_post_ffn": f32(g_post_ffn), "w_ple": f32(w_ple)[0], "w_ple_gate": f32(w_ple_gate)[0],
    }
    shared.update(_consts())
    in_maps = []
    for c in range(8):
        b, j = c // 2, c % 2
        xs = x_sample[16 * c:16 * c + 16].reshape(64, DM)
        xm = x_prompt[b, j * 1024:(j + 1) * 1024]
        xp = x_prompt[b, 0:1024] if j == 1 else np.zeros((1024, DM), np.float32)
        posv = np.concatenate([np.arange(j * 1024, (j + 1) * 1024), 16384 + np.tile(np.arange(4), 16)])
        m = dict(shared)
        m.update({
            "x": np.ascontiguousarray(np.concatenate([xp, xm, xs], 0)),
            "p": np.ascontiguousarray(np.concatenate([p_prompt[b, j * 1024:(j + 1) * 1024],
                                                      p_sample[16 * c:16 * c + 16].reshape(64, 256)], 0)),
            "pos": posv.astype(np.float32).reshape(1, T),
            "spool": np.ascontiguousarray(state_pool[16 * c:16 * c + 16]),
            "sC": np.ascontiguousarray(state_C[16 * c:16 * c + 16]),
            "sn": np.ascontiguousarray(state_n[16 * c:16 * c + 16]),
            "sm": np.ascontiguousarray(state_m[16 * c:16 * c + 16]).reshape(1, 64),
        })
        in_maps.append(m)
    if "nc" not in _NC_CACHE:
        _NC_CACHE["nc"] = build()
    res = run_bass_kernel_spmd(_NC_CACHE["nc"], in_maps, core_ids=list(range(8)))
    R = res.results
    yp = np.zeros((4, 2048, DM), np.float32)
    ys = np.zeros((128, 4, DM), np.float32)
    pool_p = np.zeros((1, 4, 15, 1024), np.float32)
    C_p = np.zeros((1, 4, 4, 256, 512), np.float32)
    n_p = np.zeros((1, 4, 4, 256), np.float32)
    m_p = np.zeros((1, 4, 4), np.float32)
    pool_s = np.zeros((1, 128, 15, 1024), np.float32)
    C_s = np.zeros((1, 128, 4, 256, 512), np.float32)
    n_s = np.zeros((1, 128, 4, 256), np.float32)
    m_s = np.zeros((1, 128, 4), np.float32)
    for c in range(8):
        b, j = c // 2, c % 2
        r = R[c]
        yp[b, j * 1024:(j + 1) * 1024] = r["y"][0:1024]
        ys[16 * c:16 * c + 16] = r["y"][1024:1088].reshape(16, 4, DM)
        if j == 1:
            pool_p[0, b] = r["poolP"]
            C_p[0, b] = r["CP"]
            n_p[0, b] = r["nP"]
            m_p[0, b] = r["mP"].reshape(4)
        pool_s[0, 16 * c:16 * c + 16] = r["poolS"].reshape(16, 15, 1024)
        C_s[0, 16 * c:16 * c + 16] = r["CS"]
        n_s[0, 16 * c:16 * c + 16] = r["nS"]
        m_s[0, 16 * c:16 * c + 16] = r["mS"]
    return (yp, ys, pool_p, C_p, n_p, m_p, pool_s, C_s, n_s, m_s)
```

```python
import concourse.bass as bass
import concourse.mybir as mybir

F32 = mybir.dt.float32
BF16 = mybir.dt.bfloat16
ALU = mybir.AluOpType
AF = mybir.ActivationFunctionType
AX = mybir.AxisListType

ENGS = ("pe", "act", "dve", "pool", "sp")
NDMASEM = 12
NDMA_Q = {"pool": 4, "sp": 12, "act": 4, "pe": 4, "dve": 4}
_DSZ = {F32: 4, BF16: 2, mybir.dt.int32: 4, mybir.dt.uint8: 1}


def _dsize(dt):
    return _DSZ[dt]


def ap_range(ap):
    t = ap.tensor
    name = t.name
    esz = _dsize(ap.dtype)
    dims = list(ap.ap)
    space = str(ap.space)
    if "DRAM" in space.upper() or "Dram" in space or "dram" in space:
        lo = ap.offset
        hi = lo + sum((c - 1) * abs(s) for s, c in dims) + 1
        return ("d:" + name, lo * esz, hi * esz, False)
    L = dims[0][0]
    fo = ap.offset % L if L > 0 else ap.offset
    hi = fo + sum((c - 1) * abs(s) for s, c in dims[1:]) + 1
    lo_b, hi_b = fo * esz, hi * esz
    if "PSUM" in space.upper() or "Psum" in space:
        b0, b1 = lo_b // 2048, (hi_b - 1) // 2048
        return ("p:" + name, b0 * 2048, (b1 + 1) * 2048, True)
    return ("s:" + name, lo_b, hi_b, False)


class Op:
    __slots__ = ("eng", "fn", "deps", "is_dma", "sem", "semval", "seq", "signal", "idx", "prev_dma")

    def __init__(self, eng, fn, is_dma):
        self.eng = eng
        self.fn = fn
        self.deps = set()
        self.is_dma = is_dma
        self.sem = None
        self.semval = 0
        self.seq = 0
        self.signal = False
        self.prev_dma = None


class Sched:
    def __init__(self, nc):
        self.nc = nc
        self.streams = {e: [] for e in ENGS}
        self.recs = {}
        self.dma_hist = {e: [] for e in ENGS}
        self.nops = 0

    def _touch(self, op, ap, is_write):
        key, lo, hi, excl = ap_range(ap)
        if excl:
            is_write = True
        lst = self.recs.setdefault(key, [])
        keep = []
        for rec in lst:
            rlo, rhi, rop, rw = rec
            if rlo < hi and lo < rhi:
                if is_write or rw:
                    if rop is not op:
                        op.deps.add(rop)
                if is_write and lo <= rlo and rhi <= hi:
                    continue
            keep.append(rec)
        if not is_write:
            keep = [rc for rc in keep if rc[3] or rc[2].eng != op.eng or rc[2].is_dma or op.is_dma
                    or not (lo <= rc[0] and rc[1] <= hi)]
        keep.append([lo, hi, op, is_write])
        self.recs[key] = keep

    def add(self, eng, fn, w=(), r=(), dma=False):
        op = Op(eng, fn, dma)
        op.idx = self.nops
        self.nops += 1
        for ap in r:
            self._touch(op, ap, False)
        for ap in w:
            self._touch(op, ap, True)
        if dma:
            h = self.dma_hist[eng]
            i = len(h)
            nq = NDMA_Q[eng]
            op.sem = (eng, i % nq)
            op.semval = 16 * (i // nq + 1)
            if i >= nq:
                op.prev_dma = h[i - nq]
            h.append(op)
        self.streams[eng].append(op)
        return op

    def dma(self, q, out, in_, **kw):
        return self.add(q, lambda e: e.dma_start(out=out, in_=in_, **kw), w=[out], r=[in_], dma=True)

    def mm(self, out, lhsT, rhs, start, stop, **kw):
        return self.add("pe", lambda e: e.matmul(out, lhsT, rhs, start=start, stop=stop, **kw),
                        w=[out], r=[lhsT, rhs])

    def tr(self, out, in_, ident):
        return self.add("pe", lambda e: e.transpose(out, in_, ident), w=[out], r=[in_, ident])

    def act(self, out, in_, func, bias=None, scale=None, accum_out=None, eng="act"):
        kw = {}
        r = [in_]
        w = [out]
        if bias is not None:
            kw["bias"] = bias
            if not isinstance(bias, (int, float)):
                r.append(bias)
        if scale is not None:
            kw["scale"] = scale
            if not isinstance(scale, (int, float)):
                r.append(scale)
        if accum_out is not None:
            kw["accum_out"] = accum_out
            w.append(accum_out)
        return self.add(eng, lambda e: e.activation(out, in_, func, **kw), w=w, r=r)

    def tt(self, eng, out, in0, in1, op):
        return self.add(eng, lambda e: e.tensor_tensor(out, in0, in1, op), w=[out], r=[in0, in1])

    def ts(self, eng, out, in0, s1, s2, op0, op1=None, accum_out=None):
        r = [in0]
        if not isinstance(s1, (int, float)):
            r.append(s1)
        if s2 is not None and not isinstance(s2, (int, float)):
            r.append(s2)
        w = [out]
        kw = {}
        if accum_out is not None:
            kw["accum_out"] = accum_out
            w.append(accum_out)
        if op1 is None:
            return self.add(eng, lambda e: e.tensor_scalar(out, in0, s1, None, op0, **kw), w=w, r=r)
        return self.add(eng, lambda e: e.tensor_scalar(out, in0, s1, s2, op0, op1, **kw), w=w, r=r)

    def stt(self, eng, out, in0, scalar, in1, op0, op1):
        r = [in0, in1]
        if not isinstance(scalar, (int, float)):
            r.append(scalar)
        return self.add(eng, lambda e: e.scalar_tensor_tensor(out, in0, scalar, in1, op0, op1), w=[out], r=r)

    def copy(self, eng, out, in_):
        if eng == "act":
            return self.add(eng, lambda e: e.copy(out, in_), w=[out], r=[in_])
        return self.add(eng, lambda e: e.tensor_copy(out, in_), w=[out], r=[in_])

    def memset(self, eng, ap, val):
        return self.add(eng, lambda e: e.memset(ap, val), w=[ap])

    def finalize(self):
        for e in ENGS:
            for op in self.streams[e]:
                for d in op.deps:
                    if d.is_dma:
                        continue
                    if d.eng == op.eng and d.eng == "pe":
                        continue
                    d.signal = True
        for e in ENGS:
            n = 0
            for op in self.streams[e]:
                if op.signal and not op.is_dma:
                    n += 1
                    op.seq = n

    def emit(self, final_waits=True):
        nc = self.nc
        self.finalize()
        import contextlib
        with contextlib.ExitStack() as st:
            esem = {e: st.enter_context(nc.semaphore("es_" + e)) for e in ENGS}
            dsem = {}
            for e in ENGS:
                if self.dma_hist[e]:
                    for i in range(NDMASEM):
                        dsem[(e, i)] = st.enter_context(nc.semaphore("ds_%s_%d" % (e, i)))
            block = st.enter_context(nc.Block())
            streams = self.streams
            dma_hist = self.dma_hist

            def run(ename, eng):
                seen = {}

                def wait(sem_key, sem, val):
                    if seen.get(sem_key, 0) >= val:
                        return
                    seen[sem_key] = val
                    eng.wait_ge(sem, val)

                for op in streams[ename]:
                    if op.prev_dma is not None:
                        p = op.prev_dma
                        wait(("d",) + p.sem, dsem[p.sem], p.semval)
                    for d in sorted(op.deps, key=lambda o: o.idx):
                        if d.is_dma:
                            wait(("d",) + d.sem, dsem[d.sem], d.semval)
                        else:
                            if d.eng == ename and ename == "pe":
                                continue
                            wait(("e", d.eng), esem[d.eng], d.seq)
                    ins = op.fn(eng)
                    if op.is_dma:
                        ins.then_inc(dsem[op.sem], 16)
                    elif op.signal:
                        ins.then_inc(esem[ename], 1)
                if final_waits:
                    h = dma_hist[ename]
                    last = {}
                    for p in h:
                        last[p.sem] = p.semval
                    for k, v in last.items():
                        wait(("d",) + k, dsem[k], v)

            @block.tensor
            def _(eng):
                run("pe", eng)

            @block.scalar
            def _(eng):
                run("act", eng)

            @block.vector
            def _(eng):
                run("dve", eng)

            @block.gpsimd
            def _(eng):
                run("pool", eng)

            @block.sync
            def _(eng):
                run("sp", eng)


class Arena:
    def __init__(self, big_f32, nbytes):
        self.big = big_f32
        self.nbytes = nbytes

    def view(self, off, shape, dt):
        esz = _dsize(dt)
        n = 1
        for s in shape[1:]:
            n *= s
        nb = n * esz
        assert off % 4 == 0 and off + nb <= self.nbytes, (off, nb, self.nbytes)
        nb4 = (nb + 3) // 4
        v = self.big[0:shape[0], off // 4: off // 4 + nb4]
        if dt != F32:
            v = v.bitcast(dt)
            v = v[:, 0:n]
        if len(shape) == 3:
            v = v.rearrange("p (a b) -> p a b", a=shape[1])
        elif len(shape) == 4:
            v = v.rearrange("p (a b c) -> p a b c", a=shape[1], b=shape[2])
        return v


import contextlib
import itertools
import numpy as np
from concourse.bass_utils import run_bass_kernel_spmd

DM = 2048
NPFX = 1024
NM = 1024
NS = 64
T = NM + NS
NSEQ = 16
DFF = 5632
COL_U, COL_Q, COL_K, COL_V, COL_O, COL_IG, COL_GA, COL_GB = 0, 1024, 2048, 3072, 5120, 7168, 7176, 9224
EPS = 1e-6
HALO = 16
SBUF_F32 = 53200


class Bump:
    def __init__(self, arena, start, limit):
        self.A, self.p, self.q = arena, start, limit
        self.peak = 0

    def _nb(self, shape, dt):
        n = 1
        for s in shape[1:]:
            n *= s
        return (n * _dsize(dt) + 31) // 32 * 32

    def alloc(self, shape, dt):
        nb = self._nb(shape, dt)
        v = self.A.view(self.p, shape, dt)
        self.p += nb
        assert self.p <= self.q, ("SBUF overflow", self.p, self.q)
        return v

    def hi(self, shape, dt):
        nb = self._nb(shape, dt)
        self.q -= nb
        assert self.p <= self.q, ("SBUF overflow", self.p, self.q)
        return self.A.view(self.q, shape, dt)


def build(stop_after=None):
    nc = bass.Bass("TRN2", target_bir_lowering=False)

    def din(name, shape):
        return nc.dram_tensor(name, shape, F32, kind="ExternalInput").ap()

    def dout(name, shape):
        return nc.dram_tensor(name, shape, F32, kind="ExternalOutput").ap()

    x = din("x", [NPFX + T, DM])
    p_in = din("p", [T, 256])
    pos = din("pos", [1, T])
    spool = din("spool", [NSEQ, 15, 1024])
    sC = din("sC", [NSEQ, 4, 256, 512])
    sn = din("sn", [NSEQ, 4, 256])
    sm = din("sm", [1, NSEQ * 4])
    g_pre_mix = din("g_pre_mix", [1, DM])
    w_in = din("w_in", [DM, 11272])
    b_i = din("b_i", [1, 4])
    b_f = din("b_f", [1, 4])
    w_pool = din("w_pool_grp", [4, 256, 256])
    s_pool = din("s_pool", [1, 1024])
    g_head = din("g_head", [1, 2048])
    w_pa = din("w_pa", [1024, DM])
    w_pb = din("w_pb", [DM, DM])
    w_out = din("w_out", [DM, DM])
    g_post_mix = din("g_post_mix", [1, DM])
    g_pre_ffn = din("g_pre_ffn", [1, DM])
    w_gate = din("w_gate", [DM, DFF])
    w_up = din("w_up", [DM, DFF])
    w_down = din("w_down", [DFF, DM])
    g_post_ffn = din("g_post_ffn", [1, DM])
    w_ple = din("w_ple", [256, DM])
    w_ple_gate = din("w_ple_gate", [DM, DM])
    cI = din("cI", [128, 128])
    cTri = din("cTri", [128, 128])
    cTriS = din("cTriS", [64, 64])
    cOnesS = din("cOnesS", [64, 64])
    cSel = din("cSel", [64, 16])
    cSelT = din("cSelT", [16, 64])
    cMq = din("cMq", [1, 1024])

    y_out = dout("y", [T, DM])
    poolP = dout("poolP", [15, 1024])
    CP = dout("CP", [4, 256, 512])
    nP = dout("nP", [4, 256])
    mP = dout("mP", [4, 1])
    poolS = dout("poolS", [NSEQ * 15, 1024])
    CS = dout("CS", [NSEQ, 4, 256, 512])
    nS = dout("nS", [NSEQ, 4, 256])
    mS = dout("mS", [NSEQ, 4])

    x1s = nc.dram_tensor("x1s", [T, DM], F32).ap()
    fsc = nc.dram_tensor("fsc", [T, DM], F32).ap()

    with contextlib.ExitStack() as st:
        big = st.enter_context(nc.sbuf_tensor("big", [128, SBUF_F32], F32))
        ps = st.enter_context(nc.psum_tensor("ps", [128, 4096], F32))
        A = Arena(big, SBUF_F32 * 4)
        S = Sched(nc)
        B = Bump(A, 0, SBUF_F32 * 4)

        def bank(i, n=512):
            return ps[:, i * 512: i * 512 + n]

        identF = B.alloc([128, 128], F32)
        identB = B.alloc([128, 128], BF16)
        tri = B.alloc([128, 128], F32)
        onesF = B.alloc([128, 128], F32)
        onesB = B.alloc([128, 8], BF16)
        triS = B.alloc([64, 64], F32)
        onesS = B.alloc([64, 64], F32)
        sel = B.alloc([64, 16], F32)
        selT = B.alloc([16, 64], F32)
        off_maskq = B.p
        maskq = B.alloc([128, 16, 64], BF16)
        gbuf = A.view(off_maskq, [128, 8, 32], F32)
        gT = B.alloc([128, 16], F32)
        spT = B.alloc([128, 8], F32)
        off_bibc = B.p
        bibc = B.alloc([128, 4], F32)
        mhalf = A.view(off_bibc + 16, [128, 1], F32)
        bfbc = B.alloc([128, 4], F32)
        gst = B.alloc([128, 17, 16], F32)
        dmbS = B.alloc([64, 4], F32)
        bLS = B.alloc([64, 4], F32)
        mxrow = B.alloc([1, 64], F32)
        bLrow = B.alloc([1, 64], F32)
        mrun = B.alloc([1, 4], F32)
        emf = B.alloc([128, 4], F32)
        scr = B.alloc([128, 64], F32)
        wg = B.alloc([128, 16, 8], BF16)
        TOP = B.q
        Zst = B.hi([128, 4, 2, 512], F32)
        nZ = B.hi([128, 4, 2], F32)
        _sc = [0]

        def sc(n=1):
            if _sc[0] + n > 64:
                _sc[0] = 0
            v = scr[:, _sc[0]: _sc[0] + n]
            _sc[0] += n
            return v

        S.dma("sp", identF, cI)
        S.dma("sp", tri, cTri)
        S.dma("sp", triS, cTriS)
        S.dma("sp", onesS, cOnesS)
        S.dma("sp", sel, cSel)
        S.dma("sp", selT, cSelT)
        S.dma("sp", gT, g_pre_mix.rearrange("o (k p) -> p (o k)", p=128), allow_slow_non_contiguous=True)
        S.dma("sp", spT, s_pool.rearrange("o (k p) -> p (o k)", p=128), allow_slow_non_contiguous=True)
        S.dma("sp", bibc, b_i.partition_broadcast(128).rearrange("p a b -> p (a b)"))
        S.dma("sp", bfbc, b_f.partition_broadcast(128).rearrange("p a b -> p (a b)"))
        S.copy("dve", identB, identF)
        S.memset("dve", onesF, 1.0)
        S.memset("dve", mhalf, -0.5)
        S.memset("dve", onesB, 1.0)
        S.memset("dve", Zst.rearrange("p a b c -> p (a b c)"), 0.0)
        S.memset("dve", nZ.rearrange("p a b -> p (a b)"), 0.0)
        S.memset("dve", mrun, 0.0)
        S.memset("dve", gst.rearrange("p a b -> p (a b)"), 0.0)

        def interleave(main, side, every, lead):
            cnt = 0
            side_done = side is None
            for _ in main:
                cnt += 1
                if not side_done and cnt >= lead and (cnt - lead) % every == 0:
                    try:
                        next(side)
                    except StopIteration:
                        side_done = True
            if not side_done:
                for _ in side:
                    pass

        evc = itertools.cycle(["act", "dve"])

        def evac(out, in_, mul=None, eng=None):
            eng = eng or next(evc)
            if eng == "act":
                if mul is None:
                    S.add("act", lambda e: e.copy(out, in_), w=[out], r=[in_])
                else:
                    S.add("act", lambda e: e.mul(out, in_, mul), w=[out], r=[in_])
            else:
                if mul is None:
                    S.copy("dve", out, in_)
                else:
                    S.ts("dve", out, in_, mul, None, ALU.mult)

        def rsqrt(ss, scale, n):
            t1, t2, t3 = sc(), sc(), sc()
            S.ts("dve", t1[0:n], ss[0:n], scale, EPS, ALU.mult, ALU.add)
            S.act(t2[0:n], t1[0:n], AF.Ln)
            S.act(t3[0:n], t2[0:n], AF.Exp, scale=-0.5)
            return t3

        def rsqrt_pow(ss, scale, n):
            t1, t3 = sc(), sc()
            S.ts("dve", t1[0:n], ss[0:n], scale, EPS, ALU.mult, ALU.add)
            S.tt("pool", t3[0:n], t1[0:n], mhalf[0:n], ALU.pow)
            return t3

        def sumsq(junk, src, n):
            ss = sc()
            S.memset("dve", ss[0:n], 0.0)
            S.act(junk, src, AF.Square, accum_out=ss[0:n])
            return ss

        def wload(dst, src_rows_cols):
            kc = dst.shape[1]
            half = max(1, kc // 2)
            v = src_rows_cols.rearrange("(k p) n -> p k n", p=128)
            for k0 in range(0, kc, half):
                k1 = min(kc, k0 + half)
                S.dma("pool", dst[:, k0:k1, :], v[:, k0:k1, :])

        m_phase = B.p

        def norm_stats(r0, n, xt, xbf):
            S.dma("sp", xt[0:n, :], x[r0:r0 + n, :])
            ss = sumsq(xbf[0:n, :], xt[0:n, :], n)
            return rsqrt(ss, 1.0 / DM, n)

        def norm_apply(rstd, n, dst3, c0, xt, xbf, psT3):
            S.ts("dve", xbf[0:n, :], xt[0:n, :], rstd[0:n], None, ALU.mult)
            for kc in range(16):
                S.tr(psT3[:, kc, 0:n], xbf[0:n, kc * 128:(kc + 1) * 128], identB[0:n, 0:n])
            S.tt("dve", dst3[:, :, c0:c0 + n], psT3[:, :, 0:n],
                 gT.unsqueeze(2).broadcast_to([128, 16, n]), ALU.mult)

        def norm_gen(tiles, dst3, xs, xb):
            nx, nb = len(xs), len(xb)
            nt = len(tiles)

            def load(i):
                S.dma("sp", xs[i % nx][0:tiles[i][1], :], x[tiles[i][0]:tiles[i][0] + tiles[i][1], :])

            for i in range(min(nx - 1, nt)):
                load(i)
            ss = sumsq(xb[0][0:tiles[0][1], :], xs[0][0:tiles[0][1], :], tiles[0][1])
            r = rsqrt(ss, 1.0 / DM, tiles[0][1])
            for i, (r0, n, c0) in enumerate(tiles):
                if i + nx - 1 < nt:
                    load(i + nx - 1)
                ssn = None
                if i + 1 < nt:
                    n1 = tiles[i + 1][1]
                    ssn = sumsq(xb[(i + 1) % nb][0:n1, :], xs[(i + 1) % nx][0:n1, :], n1)
                xt, xbf, psT3 = xs[i % nx], xb[i % nb], psTs[i % 2]
                S.ts("dve", xbf[0:n, :], xt[0:n, :], r[0:n], None, ALU.mult)
                for kc in range(16):
                    S.tr(psT3[:, kc, 0:n], xbf[0:n, kc * 128:(kc + 1) * 128], identB[0:n, 0:n])
                rn = rsqrt(ssn, 1.0 / DM, tiles[i + 1][1]) if ssn is not None else None
                S.tt("dve", dst3[:, :, c0:c0 + n], psT3[:, :, 0:n],
                     gT.unsqueeze(2).broadcast_to([128, 16, n]), ALU.mult)
                r = rn
                yield

        def norm_tiles(tiles, dst3, xs, xb):
            for _ in norm_gen(tiles, dst3, xs, xb):
                pass

        hT = B.alloc([128, 16, HALO + T], BF16)
        m_hT = B.p
        hTP = B.alloc([128, 16, NPFX], BF16)
        nrm_x = [B.alloc([128, DM], F32) for _ in range(2)]
        nrm_b = [B.alloc([128, DM], BF16) for _ in range(2)]
        m_afterP = B.p
        nrm0_x = [B.alloc([128, DM], F32) for _ in range(3)]
        nrm0_b = [B.alloc([128, DM], BF16) for _ in range(2)]
        B.p = m_afterP
        psTs = [ps[:, i * 1024:(i + 1) * 1024].bitcast(BF16).rearrange("p (a b) -> p a b", a=16) for i in range(2)]
        norm_tiles([(tt * 128, 128, tt * 128) for tt in range(8)], hTP, nrm0_x, nrm0_b)

        wload(wg, w_in[:, COL_IG:COL_IG + 8])

        def gate_group(hT3, c0, slot0):
            gps = bank(7)
            G = gps[:, 0:64].rearrange("p (t g) -> p t g", g=8)
            for i in range(8):
                for kc in range(16):
                    S.mm(gps[:, i * 8:(i + 1) * 8], hT3[:, kc, c0 + i * 128: c0 + (i + 1) * 128], wg[:, kc, :],
                         kc == 0, kc == 15)
            gb = gbuf
            v3 = lambda k: gb[:, k, :].rearrange("p (t h) -> p t h", h=4)
            zf, e, sp, lf, ig, dmb, dl, bLs = (gb[:, k, :] for k in range(8))
            bc = lambda t: t.unsqueeze(1).broadcast_to([128, 8, 4])
            S.tt("dve", v3(0), G[:, :, 4:8], bc(bfbc), ALU.add)
            S.act(e, zf, AF.Exp, scale=-1.0)
            S.act(sp, e, AF.Ln, bias=1.0)
            S.ts("dve", lf, sp, -1.0, None, ALU.mult)
            S.tt("dve", v3(4), G[:, :, 0:4], bc(bibc), ALU.add)
            b_ps, bL_ps = gps[:, 64:96], gps[:, 96:128]
            S.mm(b_ps, tri, lf, True, True)
            S.mm(bL_ps, onesF, lf, True, True)
            S.tt("dve", dmb, ig, b_ps, ALU.subtract)
            gsl = lambda a: gst[:, slot0:slot0 + 8, a:a + 4]
            S.act(gsl(0), v3(5), AF.Exp)
            S.tt("dve", dl, dmb, bL_ps, ALU.add)
            S.act(gsl(4), v3(6), AF.Exp)
            S.act(gsl(8), b_ps.rearrange("p (t h) -> p t h", h=4), AF.Exp, scale=-1.0)
            S.act(gsl(12), bL_ps.rearrange("p (t h) -> p t h", h=4), AF.Exp)
            S.copy("dve", bLs, bL_ps)
            S.copy("dve", bLrow[0:1, slot0 * 4: slot0 * 4 + 32], bLs[0:1, :])
            t1 = gps[0:32, 128:256]
            S.tr(t1, dmb, identF)
            mxc = gb[0:32, 0, 0:1]
            S.add("dve", lambda e_: e_.reduce_max(mxc, t1, AX.X), w=[mxc], r=[t1])
            t2 = gps[0:1, 256:288]
            S.tr(t2, mxc, identF[0:32, 0:32])
            S.copy("dve", mxrow[0:1, slot0 * 4: slot0 * 4 + 32], t2)

        def gate_tile(hT3, c0, n, slot, mcol, sample=False):
            gps = bank(7)
            for kc in range(16):
                S.mm(gps[0:n, 0:8], hT3[:, kc, c0:c0 + n], wg[:, kc, :], kc == 0, kc == 15)
            zf, e, sp, lf, ig = sc(4), sc(4), sc(4), sc(4), sc(4)
            S.tt("dve", zf[0:n], gps[0:n, 4:8], bfbc[0:n], ALU.add)
            S.act(e[0:n], zf[0:n], AF.Exp, scale=-1.0)
            S.act(sp[0:n], e[0:n], AF.Ln, bias=1.0)
            S.ts("dve", lf[0:n], sp[0:n], -1.0, None, ALU.mult)
            S.tt("dve", ig[0:n], gps[0:n, 0:4], bibc[0:n], ALU.add)
            triM, onesM = (triS, onesS) if sample else (tri, onesF)
            b_ps, bL_ps = gps[:, 16:20], gps[:, 32:36]
            S.mm(b_ps[0:n], triM[0:n, 0:n], lf[0:n], True, True)
            S.mm(bL_ps[0:n], onesM[0:n, 0:n], lf[0:n], True, True)
            dmb = dmbS if sample else sc(4)
            bLs = bLS if sample else sc(4)
            dl = sc(4)
            S.tt("dve", dmb[0:n], ig[0:n], b_ps[0:n], ALU.subtract)
            S.act(gst[0:n, slot, 0:4], dmb[0:n], AF.Exp)
            S.tt("dve", dl[0:n], dmb[0:n], bL_ps[0:n], ALU.add)
            S.act(gst[0:n, slot, 4:8], dl[0:n], AF.Exp)
            S.act(gst[0:n, slot, 8:12], b_ps[0:n], AF.Exp, scale=-1.0)
            S.act(gst[0:n, slot, 12:16], bL_ps[0:n], AF.Exp)
            S.copy("dve", bLs[0:n], bL_ps[0:n])

        gate_group(hTP, 0, 0)

        if stop_after == "p0":
            dbg = dout("dbg", [128, 16 * NPFX // 2])
            S.dma("sp", dbg, hTP.rearrange("p a b -> p (a b)").bitcast(F32))
            dbg2 = dout("dbg2", [128, 17 * 16])
            S.dma("sp", dbg2, gst.rearrange("p a b -> p (a b)"))
            S.emit()
            return nc

        q_1c = B.q
        kP = B.hi([128, 4, 8, 256], BF16)
        vP = B.hi([128, 4, 8, 512], BF16)
        kwp = [B.hi([128, 256], BF16) for _ in range(2)] * 2
        q_P = B.q
        wkP = [B.alloc([128, 16, 256], BF16)] * 2
        wvP = [B.alloc([128, 16, 512], BF16) for _ in range(2)]
        pb = itertools.cycle(range(4))
        pbP = itertools.cycle((4, 5, 6))

        def pproj_gen():
            for h in range(4):
                wk = wkP[h % 2]
                wv = wvP[h % 2]
                wload(wk, w_in[:, COL_K + h * 256: COL_K + (h + 1) * 256])
                wload(wv, w_in[:, COL_V + h * 512: COL_V + (h + 1) * 512])
                for tt in range(8):
                    b_ = bank(next(pbP))
                    for kc in range(16):
                        S.mm(b_[:, 0:256], hTP[:, kc, tt * 128:(tt + 1) * 128], wk[:, kc, :], kc == 0, kc == 15)
                    evac(kP[:, h, tt, :], b_[:, 0:256], mul=1.0 / 16)
                    yield
                for tt in range(8):
                    b_ = bank(next(pbP))
                    for kc in range(16):
                        S.mm(b_, hTP[:, kc, tt * 128:(tt + 1) * 128], wv[:, kc, :], kc == 0, kc == 15)
                    evac(vP[:, h, tt, :], b_)
                    yield

        S.copy("dve", hT[:, :, 0:HALO], hTP[:, :, NPFX - HALO:NPFX])
        interleave(pproj_gen(), norm_gen([(NPFX + tt * 128, 128 if tt < 8 else 64, HALO + tt * 128)
                                          for tt in range(9)], hT, nrm_x, nrm_b), every=6, lead=2)
        gate_group(hT, HALO, 8)
        gate_tile(hT, HALO + 8 * 128, 64, 16, 16, sample=True)

        def state_update(h, ktile, vtile, slot, kw, n=128):
            S.ts("dve", kw[0:n], ktile, gst[0:n, slot, 4 + h:5 + h], None, ALU.mult)
            un = bank(2)[:, 2 * h:2 * h + 2]
            for dc in range(2):
                S.mm(un[:, dc:dc + 1], kw[0:n, dc * 128:(dc + 1) * 128], onesB[0:n, 0:1], True, True)
            for dc in range(2):
                U = bank(4 + h)
                S.mm(U, kw[0:n, dc * 128:(dc + 1) * 128], vtile, True, True)
                S.stt("dve", Zst[:, h, dc, :], Zst[:, h, dc, :], gst[:, slot, 12 + h:13 + h], U, ALU.mult, ALU.add)
            S.stt("dve", nZ[:, h, :], nZ[:, h, :], gst[:, slot, 12 + h:13 + h], un, ALU.mult, ALU.add)

        def prec_gen():
            for tt in range(8):
                for h in range(4):
                    state_update(h, kP[:, h, tt, :], vP[:, h, tt, :], tt, kwp[h])
                    yield

        B.p = m_hT
        bmT = B.alloc([128, 16, T], BF16)
        m_h = B.p
        for t_ in range(16):
            tmp = sc(4)
            S.tt("dve", tmp[0:1], mrun, mxrow[0:1, t_ * 4:(t_ + 1) * 4], ALU.max)
            S.tt("dve", mrun, tmp[0:1], bLrow[0:1, t_ * 4:(t_ + 1) * 4], ALU.add)
        mrep = bank(7)[:, 320:324]
        S.mm(mrep, onesF[0:1, :], mrun, True, True)
        S.act(emf, mrep, AF.Exp, scale=-1.0)
        S.dma("sp", mP.rearrange("h o -> o h"), mrun)

        if stop_after == "pP":
            dbg = dout("dbg", [128, 4096])
            S.dma("sp", dbg, Zst.rearrange("p a b c -> p (a b c)"))
            dbg2 = dout("dbg2", [128, 17 * 16])
            S.dma("sp", dbg2, gst.rearrange("p a b -> p (a b)"))
            S.emit()
            return nc

        B.p = m_h
        B.q = q_1c
        S.dma("pool", maskq.rearrange("p a b -> p (a b)"), cMq.partition_broadcast(128).rearrange("p a b -> p (a b)"))
        emR = B.alloc([128, 64], F32)
        S.dma("sp", emR, sm.partition_broadcast(128).rearrange("p a b -> p (a b)"))
        S.act(emR, emR, AF.Exp)
        mprevT = B.alloc([4, 16], F32)
        S.dma("sp", mprevT, sm.rearrange("o (s h) -> (o h) s", h=4), allow_slow_non_contiguous=True)
        wslots = [B.alloc([128, 16, 512], BF16) for _ in range(2)]
        qT = B.alloc([128, 2, T], BF16)
        kT = B.alloc([128, 2, T], BF16)
        ktok = B.alloc([128, 9, 256], BF16)
        vtok = B.alloc([128, 9, 512], BF16)
        assert B.p <= q_P, ("head-0 projection buffers overlap the prefix k/v", B.p, q_P)
        wslots.append(B.alloc([128, 16, 512], BF16))
        ghbc = B.alloc([128, 512], F32)
        Zbf = B.alloc([128, 2, 512], BF16)
        nbf = B.alloc([128, 2], BF16)
        kw = B.alloc([128, 256], BF16)
        Sp = B.alloc([128, 128], BF16)
        so = B.alloc([128, 512], F32)
        bmb = B.alloc([128, 512], BF16)
        junk = bmb
        Cs = [B.alloc([128, 2, 512], F32) for _ in range(3)]
        Cb = [B.alloc([128, 2, 512], BF16) for _ in range(1)]
        Cn = [B.alloc([128, 2, 512], F32) for _ in range(2)]
        qm = B.alloc([128, 2, 16, 64], BF16)
        nsT = B.alloc([128, 2, 16], F32)
        nsb = B.alloc([128, 2, 16], BF16)
        wkm = B.alloc([64, 16], F32)
        wkmb = B.alloc([64, 16], BF16)
        kwm = [B.alloc([64, 256], BF16)] * 2
        decR = B.alloc([128, 64], F32)
        nsq = B.alloc([16, 256], F32)
        small = B.alloc([64, 64], F32)
        m_1c_end = B.p

        mnew_sm = B.alloc([16, 4], F32)
        dec_sm = B.alloc([16, 4], F32)
        wkS = B.alloc([64, 4], F32)
        BD = B.alloc([4, 4, 16], F32)

        def sample_scalars():
            tps = bank(7)
            S.tr(tps[0:4, 128:192], dmbS[0:64, 0:4], identF[0:64, 0:64])
            mxT = small[0:4, 0:16]
            S.add("dve", lambda e_: e_.reduce_max(mxT, tps[0:4, 128:192].rearrange("p (s j) -> p s j", j=4), AX.X),
                  w=[mxT], r=[tps[0:4, 128:192]])
            S.tr(tps[0:4, 256:320], bLS[0:64, 0:4], identF[0:64, 0:64])
            bLT = small[0:4, 16:32]
            S.copy("dve", bLT, tps[0:4, 256:320].rearrange("p (s j) -> p s j", j=4)[:, :, 0])
            mnewT = small[0:4, 32:48]
            S.tt("dve", mnewT, mprevT, mxT, ALU.max)
            S.tt("dve", mnewT, mnewT, bLT, ALU.add)
            S.dma("sp", mS.rearrange("s h -> h s"), mnewT, allow_slow_non_contiguous=True)
            decT = small[0:4, 48:64]
            S.tt("dve", decT, bLT, mprevT, ALU.add)
            S.tt("dve", decT, decT, mnewT, ALU.subtract)
            S.act(decT, decT, AF.Exp)
            S.tr(tps[0:16, 384:388], mnewT, identF[0:4, 0:4])
            S.copy("dve", mnew_sm, tps[0:16, 384:388])
            S.tr(tps[0:16, 400:404], decT, identF[0:4, 0:4])
            S.copy("dve", dec_sm, tps[0:16, 400:404])
            mtok = tps[0:64, 416:420]
            S.mm(mtok, selT, mnew_sm, True, True)
            S.tt("dve", wkS, dmbS, bLS, ALU.add)
            S.tt("dve", wkS, wkS, mtok, ALU.subtract)
            S.act(wkS, wkS, AF.Exp)
            S.tt("dve", BD, decT.unsqueeze(1).broadcast_to([4, 4, 16]),
                 identF[0:4, 0:4].unsqueeze(2).broadcast_to([4, 4, 16]), ALU.mult)
            drep = bank(6)[:, 0:64]
            S.mm(drep, onesF[0:4, :], BD.rearrange("p a b -> p (a b)"), True, True)
            S.copy("dve", decR, drep)

        ghb = [ghbc, ghbc]
        so2 = [so, B.alloc([128, 512], F32)]
        Sp2 = [Sp, B.alloc([128, 128], BF16)]
        kw2 = [kw, B.alloc([128, 256], BF16)]
        bmb2 = [bmb, B.alloc([128, 512], BF16)]
        bmb3 = bmb2 + [B.alloc([128, 512], BF16)]
        qTs = B.alloc([128, 2, 64], BF16)
        ktoks = B.alloc([64, 256], BF16)
        vtoks = B.alloc([64, 512], BF16)
        sgo_s = B.alloc([64, 512], F32)
        Sp_s = B.alloc([64, 64], BF16)

        pb2 = itertools.cycle(range(2))

        def qkv_loads(h):
            wqk, wv = wslots[0], wslots[1]
            wload(wqk[:, :, 0:256], w_in[:, COL_Q + h * 256: COL_Q + (h + 1) * 256])
            wload(wqk[:, :, 256:512], w_in[:, COL_K + h * 256: COL_K + (h + 1) * 256])
            wload(wv, w_in[:, COL_V + h * 512: COL_V + (h + 1) * 512])

        def proj_gen(h):
            wqk, wv, wo = wslots[0], wslots[1], wslots[2]
            if h == 0:
                qkv_loads(0)
            if h > 0:
                wload(wo, w_in[:, COL_O + h * 512: COL_O + (h + 1) * 512])
            for which, dstT, mul in ((0, qT, None), (1, kT, 1.0 / 16)):
                for dc in range(2):
                    for (t0, n) in ((0, 512), (512, 512), (1024, 64)):
                        b_ = bank(next(pb2))
                        for kc in range(16):
                            S.mm(b_[:, 0:n], wqk[:, kc, which * 256 + dc * 128: which * 256 + (dc + 1) * 128],
                                 hT[:, kc, HALO + t0: HALO + t0 + n], kc == 0, kc == 15)
                        evac(dstT[:, dc, t0:t0 + n], b_[:, 0:n], mul=mul)
                        yield
            for tt in range(9):
                n = 128 if tt < 8 else 64
                c0 = HALO + tt * 128
                b_ = bank(next(pb2))
                for kc in range(16):
                    S.mm(b_[0:n, 0:256], hT[:, kc, c0:c0 + n], wqk[:, kc, 256:512], kc == 0, kc == 15)
                evac(ktok[0:n, tt, :], b_[0:n, 0:256], mul=1.0 / 16)
                yield
                b_ = bank(next(pb2))
                for kc in range(16):
                    S.mm(b_[0:n, :], hT[:, kc, c0:c0 + n], wv[:, kc, :], kc == 0, kc == 15)
                evac(vtok[0:n, tt, :], b_[0:n, :])
                yield

        def sample_prep(h):
            wo = wslots[2]
            c0 = HALO + NM
            ops = bank(6)
            for kc in range(16):
                S.mm(ops[0:64, :], hT[:, kc, c0:c0 + 64], wo[:, kc, :], kc == 0, kc == 15)
            S.act(sgo_s, ops[0:64, :], AF.Sigmoid)
            STp = bank(0)[0:64, 0:64]
            for dc in range(2):
                S.mm(STp, kT[:, dc, NM:NM + 64], qT[:, dc, NM:NM + 64], dc == 0, dc == 1)
            S.stt("dve", Sp_s, STp, gst[0:64, 16, h:h + 1], triS, ALU.mult, ALU.mult)
            S.copy("dve", qTs, qT[:, :, NM:NM + 64])
            S.copy("dve", ktoks, ktok[0:64, 8, :])
            S.copy("dve", vtoks, vtok[0:64, 8, :])
            for dc in range(2):
                S.tt("dve", qm[:, dc], qTs[:, dc, :].unsqueeze(1).broadcast_to([128, 16, 64]), maskq, ALU.mult)
            S.dma("sp", nsq, sn[:, h, :])
            for dc in range(2):
                tpn = bank(7)[:, 64 + dc * 16: 80 + dc * 16]
                S.tr(tpn, nsq[0:16, dc * 128:(dc + 1) * 128], identF[0:16, 0:16])
                S.copy("dve", nsT[:, dc, :], tpn)
            emh = emR.rearrange("p (s hh) -> p s hh", hh=4)[:, :, h]
            S.tt("dve", nsb, nsT, emh.unsqueeze(1).broadcast_to([128, 2, 16]), ALU.mult)

        def sample_gen(h):
            Np = bank(2)[0:64, :]
            Dp = bank(3)[0:64, 0:1]

            def kwm_for(s_):
                S.ts("dve", wkm[:, s_:s_ + 1], sel[:, s_:s_ + 1], wkS[:, h:h + 1], None, ALU.mult)
                S.ts("dve", kwm[0], ktoks, wkm[:, s_:s_ + 1], None, ALU.mult)

            kwm_for(0)
            for s_ in range(NSEQ):
                cs = Cs[s_ % 3]
                cb = Cb[0]
                S.dma("sp", cs, sC[s_, h].rearrange("(dc p) e -> p dc e", p=128))
                S.add("act", lambda e_, cb=cb, cs=cs, s_=s_: e_.mul(
                    cb.rearrange("p a b -> p (a b)"), cs.rearrange("p a b -> p (a b)"),
                    emR[:, s_ * 4 + h: s_ * 4 + h + 1]),
                    w=[cb], r=[cs, emR[:, s_ * 4 + h: s_ * 4 + h + 1]])
                yield
                for dc in range(2):
                    S.mm(Np, qm[:, dc, s_, :], cb[:, dc, :], (s_ == 0 and dc == 0), False)
                for dc in range(2):
                    S.mm(Dp, qm[:, dc, s_, :], nsb[:, dc, s_:s_ + 1], (s_ == 0 and dc == 0), False)
                kwm_ = kwm[0]
                cn = Cn[s_ % 2]
                for dc in range(2):
                    U = bank(4 + dc)
                    S.mm(U, kwm_[:, dc * 128:(dc + 1) * 128], vtoks, True, True)
                    S.stt("dve", cn[:, dc, :], cs[:, dc, :], decR[:, h * 16 + s_: h * 16 + s_ + 1], U,
                          ALU.mult, ALU.add)
                S.dma("pool", CS[s_, h].rearrange("(dc p) e -> p dc e", p=128), cn)
                if s_ + 1 < NSEQ:
                    kwm_for(s_ + 1)
                yield

        def finish_tile(h, n, Np, Dp, so_, ebcol, bmb_):
            d2, d3, d4 = sc(), sc(), sc()
            S.act(d2[0:n], Dp, AF.Abs)
            S.tt("dve", d3[0:n], d2[0:n], ebcol, ALU.max)
            S.add("dve", lambda e_, d3=d3, d4=d4, n=n: e_.reciprocal(d4[0:n], d3[0:n]), w=[d4[0:n]], r=[d3[0:n]])
            S.add("act", lambda e_, d4=d4, n=n: e_.mul(Np, Np, d4[0:n]), w=[Np], r=[Np, d4[0:n]])
            ss = sumsq(bmb_[0:n], Np, n)
            rstd = rsqrt_pow(ss, 1.0 / 512, n)
            S.stt("dve", Np, Np, rstd[0:n], ghb[h % 2][0:n, :], ALU.mult, ALU.mult)
            S.tt("dve", bmb_[0:n], Np, so_, ALU.mult)

        def bm_transpose(h, n, t0, bmb_):
            bps = bank(7).bitcast(BF16).rearrange("p (a b) -> p a b", b=128)
            for e4 in range(4):
                S.tr(bps[:, e4, 0:n], bmb_[0:n, e4 * 128:(e4 + 1) * 128], identB[0:n, 0:n])
            evac(bmT[:, h * 4:(h + 1) * 4, t0:t0 + n], bps[:, 0:4, 0:n], eng="act")

        def sample_fin(h):
            Np = bank(2)[0:64, :]
            Dp = bank(3)[0:64, 0:1]
            S.mm(Np, Sp_s, vtoks, False, True)
            S.mm(Dp, Sp_s, onesB[0:64, 0:1], False, True)
            S.copy("dve", wkmb, wkm)
            npn = bank(6)[0:16, 0:256]
            S.mm(npn, wkmb, ktoks, True, True)
            S.stt("dve", nsq, nsq, dec_sm[:, h:h + 1], npn, ALU.mult, ALU.add)
            S.dma("sp", nS[:, h, :], nsq)
            finish_tile(h, 64, Np, Dp, sgo_s, gst[0:64, 16, 8 + h:9 + h], bmb2[0])
            bm_transpose(h, 64, NM, bmb2[0])

        def prompt_loop(h):
            wo = wslots[2]
            S.dma("sp", ghbc, g_head[:, h * 512:(h + 1) * 512].partition_broadcast(128).rearrange("p a b -> p (a b)"))
            S.copy("act", Zbf.rearrange("p a b -> p (a b)"), Zst[:, h].rearrange("p a b -> p (a b)"))
            S.copy("act", nbf, nZ[:, h, :])

            def stA(tt):
                c0 = HALO + tt * 128
                t0 = tt * 128
                slot = 8 + tt
                ops = bank(6)
                for kc in range(16):
                    S.mm(ops, hT[:, kc, c0:c0 + 128], wo[:, kc, :], kc == 0, kc == 15)
                S.act(so2[tt % 2], ops, AF.Sigmoid)
                STp = bank(0)[:, 0:128]
                for dc in range(2):
                    S.mm(STp, kT[:, dc, t0:t0 + 128], qT[:, dc, t0:t0 + 128], dc == 0, dc == 1)
                S.stt("dve", Sp2[tt % 2], STp, gst[:, slot, h:h + 1], tri, ALU.mult, ALU.mult)
                S.ts("dve", kw2[tt % 2], ktok[:, tt, :], gst[:, slot, 4 + h:5 + h], None, ALU.mult)

            def stB(tt):
                t0 = tt * 128
                slot = 8 + tt
                Np = bank(2 + tt % 2)
                Dp = bank(1)[:, 0:1]
                for dc in range(2):
                    S.mm(Np, qT[:, dc, t0:t0 + 128], Zbf[:, dc, :], dc == 0, False)
                S.mm(Np, Sp2[tt % 2], vtok[:, tt, :], False, True)
                for dc in range(2):
                    S.mm(Dp, qT[:, dc, t0:t0 + 128], nbf[:, dc:dc + 1], dc == 0, False)
                S.mm(Dp, Sp2[tt % 2], onesB[:, 0:1], False, True)
                kw_ = kw2[tt % 2]
                un = bank(1)[:, 8:10]
                for dc in range(2):
                    S.mm(un[:, dc:dc + 1], kw_[:, dc * 128:(dc + 1) * 128], onesB[:, 0:1], True, True)
                for dc in range(2):
                    U = bank(4 + dc)
                    S.mm(U, kw_[:, dc * 128:(dc + 1) * 128], vtok[:, tt, :], True, True)
                    S.stt("dve", Zst[:, h, dc, :], Zst[:, h, dc, :], gst[:, slot, 12 + h:13 + h], U, ALU.mult, ALU.add)
                S.stt("dve", nZ[:, h, :], nZ[:, h, :], gst[:, slot, 12 + h:13 + h], un, ALU.mult, ALU.add)
                if tt < 7:
                    S.copy("act", Zbf.rearrange("p a b -> p (a b)"), Zst[:, h].rearrange("p a b -> p (a b)"))
                    S.copy("act", nbf, nZ[:, h, :])
                else:
                    co = Cn[0]
                    S.ts("dve", co.rearrange("p a b -> p (a b)"), Zst[:, h].rearrange("p a b -> p (a b)"),
                         emf[:, h:h + 1], None, ALU.mult)
                    S.dma("sp", CP[h].rearrange("(dc p) e -> p dc e", p=128), co)
                    no = sc(2)
                    S.ts("dve", no, nZ[:, h, :], emf[:, h:h + 1], None, ALU.mult)
                    S.dma("sp", nP[h:h + 1, :].rearrange("o (dc p) -> p (o dc)", p=128), no,
                          allow_slow_non_contiguous=True)

            def stC1(tt):
                slot = 8 + tt
                finish_tile(h, 128, bank(2 + tt % 2), bank(1)[:, 0:1], so2[tt % 2],
                            gst[:, slot, 8 + h:9 + h], bmb3[tt % 3])

            stA(0)
            for tt in range(8):
                if tt + 1 < 8:
                    stA(tt + 1)
                if tt >= 2:
                    bm_transpose(h, 128, (tt - 2) * 128, bmb3[(tt - 2) % 3])
                stB(tt)
                stC1(tt)
            bm_transpose(h, 128, 6 * 128, bmb3[6 % 3])
            bm_transpose(h, 128, 7 * 128, bmb3[7 % 3])

        for h in range(4):
            if h == 0:
                interleave(proj_gen(0), prec_gen(), every=1, lead=1)
                wload(wslots[2], w_in[:, COL_O:COL_O + 512])
                sample_scalars()
            else:
                interleave(proj_gen(h), sample_gen(h - 1), every=1, lead=2)
            if h > 0:
                sample_fin(h - 1)
            if h + 1 < 4:
                qkv_loads(h + 1)
            sample_prep(h)
            prompt_loop(h)
        for _ in sample_gen(3):
            pass
        sample_fin(3)

        if stop_after == "p1c":
            dbg = dout("dbg", [128, 16 * T // 2])
            S.dma("sp", dbg, bmT.rearrange("p a b -> p (a b)").bitcast(F32))
            S.emit()
            return nc

        B.p = m_h
        aT = B.alloc([128, 8, T], BF16)
        m_a = B.p
        uT = B.alloc([128, 8, HALO + T], F32)
        zT = B.alloc([128, 8, T], BF16)
        B.q = TOP
        wpg = B.alloc([128, 8, 256], BF16)
        posb = B.alloc([128, T], F32)
        ext = B.alloc([128, 8, 16, 20], F32)
        exa = B.alloc([128, 16, 20], F32)
        exb = B.alloc([128, 16, 20], F32)
        pso = B.alloc([128, 1024], F32)
        m_ov = B.p
        wslots = [B.alloc([128, 16, 256], BF16) for _ in range(2)]
        B.p = m_ov
        icnts = [B.alloc([128, NM], F32) for _ in range(3)]
        icnts.append(icnts[0])
        sa = B.alloc([128, HALO + NM], F32)
        sb = B.alloc([128, HALO + NM], F32)
        sa_p = B.alloc([128, HALO + NM], F32)
        sb_p = B.alloc([128, HALO + NM], F32)
        exa_p = B.alloc([128, 16, 20], F32)
        exb_p = B.alloc([128, 16, 20], F32)
        pso2 = sa_p[:, 0:1024]
        B.p = max(B.p, m_ov + 2 * 8192)
        S.dma("sp", posb, pos.partition_broadcast(128).rearrange("p a b -> p (a b)"))
        S.dma("pool", wpg.rearrange("p (g k) n -> p g k n", k=2),
              w_pool.rearrange("g (k p) n -> p g k n", p=128))
        for half in range(4):
            wu = wslots[half % 2]
            wload(wu, w_in[:, COL_U + half * 256: COL_U + (half + 1) * 256])
            for c4 in range(2):
                c = half * 2 + c4
                for (t0, n) in ((0, 512), (512, 512), (1024, HALO + T - 1024)):
                    b_ = bank(next(pb))
                    for kc in range(16):
                        S.mm(b_[:, 0:n], wu[:, kc, c4 * 128:(c4 + 1) * 128], hT[:, kc, t0:t0 + n], kc == 0, kc == 15)
                    evac(uT[:, c, t0:t0 + n], b_[:, 0:n])
        spf = spool.rearrange("s r c -> (s r) c")
        bufTc = B.alloc([128, 8, 240], F32)
        for blk, (r0, rn) in enumerate(((0, 128), (128, 112))):
            S.dma("sp", pso[0:rn, :], spf[r0:r0 + rn, :])
            for c in range(8):
                tp = bank(next(pb))
                S.tr(tp[:, 0:rn], pso[0:rn, c * 128:(c + 1) * 128], identF[0:rn, 0:rn])
                evac(bufTc[:, c, r0:r0 + rn], tp[:, 0:rn])
        S.memset("pool", ext.rearrange("p a b c -> p (a b c)"), 0.0)
        def mk_icnt(g, wdw):
            S.ts("dve", icnts[g], posb[:, 0:NM], 1.0, float(wdw), ALU.add, ALU.min)
            S.add("dve", lambda e_, ic=icnts[g]: e_.reciprocal(ic, ic), w=[icnts[g]], r=[icnts[g]])

        for g, wdw in enumerate((2, 4, 8)):
            mk_icnt(g, wdw)
        for g, wdw in enumerate((2, 4, 8, 16)):
            icnt = icnts[g]
            if g == 3:
                mk_icnt(3, 16)
            for c in (2 * g, 2 * g + 1):
                en = "dve"
                S.copy(en, ext[:, c, :, 0:15], bufTc[:, c, :].rearrange("p (s r) -> p s r", r=15))
                S.copy(en, ext[:, c, :, 15:19],
                       uT[:, c, HALO + NM: HALO + T].rearrange("p (s j) -> p s j", j=4))
                cur = uT[:, c, 0:HALO + NM]
                step = 1
                bufs = [sa_p, sb_p] if en == "pool" else [sa, sb]
                bi = 0
                while step < wdw:
                    nxt = bufs[bi]
                    bi ^= 1
                    S.tt(en, nxt[:, step:], cur[:, step:], cur[:, 0:HALO + NM - step], ALU.add)
                    cur = nxt
                    step *= 2
                mean = bufs[bi]
                S.tt(en, mean[:, 0:NM], cur[:, HALO:], icnt, ALU.mult)
                S.tt(en, zT[:, c, 0:NM], mean[:, 0:NM], uT[:, c, HALO:HALO + NM], ALU.subtract)
                cur = ext[:, c]
                step = 1
                bufs = [exa_p, exb_p] if en == "pool" else [exa, exb]
                bi = 0
                while step < wdw:
                    nxt = bufs[bi]
                    bi ^= 1
                    S.tt(en, nxt[:, :, step:19], cur[:, :, step:19], cur[:, :, 0:19 - step], ALU.add)
                    cur = nxt
                    step *= 2
                if en == "dve":
                    S.stt("dve", zT[:, c, NM:T].rearrange("p (s j) -> p s j", j=4), cur[:, :, 15:19], 1.0 / wdw,
                          ext[:, c, :, 15:19], ALU.mult, ALU.subtract)
                else:
                    oth = bufs[bi]
                    S.memset("pool", oth[:, :, 0:4], 1.0 / wdw)
                    S.tt("pool", oth[:, :, 4:8], cur[:, :, 15:19], oth[:, :, 0:4], ALU.mult)
                    S.tt("pool", zT[:, c, NM:T].rearrange("p (s j) -> p s j", j=4), oth[:, :, 4:8],
                         ext[:, c, :, 15:19], ALU.subtract)
        for c in range(8):
            g = c // 2
            for (t0, n) in ((0, 512), (512, 512), (1024, 64)):
                b_ = bank(next(pb))
                for k in range(2):
                    S.mm(b_[:, 0:n], wpg[:, 2 * g + k, (c % 2) * 128:(c % 2) * 128 + 128], zT[:, 2 * g + k, t0:t0 + n],
                         k == 0, k == 1)
                S.add("act", lambda e_, c=c, t0=t0, n=n, b_=b_: e_.mul(aT[:, c, t0:t0 + n], b_[:, 0:n], spT[:, c:c + 1]),
                      w=[aT[:, c, t0:t0 + n]], r=[b_[:, 0:n], spT[:, c:c + 1]])
        for c in range(8):
            tp = bank(next(pb))
            S.tr(tp[0:15, 0:128], uT[:, c, HALO + NM - 15: HALO + NM], identF)
            evac(pso[0:15, c * 128:(c + 1) * 128], tp[0:15, 0:128])
        S.dma("sp", poolP, pso[0:15, :])
        extr = bufTc.rearrange("p c (s r) -> p c s r", r=15)
        for c in range(8):
            S.copy("pool", extr[:, c], ext[:, c, :, 4:19])
        for blk, (r0, rn) in enumerate(((0, 128), (128, 112))):
            dst = pso if blk == 0 else pso2
            for c in range(8):
                tp = bank(next(pb))
                S.tr(tp[0:rn, 0:128], extr[:, c].rearrange("p s r -> p (s r)")[:, r0:r0 + rn], identF)
                evac(dst[0:rn, c * 128:(c + 1) * 128], tp[0:rn, 0:128])
            S.dma("sp", poolS[r0:r0 + rn, :], dst[0:rn, :])

        if stop_after == "p1b":
            dbg = dout("dbg", [128, 8 * T // 2])
            S.dma("sp", dbg, aT.rearrange("p a b -> p (a b)").bitcast(F32))
            S.emit()
            return nc

        B.p = m_a
        B.q = TOP
        mgT = B.hi([128, 16, T], BF16)
        wo_ = [B.hi([128, 16, 512], BF16) for _ in range(2)]
        wsl = [[B.alloc([128, 16, 128], BF16), B.alloc([128, 16, 128], BF16), B.alloc([128, 8, 128], BF16),
                B.alloc([128, 16, 128], BF16)] for _ in range(2)]
        sga = [B.alloc([128, 512], F32) for _ in range(2)]
        sgb = [B.alloc([128, 512], F32) for _ in range(2)]
        t1 = [B.alloc([128, 512], F32) for _ in range(2)]
        it = 0
        for c in range(16):
            wga, wgb, wpa_, wpb_ = wsl[c % 2]
            wload(wga, w_in[:, COL_GA + c * 128: COL_GA + (c + 1) * 128])
            wload(wgb, w_in[:, COL_GB + c * 128: COL_GB + (c + 1) * 128])
            wload(wpa_, w_pa[:, c * 128:(c + 1) * 128])
            wload(wpb_, w_pb[:, c * 128:(c + 1) * 128])
            if c == 1:
                for nn in range(2):
                    wload(wo_[nn], w_out[:, nn * 512:(nn + 1) * 512])
            for (t0, n) in ((0, 512), (512, 512), (1024, 64)):
                bb = (it % 2) * 4
                it += 1
                pga, pgb, pya, pyb = bank(bb), bank(bb + 1), bank(bb + 2), bank(bb + 3)
                for kc in range(16):
                    S.mm(pga[:, 0:n], wga[:, kc, :], hT[:, kc, HALO + t0:HALO + t0 + n], kc == 0, kc == 15)
                for kc in range(16):
                    S.mm(pgb[:, 0:n], wgb[:, kc, :], hT[:, kc, HALO + t0:HALO + t0 + n], kc == 0, kc == 15)
                for kc in range(8):
                    S.mm(pya[:, 0:n], wpa_[:, kc, :], aT[:, kc, t0:t0 + n], kc == 0, kc == 7)
                for kc in range(16):
                    S.mm(pyb[:, 0:n], wpb_[:, kc, :], bmT[:, kc, t0:t0 + n], kc == 0, kc == 15)
                j = it % 2
                S.act(sga[j][:, 0:n], pga[:, 0:n], AF.Sigmoid)
                S.act(sgb[j][:, 0:n], pgb[:, 0:n], AF.Sigmoid)
                S.tt("dve", sga[j][:, 0:n], sga[j][:, 0:n], pya[:, 0:n], ALU.mult)
                S.tt("dve", t1[j][:, 0:n], sgb[j][:, 0:n], pyb[:, 0:n], ALU.mult)
                S.tt("dve", mgT[:, c, t0:t0 + n], sga[j][:, 0:n], t1[j][:, 0:n], ALU.add)

        B.p = m_phase
        h2T = B.alloc([128, 16, T], BF16)
        m_h2 = B.p
        wo_ = wo_ + [B.alloc([128, 16, 512], BF16) for _ in range(2)]
        for nn in range(2, 4):
            wload(wo_[nn], w_out[:, nn * 512:(nn + 1) * 512])
        g1 = B.alloc([128, DM], F32)
        g2 = B.alloc([128, DM], F32)
        S.dma("sp", g1, g_post_mix.partition_broadcast(128).rearrange("p a b -> p (a b)"))
        S.dma("sp", g2, g_pre_ffn.partition_broadcast(128).rearrange("p a b -> p (a b)"))
        xt2 = [B.alloc([128, DM], F32) for _ in range(2)]
        tb = B.alloc([128, DM], F32)
        hb = [B.alloc([128, DM], BF16) for _ in range(2)]
        jk = B.alloc([128, 512], BF16)
        def mm2(tt):
            n = 128 if tt < 8 else 64
            t0 = tt * 128
            S.dma("sp", xt2[tt % 2][0:n], x[NPFX + t0: NPFX + t0 + n, :])
            for nn in range(4):
                b_ = bank((tt % 2) * 4 + nn)
                for kc in range(16):
                    S.mm(b_[0:n, :], mgT[:, kc, t0:t0 + n], wo_[nn][:, kc, :], kc == 0, kc == 15)

        def post2(tt):
            n = 128 if tt < 8 else 64
            t0 = tt * 128
            xt = xt2[tt % 2]
            bb = (tt % 2) * 4
            sst = sc(4)
            S.memset("dve", sst[0:n], 0.0)
            for nn in range(4):
                S.act(jk[0:n], bank(bb + nn)[0:n, :], AF.Square, accum_out=sst[0:n, nn:nn + 1])
            ss = sc()
            S.add("dve", lambda e_, ss=ss, sst=sst, n=n: e_.reduce_sum(ss[0:n], sst[0:n], AX.X), w=[ss[0:n]], r=[sst[0:n]])
            rstd = rsqrt(ss, 1.0 / DM, n)
            for nn in range(4):
                S.stt("dve", tb[0:n, nn * 512:(nn + 1) * 512], bank(bb + nn)[0:n, :], rstd[0:n],
                      g1[0:n, nn * 512:(nn + 1) * 512], ALU.mult, ALU.mult)
            S.tt("dve", xt[0:n], xt[0:n], tb[0:n], ALU.add)
            S.dma("pool", x1s[t0:t0 + n, :], xt[0:n])
            ss2 = sumsq(hb[tt % 2][0:n], xt[0:n], n)
            rstd2 = rsqrt(ss2, 1.0 / DM, n)
            S.stt("dve", hb[tt % 2][0:n], xt[0:n], rstd2[0:n], g2[0:n], ALU.mult, ALU.mult)
            pT3 = ps[:, bb * 512: bb * 512 + 1024].bitcast(BF16).rearrange("p (a b) -> p a b", a=16)
            for kc in range(16):
                S.tr(pT3[:, kc, 0:n], hb[tt % 2][0:n, kc * 128:(kc + 1) * 128], identB[0:n, 0:n])
            evac(h2T[:, :, t0:t0 + n], pT3[:, :, 0:n], eng="act")

        mm2(0)
        for tt in range(9):
            if tt + 1 < 9:
                mm2(tt + 1)
            post2(tt)

        B.p = m_h2
        B.q = TOP
        actT = B.hi([128, 44, T], BF16)
        wgu = [[B.alloc([128, 16, 128], BF16), B.alloc([128, 16, 128], BF16)] for _ in range(3)]
        sil = [B.alloc([128, 512], F32) for _ in range(2)]
        it = 0
        for fc in range(44):
            wg_, wu_ = wgu[fc % 3]
            wload(wg_, w_gate[:, fc * 128:(fc + 1) * 128])
            wload(wu_, w_up[:, fc * 128:(fc + 1) * 128])
            for (t0, n) in ((0, 512), (512, 512), (1024, 64)):
                bb = (it % 4) * 2
                it += 1
                pg, pu = bank(bb), bank(bb + 1)
                for kc in range(16):
                    S.mm(pg[:, 0:n], wg_[:, kc, :], h2T[:, kc, t0:t0 + n], kc == 0, kc == 15)
                for kc in range(16):
                    S.mm(pu[:, 0:n], wu_[:, kc, :], h2T[:, kc, t0:t0 + n], kc == 0, kc == 15)
                sl = sil[it % 2]
                S.act(sl[:, 0:n], pg[:, 0:n], AF.Silu)
                S.tt("dve", actT[:, fc, t0:t0 + n], sl[:, 0:n], pu[:, 0:n], ALU.mult)

        B.p = m_phase
        wd = [B.alloc([128, 44, 256], BF16) for _ in range(2)]
        fe = [B.alloc([128, 256], F32) for _ in range(4)]
        jkf = B.alloc([128, 256], BF16)
        ssf = B.alloc([128, 9, 8], F32)
        rstd5 = B.alloc([128, 16], F32)
        S.memset("dve", ssf.rearrange("p a b -> p (a b)"), 0.0)
        m_p4 = B.p
        wpg_ = [B.alloc([128, 16, 512], BF16) for _ in range(3)]
        m_p4e = B.p
        it = 0
        for nn in range(8):
            w_ = wd[nn % 2]
            v = w_down[:, nn * 256:(nn + 1) * 256].rearrange("(k p) n -> p k n", p=128)
            for k0 in range(0, 44, 11):
                S.dma("pool", w_[:, k0:k0 + 11, :], v[:, k0:k0 + 11, :])
            if nn == 1:
                for j in range(3):
                    wload(wpg_[j], w_ple_gate[:, j * 512:(j + 1) * 512])
            for tt in range(9):
                n = 128 if tt < 8 else 64
                t0 = tt * 128
                b_ = bank(it % 8)
                f_ = fe[it % 4]
                it += 1
                for kc in range(44):
                    S.mm(b_[0:n, 0:256], actT[:, kc, t0:t0 + n], w_[:, kc, :], kc == 0, kc == 43)
                S.act(jkf[0:n], b_[0:n, 0:256], AF.Square, accum_out=ssf[0:n, tt, nn:nn + 1])
                evac(f_[0:n], b_[0:n, 0:256], eng="dve")
                S.dma("sp", fsc[t0:t0 + n, nn * 256:(nn + 1) * 256], f_[0:n])

        B.p = m_phase
        B.q = TOP
        ft = [B.alloc([128, DM], F32) for _ in range(3)]
        x1t = [B.alloc([128, DM], F32) for _ in range(3)]
        assert B.p <= m_p4
        B.p = m_p4e
        wpl = B.alloc([128, 2, DM], BF16)
        wload(wpl, w_ple)
        wpg_ = wpg_ + [B.alloc([128, 16, 512], BF16)]
        wload(wpg_[3], w_ple_gate[:, 3 * 512:4 * 512])
        g3 = B.alloc([128, DM], F32)
        S.dma("sp", g3, g_post_ffn.partition_broadcast(128).rearrange("p a b -> p (a b)"))
        x2b = [B.alloc([128, DM], BF16) for _ in range(2)]
        x2T = [B.alloc([128, 16, 128], BF16) for _ in range(2)]
        pt = [B.alloc([128, 256], F32) for _ in range(2)]
        ptb = [B.alloc([128, 256], BF16) for _ in range(2)]
        pT = [B.alloc([128, 2, 128], BF16) for _ in range(2)]
        sg = [B.alloc([128, 1024], F32) for _ in range(2)]
        yt = [B.alloc([128, DM], F32) for _ in range(2)]
        jk5 = B.alloc([128, DM], BF16)

        ss5 = sc(9)
        S.add("dve", lambda e_: e_.reduce_sum(ss5, ssf, AX.X), w=[ss5], r=[ssf])
        t5a, t5b = sc(9), sc(9)
        S.ts("dve", t5a, ss5, 1.0 / DM, EPS, ALU.mult, ALU.add)
        S.act(t5b, t5a, AF.Ln)
        S.act(rstd5[:, 0:9], t5b, AF.Exp, scale=-0.5)

        def pre5(tt):
            n = 128 if tt < 8 else 64
            t0 = tt * 128
            f_, x1_ = ft[tt % 3], x1t[tt % 3]
            S.dma("sp", f_[0:n], fsc[t0:t0 + n, :])
            S.dma("sp", x1_[0:n], x1s[t0:t0 + n, :])
            S.dma("sp", pt[tt % 2][0:n], p_in[t0:t0 + n, :])
            S.stt("dve", f_[0:n], f_[0:n], rstd5[0:n, tt:tt + 1], g3[0:n], ALU.mult, ALU.mult)
            S.tt("dve", x1_[0:n], x1_[0:n], f_[0:n], ALU.add)
            S.copy("pool", x2b[tt % 2][0:n], x1_[0:n])
            S.copy("pool", ptb[tt % 2][0:n], pt[tt % 2][0:n])

        def preT5(tt):
            n = 128 if tt < 8 else 64
            pT3 = ps[:, 0:1024].bitcast(BF16).rearrange("p (a b) -> p a b", a=16)
            for kc in range(16):
                S.tr(pT3[:, kc, 0:n], x2b[tt % 2][0:n, kc * 128:(kc + 1) * 128], identB[0:n, 0:n])
            evac(x2T[tt % 2][:, :, 0:n], pT3[:, :, 0:n], eng="act")
            pP3 = ps[:, 1024:1536].bitcast(BF16).rearrange("p (a b) -> p a b", b=128)
            for k in range(2):
                S.tr(pP3[:, k, 0:n], ptb[tt % 2][0:n, k * 128:(k + 1) * 128], identB[0:n, 0:n])
            evac(pT[tt % 2][:, :, 0:n], pP3[:, 0:2, 0:n], eng="dve")

        def main5(tt, between=None):
            n = 128 if tt < 8 else 64
            t0 = tt * 128
            x1_ = x1t[tt % 3]
            y_ = yt[tt % 2]
            for hh in range(2):
                sg_ = sg[hh]
                for q in range(2):
                    nn = hh * 2 + q
                    bg, bp = bank(4 + q), bank(6 + q)
                    for kc in range(16):
                        S.mm(bg[0:n, :], x2T[tt % 2][:, kc, 0:n], wpg_[nn][:, kc, :], kc == 0, kc == 15)
                    for k in range(2):
                        S.mm(bp[0:n, :], pT[tt % 2][:, k, 0:n], wpl[:, k, nn * 512:(nn + 1) * 512], k == 0, k == 1)
                if hh == 1 and between is not None:
                    between()
                for q in range(2):
                    bg, bp = bank(4 + q), bank(6 + q)
                    S.act(sg_[0:n, q * 512:(q + 1) * 512], bg[0:n, :], AF.Sigmoid)
                    S.tt("dve", sg_[0:n, q * 512:(q + 1) * 512], sg_[0:n, q * 512:(q + 1) * 512], bp[0:n, :], ALU.mult)
                S.tt("dve", y_[0:n, hh * 1024:(hh + 1) * 1024], sg_[0:n], x1_[0:n, hh * 1024:(hh + 1) * 1024], ALU.add)
            S.dma("pool", y_out[t0:t0 + n, :], y_[0:n])

        pre5(0)
        preT5(0)
        for tt in range(9):
            if tt + 1 < 9:
                pre5(tt + 1)
                main5(tt, between=lambda tt=tt: preT5(tt + 1))
            else:
                main5(tt)

        S.emit()
    return nc


_NC_CACHE = {}


def _consts():
    i = np.arange(128)
    tri = (i[:, None] <= i[None, :]).astype(np.float32)
    j = np.arange(64)
    same = (j[:, None] // 4) == (j[None, :] // 4)
    triS = (same & (j[:, None] <= j[None, :])).astype(np.float32)
    onesS = same.astype(np.float32)
    sel = ((j[:, None] // 4) == np.arange(16)[None, :]).astype(np.float32)
    return {
        "cI": np.eye(128, dtype=np.float32), "cTri": tri, "cTriS": triS, "cOnesS": onesS,
        "cSel": sel, "cSelT": np.ascontiguousarray(sel.T),
        "cMq": np.ascontiguousarray(sel.T).reshape(1, 1024),
    }


def kernel(x_prompt, x_sample, p_prompt, p_sample, state_pool, state_C, state_n, state_m,
           g_pre_mix, w_in, b_i, b_f, w_pool_grp, s_pool, g_head, w_pa, w_pb, w_out,
           g_post_mix, g_pre_ffn, w_gate, w_up, w_down, g_post_ffn, w_ple, w_ple_gate):
    f32 = lambda a: np.ascontiguousarray(np.asarray(a, dtype=np.float32))
    x_prompt, x_sample = f32(x_prompt), f32(x_sample)
    p_prompt, p_sample = f32(p_prompt)[0], f32(p_sample)[0]
    state_pool, state_C, state_n, state_m = f32(state_pool)[0], f32(state_C)[0], f32(state_n)[0], f32(state_m)[0]
    shared = {
        "g_pre_mix": f32(g_pre_mix), "w_in": f32(w_in)[0], "b_i": f32(b_i), "b_f": f32(b_f),
        "w_pool_grp": f32(w_pool_grp)[0], "s_pool": f32(s_pool), "g_head": f32(g_head).reshape(1, 2048),
        "w_pa": f32(w_pa)[0], "w_pb": f32(w_pb)[0], "w_out": f32(w_out)[0], "g_post_mix": f32(g_post_mix),
        "g_pre_ffn": f32(g_pre_ffn), "w_gate": f32(w_gate)[0], "w_up": f32(w_up)[0], "w_down": f32(w_down)[0],
        "g_post_ffn": f32(g_post_ffn), "w_ple": f32(w_ple)[0], "w_ple_gate": f32(w_ple_gate)[0],
    }
    shared.update(_consts())
    in_maps = []
    for c in range(8):
        b, j = c // 2, c % 2
        xs = x_sample[16 * c:16 * c + 16].reshape(64, DM)
        xm = x_prompt[b, j * 1024:(j + 1) * 1024]
        xp = x_prompt[b, 0:1024] if j == 1 else np.zeros((1024, DM), np.float32)
        posv = np.concatenate([np.arange(j * 1024, (j + 1) * 1024), 16384 + np.tile(np.arange(4), 16)])
        m = dict(shared)
        m.update({
            "x": np.ascontiguousarray(np.concatenate([xp, xm, xs], 0)),
            "p": np.ascontiguousarray(np.concatenate([p_prompt[b, j * 1024:(j + 1) * 1024],
                                                      p_sample[16 * c:16 * c + 16].reshape(64, 256)], 0)),
            "pos": posv.astype(np.float32).reshape(1, T),
            "spool": np.ascontiguousarray(state_pool[16 * c:16 * c + 16]),
            "sC": np.ascontiguousarray(state_C[16 * c:16 * c + 16]),
            "sn": np.ascontiguousarray(state_n[16 * c:16 * c + 16]),
            "sm": np.ascontiguousarray(state_m[16 * c:16 * c + 16]).reshape(1, 64),
        })
        in_maps.append(m)
    if "nc" not in _NC_CACHE:
        _NC_CACHE["nc"] = build()
    res = run_bass_kernel_spmd(_NC_CACHE["nc"], in_maps, core_ids=list(range(8)))
    R = res.results
    yp = np.zeros((4, 2048, DM), np.float32)
    ys = np.zeros((128, 4, DM), np.float32)
    pool_p = np.zeros((1, 4, 15, 1024), np.float32)
    C_p = np.zeros((1, 4, 4, 256, 512), np.float32)
    n_p = np.zeros((1, 4, 4, 256), np.float32)
    m_p = np.zeros((1, 4, 4), np.float32)
    pool_s = np.zeros((1, 128, 15, 1024), np.float32)
    C_s = np.zeros((1, 128, 4, 256, 512), np.float32)
    n_s = np.zeros((1, 128, 4, 256), np.float32)
    m_s = np.zeros((1, 128, 4), np.float32)
    for c in range(8):
        b, j = c // 2, c % 2
        r = R[c]
        yp[b, j * 1024:(j + 1) * 1024] = r["y"][0:1024]
        ys[16 * c:16 * c + 16] = r["y"][1024:1088].reshape(16, 4, DM)
        if j == 1:
            pool_p[0, b] = r["poolP"]
            C_p[0, b] = r["CP"]
            n_p[0, b] = r["nP"]
            m_p[0, b] = r["mP"].reshape(4)
        pool_s[0, 16 * c:16 * c + 16] = r["poolS"].reshape(16, 15, 1024)
        C_s[0, 16 * c:16 * c + 16] = r["CS"]
        n_s[0, 16 * c:16 * c + 16] = r["nS"]
        m_s[0, 16 * c:16 * c + 16] = r["mS"]
    return (yp, ys, pool_p, C_p, n_p, m_p, pool_s, C_s, n_s, m_s)
```

```python
import concourse.bass as bass
import concourse.mybir as mybir

F32 = mybir.dt.float32
BF16 = mybir.dt.bfloat16
ALU = mybir.AluOpType
AF = mybir.ActivationFunctionType
AX = mybir.AxisListType

ENGS = ("pe", "act", "dve", "pool", "sp")
NDMASEM = 12
NDMA_Q = {"pool": 4, "sp": 12, "act": 4, "pe": 4, "dve": 4}
_DSZ = {F32: 4, BF16: 2, mybir.dt.int32: 4, mybir.dt.uint8: 1}


def _dsize(dt):
    return _DSZ[dt]


def ap_range(ap):
    t = ap.tensor
    name = t.name
    esz = _dsize(ap.dtype)
    dims = list(ap.ap)
    space = str(ap.space)
    if "DRAM" in space.upper() or "Dram" in space or "dram" in space:
        lo = ap.offset
        hi = lo + sum((c - 1) * abs(s) for s, c in dims) + 1
        return ("d:" + name, lo * esz, hi * esz, False)
    L = dims[0][0]
    fo = ap.offset % L if L > 0 else ap.offset
    hi = fo + sum((c - 1) * abs(s) for s, c in dims[1:]) + 1
    lo_b, hi_b = fo * esz, hi * esz
    if "PSUM" in space.upper() or "Psum" in space:
        b0, b1 = lo_b // 2048, (hi_b - 1) // 2048
        return ("p:" + name, b0 * 2048, (b1 + 1) * 2048, True)
    return ("s:" + name, lo_b, hi_b, False)


class Op:
    __slots__ = ("eng", "fn", "deps", "is_dma", "sem", "semval", "seq", "signal", "idx", "prev_dma")

    def __init__(self, eng, fn, is_dma):
        self.eng = eng
        self.fn = fn
        self.deps = set()
        self.is_dma = is_dma
        self.sem = None
        self.semval = 0
        self.seq = 0
        self.signal = False
        self.prev_dma = None


class Sched:
    def __init__(self, nc):
        self.nc = nc
        self.streams = {e: [] for e in ENGS}
        self.recs = {}
        self.dma_hist = {e: [] for e in ENGS}
        self.nops = 0

    def _touch(self, op, ap, is_write):
        key, lo, hi, excl = ap_range(ap)
        if excl:
            is_write = True
        lst = self.recs.setdefault(key, [])
        keep = []
        for rec in lst:
            rlo, rhi, rop, rw = rec
            if rlo < hi and lo < rhi:
                if is_write or rw:
                    if rop is not op:
                        op.deps.add(rop)
                if is_write and lo <= rlo and rhi <= hi:
                    continue
            keep.append(rec)
        if not is_write:
            keep = [rc for rc in keep if rc[3] or rc[2].eng != op.eng or rc[2].is_dma or op.is_dma
                    or not (lo <= rc[0] and rc[1] <= hi)]
        keep.append([lo, hi, op, is_write])
        self.recs[key] = keep

    def add(self, eng, fn, w=(), r=(), dma=False):
        op = Op(eng, fn, dma)
        op.idx = self.nops
        self.nops += 1
        for ap in r:
            self._touch(op, ap, False)
        for ap in w:
            self._touch(op, ap, True)
        if dma:
            h = self.dma_hist[eng]
            i = len(h)
            nq = NDMA_Q[eng]
            op.sem = (eng, i % nq)
            op.semval = 16 * (i // nq + 1)
            if i >= nq:
                op.prev_dma = h[i - nq]
            h.append(op)
        self.streams[eng].append(op)
        return op

    def dma(self, q, out, in_, **kw):
        return self.add(q, lambda e: e.dma_start(out=out, in_=in_, **kw), w=[out], r=[in_], dma=True)

    def mm(self, out, lhsT, rhs, start, stop, **kw):
        return self.add("pe", lambda e: e.matmul(out, lhsT, rhs, start=start, stop=stop, **kw),
                        w=[out], r=[lhsT, rhs])

    def tr(self, out, in_, ident):
        return self.add("pe", lambda e: e.transpose(out, in_, ident), w=[out], r=[in_, ident])

    def act(self, out, in_, func, bias=None, scale=None, accum_out=None, eng="act"):
        kw = {}
        r = [in_]
        w = [out]
        if bias is not None:
            kw["bias"] = bias
            if not isinstance(bias, (int, float)):
                r.append(bias)
        if scale is not None:
            kw["scale"] = scale
            if not isinstance(scale, (int, float)):
                r.append(scale)
        if accum_out is not None:
            kw["accum_out"] = accum_out
            w.append(accum_out)
        return self.add(eng, lambda e: e.activation(out, in_, func, **kw), w=w, r=r)

    def tt(self, eng, out, in0, in1, op):
        return self.add(eng, lambda e: e.tensor_tensor(out, in0, in1, op), w=[out], r=[in0, in1])

    def ts(self, eng, out, in0, s1, s2, op0, op1=None, accum_out=None):
        r = [in0]
        if not isinstance(s1, (int, float)):
            r.append(s1)
        if s2 is not None and not isinstance(s2, (int, float)):
            r.append(s2)
        w = [out]
        kw = {}
        if accum_out is not None:
            kw["accum_out"] = accum_out
            w.append(accum_out)
        if op1 is None:
            return self.add(eng, lambda e: e.tensor_scalar(out, in0, s1, None, op0, **kw), w=w, r=r)
        return self.add(eng, lambda e: e.tensor_scalar(out, in0, s1, s2, op0, op1, **kw), w=w, r=r)

    def stt(self, eng, out, in0, scalar, in1, op0, op1):
        r = [in0, in1]
        if not isinstance(scalar, (int, float)):
            r.append(scalar)
        return self.add(eng, lambda e: e.scalar_tensor_tensor(out, in0, scalar, in1, op0, op1), w=[out], r=r)

    def copy(self, eng, out, in_):
        if eng == "act":
            return self.add(eng, lambda e: e.copy(out, in_), w=[out], r=[in_])
        return self.add(eng, lambda e: e.tensor_copy(out, in_), w=[out], r=[in_])

    def memset(self, eng, ap, val):
        return self.add(eng, lambda e: e.memset(ap, val), w=[ap])

    def finalize(self):
        for e in ENGS:
            for op in self.streams[e]:
                for d in op.deps:
                    if d.is_dma:
                        continue
                    if d.eng == op.eng and d.eng == "pe":
                        continue
                    d.signal = True
        for e in ENGS:
            n = 0
            for op in self.streams[e]:
                if op.signal and not op.is_dma:
                    n += 1
                    op.seq = n

    def emit(self, final_waits=True):
        nc = self.nc
        self.finalize()
        import contextlib
        with contextlib.ExitStack() as st:
            esem = {e: st.enter_context(nc.semaphore("es_" + e)) for e in ENGS}
            dsem = {}
            for e in ENGS:
                if self.dma_hist[e]:
                    for i in range(NDMASEM):
                        dsem[(e, i)] = st.enter_context(nc.semaphore("ds_%s_%d" % (e, i)))
            block = st.enter_context(nc.Block())
            streams = self.streams
            dma_hist = self.dma_hist

            def run(ename, eng):
                seen = {}

                def wait(sem_key, sem, val):
                    if seen.get(sem_key, 0) >= val:
                        return
                    seen[sem_key] = val
                    eng.wait_ge(sem, val)

                for op in streams[ename]:
                    if op.prev_dma is not None:
                        p = op.prev_dma
                        wait(("d",) + p.sem, dsem[p.sem], p.semval)
                    for d in sorted(op.deps, key=lambda o: o.idx):
                        if d.is_dma:
                            wait(("d",) + d.sem, dsem[d.sem], d.semval)
                        else:
                            if d.eng == ename and ename == "pe":
                                continue
                            wait(("e", d.eng), esem[d.eng], d.seq)
                    ins = op.fn(eng)
                    if op.is_dma:
                        ins.then_inc(dsem[op.sem], 16)
                    elif op.signal:
                        ins.then_inc(esem[ename], 1)
                if final_waits:
                    h = dma_hist[ename]
                    last = {}
                    for p in h:
                        last[p.sem] = p.semval
                    for k, v in last.items():
                        wait(("d",) + k, dsem[k], v)

            @block.tensor
            def _(eng):
                run("pe", eng)

            @block.scalar
            def _(eng):
                run("act", eng)

            @block.vector
            def _(eng):
                run("dve", eng)

            @block.gpsimd
            def _(eng):
                run("pool", eng)

            @block.sync
            def _(eng):
                run("sp", eng)


class Arena:
    def __init__(self, big_f32, nbytes):
        self.big = big_f32
        self.nbytes = nbytes

    def view(self, off, shape, dt):
        esz = _dsize(dt)
        n = 1
        for s in shape[1:]:
            n *= s
        nb = n * esz
        assert off % 4 == 0 and off + nb <= self.nbytes, (off, nb, self.nbytes)
        nb4 = (nb + 3) // 4
        v = self.big[0:shape[0], off // 4: off // 4 + nb4]
        if dt != F32:
            v = v.bitcast(dt)
            v = v[:, 0:n]
        if len(shape) == 3:
            v = v.rearrange("p (a b) -> p a b", a=shape[1])
        elif len(shape) == 4:
            v = v.rearrange("p (a b c) -> p a b c", a=shape[1], b=shape[2])
        return v


import contextlib
import itertools
import numpy as np
from concourse.bass_utils import run_bass_kernel_spmd

DM = 2048
NPFX = 1024
NM = 1024
NS = 64
T = NM + NS
NSEQ = 16
DFF = 5632
COL_U, COL_Q, COL_K, COL_V, COL_O, COL_IG, COL_GA, COL_GB = 0, 1024, 2048, 3072, 5120, 7168, 7176, 9224
EPS = 1e-6
HALO = 16
SBUF_F32 = 53200


class Bump:
    def __init__(self, arena, start, limit):
        self.A, self.p, self.q = arena, start, limit
        self.peak = 0

    def _nb(self, shape, dt):
        n = 1
        for s in shape[1:]:
            n *= s
        return (n * _dsize(dt) + 31) // 32 * 32

    def alloc(self, shape, dt):
        nb = self._nb(shape, dt)
        v = self.A.view(self.p, shape, dt)
        self.p += nb
        assert self.p <= self.q, ("SBUF overflow", self.p, self.q)
        return v

    def hi(self, shape, dt):
        nb = self._nb(shape, dt)
        self.q -= nb
        assert self.p <= self.q, ("SBUF overflow", self.p, self.q)
        return self.A.view(self.q, shape, dt)


def build(stop_after=None):
    nc = bass.Bass("TRN2", target_bir_lowering=False)

    def din(name, shape):
        return nc.dram_tensor(name, shape, F32, kind="ExternalInput").ap()

    def dout(name, shape):
        return nc.dram_tensor(name, shape, F32, kind="ExternalOutput").ap()

    x = din("x", [NPFX + T, DM])
    p_in = din("p", [T, 256])
    pos = din("pos", [1, T])
    spool = din("spool", [NSEQ, 15, 1024])
    sC = din("sC", [NSEQ, 4, 256, 512])
    sn = din("sn", [NSEQ, 4, 256])
    sm = din("sm", [1, NSEQ * 4])
    g_pre_mix = din("g_pre_mix", [1, DM])
    w_in = din("w_in", [DM, 11272])
    b_i = din("b_i", [1, 4])
    b_f = din("b_f", [1, 4])
    w_pool = din("w_pool_grp", [4, 256, 256])
    s_pool = din("s_pool", [1, 1024])
    g_head = din("g_head", [1, 2048])
    w_pa = din("w_pa", [1024, DM])
    w_pb = din("w_pb", [DM, DM])
    w_out = din("w_out", [DM, DM])
    g_post_mix = din("g_post_mix", [1, DM])
    g_pre_ffn = din("g_pre_ffn", [1, DM])
    w_gate = din("w_gate", [DM, DFF])
    w_up = din("w_up", [DM, DFF])
    w_down = din("w_down", [DFF, DM])
    g_post_ffn = din("g_post_ffn", [1, DM])
    w_ple = din("w_ple", [256, DM])
    w_ple_gate = din("w_ple_gate", [DM, DM])
    cI = din("cI", [128, 128])
    cTri = din("cTri", [128, 128])
    cTriS = din("cTriS", [64, 64])
    cOnesS = din("cOnesS", [64, 64])
    cSel = din("cSel", [64, 16])
    cSelT = din("cSelT", [16, 64])
    cMq = din("cMq", [1, 1024])

    y_out = dout("y", [T, DM])
    poolP = dout("poolP", [15, 1024])
    CP = dout("CP", [4, 256, 512])
    nP = dout("nP", [4, 256])
    mP = dout("mP", [4, 1])
    poolS = dout("poolS", [NSEQ * 15, 1024])
    CS = dout("CS", [NSEQ, 4, 256, 512])
    nS = dout("nS", [NSEQ, 4, 256])
    mS = dout("mS", [NSEQ, 4])

    x1s = nc.dram_tensor("x1s", [T, DM], F32).ap()
    fsc = nc.dram_tensor("fsc", [T, DM], F32).ap()

    with contextlib.ExitStack() as st:
        big = st.enter_context(nc.sbuf_tensor("big", [128, SBUF_F32], F32))
        ps = st.enter_context(nc.psum_tensor("ps", [128, 4096], F32))
        A = Arena(big, SBUF_F32 * 4)
        S = Sched(nc)
        B = Bump(A, 0, SBUF_F32 * 4)

        def bank(i, n=512):
            return ps[:, i * 512: i * 512 + n]

        identF = B.alloc([128, 128], F32)
        identB = B.alloc([128, 128], BF16)
        tri = B.alloc([128, 128], F32)
        onesF = B.alloc([128, 128], F32)
        onesB = B.alloc([128, 8], BF16)
        triS = B.alloc([64, 64], F32)
        onesS = B.alloc([64, 64], F32)
        sel = B.alloc([64, 16], F32)
        selT = B.alloc([16, 64], F32)
        off_maskq = B.p
        maskq = B.alloc([128, 16, 64], BF16)
        gbuf = A.view(off_maskq, [128, 8, 32], F32)
        gT = B.alloc([128, 16], F32)
        spT = B.alloc([128, 8], F32)
        off_bibc = B.p
        bibc = B.alloc([128, 4], F32)
        mhalf = A.view(off_bibc + 16, [128, 1], F32)
        bfbc = B.alloc([128, 4], F32)
        gst = B.alloc([128, 17, 16], F32)
        dmbS = B.alloc([64, 4], F32)
        bLS = B.alloc([64, 4], F32)
        mxrow = B.alloc([1, 64], F32)
        bLrow = B.alloc([1, 64], F32)
        mrun = B.alloc([1, 4], F32)
        emf = B.alloc([128, 4], F32)
        scr = B.alloc([128, 64], F32)
        wg = B.alloc([128, 16, 8], BF16)
        TOP = B.q
        Zst = B.hi([128, 4, 2, 512], F32)
        nZ = B.hi([128, 4, 2], F32)
        _sc = [0]

        def sc(n=1):
            if _sc[0] + n > 64:
                _sc[0] = 0
            v = scr[:, _sc[0]: _sc[0] + n]
            _sc[0] += n
            return v

        S.dma("sp", identF, cI)
        S.dma("sp", tri, cTri)
        S.dma("sp", triS, cTriS)
        S.dma("sp", onesS, cOnesS)
        S.dma("sp", sel, cSel)
        S.dma("sp", selT, cSelT)
        S.dma("sp", gT, g_pre_mix.rearrange("o (k p) -> p (o k)", p=128), allow_slow_non_contiguous=True)
        S.dma("sp", spT, s_pool.rearrange("o (k p) -> p (o k)", p=128), allow_slow_non_contiguous=True)
        S.dma("sp", bibc, b_i.partition_broadcast(128).rearrange("p a b -> p (a b)"))
        S.dma("sp", bfbc, b_f.partition_broadcast(128).rearrange("p a b -> p (a b)"))
        S.copy("dve", identB, identF)
        S.memset("dve", onesF, 1.0)
        S.memset("dve", mhalf, -0.5)
        S.memset("dve", onesB, 1.0)
        S.memset("dve", Zst.rearrange("p a b c -> p (a b c)"), 0.0)
        S.memset("dve", nZ.rearrange("p a b -> p (a b)"), 0.0)
        S.memset("dve", mrun, 0.0)
        S.memset("dve", gst.rearrange("p a b -> p (a b)"), 0.0)

        def interleave(main, side, every, lead):
            cnt = 0
            side_done = side is None
            for _ in main:
                cnt += 1
                if not side_done and cnt >= lead and (cnt - lead) % every == 0:
                    try:
                        next(side)
                    except StopIteration:
                        side_done = True
            if not side_done:
                for _ in side:
                    pass

        evc = itertools.cycle(["act", "dve"])

        def evac(out, in_, mul=None, eng=None):
            eng = eng or next(evc)
            if eng == "act":
                if mul is None:
                    S.add("act", lambda e: e.copy(out, in_), w=[out], r=[in_])
                else:
                    S.add("act", lambda e: e.mul(out, in_, mul), w=[out], r=[in_])
            else:
                if mul is None:
                    S.copy("dve", out, in_)
                else:
                    S.ts("dve", out, in_, mul, None, ALU.mult)

        def rsqrt(ss, scale, n):
            t1, t2, t3 = sc(), sc(), sc()
            S.ts("dve", t1[0:n], ss[0:n], scale, EPS, ALU.mult, ALU.add)
            S.act(t2[0:n], t1[0:n], AF.Ln)
            S.act(t3[0:n], t2[0:n], AF.Exp, scale=-0.5)
            return t3

        def rsqrt_pow(ss, scale, n):
            t1, t3 = sc(), sc()
            S.ts("dve", t1[0:n], ss[0:n], scale, EPS, ALU.mult, ALU.add)
            S.tt("pool", t3[0:n], t1[0:n], mhalf[0:n], ALU.pow)
            return t3

        def sumsq(junk, src, n):
            ss = sc()
            S.memset("dve", ss[0:n], 0.0)
            S.act(junk, src, AF.Square, accum_out=ss[0:n])
            return ss

        def wload(dst, src_rows_cols):
            kc = dst.shape[1]
            half = max(1, kc // 2)
            v = src_rows_cols.rearrange("(k p) n -> p k n", p=128)
            for k0 in range(0, kc, half):
                k1 = min(kc, k0 + half)
                S.dma("pool", dst[:, k0:k1, :], v[:, k0:k1, :])

        m_phase = B.p

        def norm_stats(r0, n, xt, xbf):
            S.dma("sp", xt[0:n, :], x[r0:r0 + n, :])
            ss = sumsq(xbf[0:n, :], xt[0:n, :], n)
            return rsqrt(ss, 1.0 / DM, n)

        def norm_apply(rstd, n, dst3, c0, xt, xbf, psT3):
            S.ts("dve", xbf[0:n, :], xt[0:n, :], rstd[0:n], None, ALU.mult)
            for kc in range(16):
                S.tr(psT3[:, kc, 0:n], xbf[0:n, kc * 128:(kc + 1) * 128], identB[0:n, 0:n])
            S.tt("dve", dst3[:, :, c0:c0 + n], psT3[:, :, 0:n],
                 gT.unsqueeze(2).broadcast_to([128, 16, n]), ALU.mult)

        def norm_gen(tiles, dst3, xs, xb):
            nx, nb = len(xs), len(xb)
            nt = len(tiles)

            def load(i):
                S.dma("sp", xs[i % nx][0:tiles[i][1], :], x[tiles[i][0]:tiles[i][0] + tiles[i][1], :])

            for i in range(min(nx - 1, nt)):
                load(i)
            ss = sumsq(xb[0][0:tiles[0][1], :], xs[0][0:tiles[0][1], :], tiles[0][1])
            r = rsqrt(ss, 1.0 / DM, tiles[0][1])
            for i, (r0, n, c0) in enumerate(tiles):
                if i + nx - 1 < nt:
                    load(i + nx - 1)
                ssn = None
                if i + 1 < nt:
                    n1 = tiles[i + 1][1]
                    ssn = sumsq(xb[(i + 1) % nb][0:n1, :], xs[(i + 1) % nx][0:n1, :], n1)
                xt, xbf, psT3 = xs[i % nx], xb[i % nb], psTs[i % 2]
                S.ts("dve", xbf[0:n, :], xt[0:n, :], r[0:n], None, ALU.mult)
                for kc in range(16):
                    S.tr(psT3[:, kc, 0:n], xbf[0:n, kc * 128:(kc + 1) * 128], identB[0:n, 0:n])
                rn = rsqrt(ssn, 1.0 / DM, tiles[i + 1][1]) if ssn is not None else None
                S.tt("dve", dst3[:, :, c0:c0 + n], psT3[:, :, 0:n],
                     gT.unsqueeze(2).broadcast_to([128, 16, n]), ALU.mult)
                r = rn
                yield

        def norm_tiles(tiles, dst3, xs, xb):
            for _ in norm_gen(tiles, dst3, xs, xb):
                pass

        hT = B.alloc([128, 16, HALO + T], BF16)
        m_hT = B.p
        hTP = B.alloc([128, 16, NPFX], BF16)
        nrm_x = [B.alloc([128, DM], F32) for _ in range(2)]
        nrm_b = [B.alloc([128, DM], BF16) for _ in range(2)]
        m_afterP = B.p
        nrm0_x = [B.alloc([128, DM], F32) for _ in range(3)]
        nrm0_b = [B.alloc([128, DM], BF16) for _ in range(2)]
        B.p = m_afterP
        psTs = [ps[:, i * 1024:(i + 1) * 1024].bitcast(BF16).rearrange("p (a b) -> p a b", a=16) for i in range(2)]
        norm_tiles([(tt * 128, 128, tt * 128) for tt in range(8)], hTP, nrm0_x, nrm0_b)

        wload(wg, w_in[:, COL_IG:COL_IG + 8])

        def gate_group(hT3, c0, slot0):
            gps = bank(7)
            G = gps[:, 0:64].rearrange("p (t g) -> p t g", g=8)
            for i in range(8):
                for kc in range(16):
                    S.mm(gps[:, i * 8:(i + 1) * 8], hT3[:, kc, c0 + i * 128: c0 + (i + 1) * 128], wg[:, kc, :],
                         kc == 0, kc == 15)
            gb = gbuf
            v3 = lambda k: gb[:, k, :].rearrange("p (t h) -> p t h", h=4)
            zf, e, sp, lf, ig, dmb, dl, bLs = (gb[:, k, :] for k in range(8))
            bc = lambda t: t.unsqueeze(1).broadcast_to([128, 8, 4])
            S.tt("dve", v3(0), G[:, :, 4:8], bc(bfbc), ALU.add)
            S.act(e, zf, AF.Exp, scale=-1.0)
            S.act(sp, e, AF.Ln, bias=1.0)
            S.ts("dve", lf, sp, -1.0, None, ALU.mult)
            S.tt("dve", v3(4), G[:, :, 0:4], bc(bibc), ALU.add)
            b_ps, bL_ps = gps[:, 64:96], gps[:, 96:128]
            S.mm(b_ps, tri, lf, True, True)
            S.mm(bL_ps, onesF, lf, True, True)
            S.tt("dve", dmb, ig, b_ps, ALU.subtract)
            gsl = lambda a: gst[:, slot0:slot0 + 8, a:a + 4]
            S.act(gsl(0), v3(5), AF.Exp)
            S.tt("dve", dl, dmb, bL_ps, ALU.add)
            S.act(gsl(4), v3(6), AF.Exp)
            S.act(gsl(8), b_ps.rearrange("p (t h) -> p t h", h=4), AF.Exp, scale=-1.0)
            S.act(gsl(12), bL_ps.rearrange("p (t h) -> p t h", h=4), AF.Exp)
            S.copy("dve", bLs, bL_ps)
            S.copy("dve", bLrow[0:1, slot0 * 4: slot0 * 4 + 32], bLs[0:1, :])
            t1 = gps[0:32, 128:256]
            S.tr(t1, dmb, identF)
            mxc = gb[0:32, 0, 0:1]
            S.add("dve", lambda e_: e_.reduce_max(mxc, t1, AX.X), w=[mxc], r=[t1])
            t2 = gps[0:1, 256:288]
            S.tr(t2, mxc, identF[0:32, 0:32])
            S.copy("dve", mxrow[0:1, slot0 * 4: slot0 * 4 + 32], t2)

        def gate_tile(hT3, c0, n, slot, mcol, sample=False):
            gps = bank(7)
            for kc in range(16):
                S.mm(gps[0:n, 0:8], hT3[:, kc, c0:c0 + n], wg[:, kc, :], kc == 0, kc == 15)
            zf, e, sp, lf, ig = sc(4), sc(4), sc(4), sc(4), sc(4)
            S.tt("dve", zf[0:n], gps[0:n, 4:8], bfbc[0:n], ALU.add)
            S.act(e[0:n], zf[0:n], AF.Exp, scale=-1.0)
            S.act(sp[0:n], e[0:n], AF.Ln, bias=1.0)
            S.ts("dve", lf[0:n], sp[0:n], -1.0, None, ALU.mult)
            S.tt("dve", ig[0:n], gps[0:n, 0:4], bibc[0:n], ALU.add)
            triM, onesM = (triS, onesS) if sample else (tri, onesF)
            b_ps, bL_ps = gps[:, 16:20], gps[:, 32:36]
            S.mm(b_ps[0:n], triM[0:n, 0:n], lf[0:n], True, True)
            S.mm(bL_ps[0:n], onesM[0:n, 0:n], lf[0:n], True, True)
            dmb = dmbS if sample else sc(4)
            bLs = bLS if sample else sc(4)
            dl = sc(4)
            S.tt("dve", dmb[0:n], ig[0:n], b_ps[0:n], ALU.subtract)
            S.act(gst[0:n, slot, 0:4], dmb[0:n], AF.Exp)
            S.tt("dve", dl[0:n], dmb[0:n], bL_ps[0:n], ALU.add)
            S.act(gst[0:n, slot, 4:8], dl[0:n], AF.Exp)
            S.act(gst[0:n, slot, 8:12], b_ps[0:n], AF.Exp, scale=-1.0)
            S.act(gst[0:n, slot, 12:16], bL_ps[0:n], AF.Exp)
            S.copy("dve", bLs[0:n], bL_ps[0:n])

        gate_group(hTP, 0, 0)

        if stop_after == "p0":
            dbg = dout("dbg", [128, 16 * NPFX // 2])
            S.dma("sp", dbg, hTP.rearrange("p a b -> p (a b)").bitcast(F32))
            dbg2 = dout("dbg2", [128, 17 * 16])
            S.dma("sp", dbg2, gst.rearrange("p a b -> p (a b)"))
            S.emit()
            return nc

        q_1c = B.q
        kP = B.hi([128, 4, 8, 256], BF16)
        vP = B.hi([128, 4, 8, 512], BF16)
        kwp = [B.hi([128, 256], BF16) for _ in range(2)] * 2
        q_P = B.q
        wkP = [B.alloc([128, 16, 256], BF16)] * 2
        wvP = [B.alloc([128, 16, 512], BF16) for _ in range(2)]
        pb = itertools.cycle(range(4))
        pbP = itertools.cycle((4, 5, 6))

        def pproj_gen():
            for h in range(4):
                wk = wkP[h % 2]
                wv = wvP[h % 2]
                wload(wk, w_in[:, COL_K + h * 256: COL_K + (h + 1) * 256])
                wload(wv, w_in[:, COL_V + h * 512: COL_V + (h + 1) * 512])
                for tt in range(8):
                    b_ = bank(next(pbP))
                    for kc in range(16):
                        S.mm(b_[:, 0:256], hTP[:, kc, tt * 128:(tt + 1) * 128], wk[:, kc, :], kc == 0, kc == 15)
                    evac(kP[:, h, tt, :], b_[:, 0:256], mul=1.0 / 16)
                    yield
                for tt in range(8):
                    b_ = bank(next(pbP))
                    for kc in range(16):
                        S.mm(b_, hTP[:, kc, tt * 128:(tt + 1) * 128], wv[:, kc, :], kc == 0, kc == 15)
                    evac(vP[:, h, tt, :], b_)
                    yield

        S.copy("dve", hT[:, :, 0:HALO], hTP[:, :, NPFX - HALO:NPFX])
        interleave(pproj_gen(), norm_gen([(NPFX + tt * 128, 128 if tt < 8 else 64, HALO + tt * 128)
                                          for tt in range(9)], hT, nrm_x, nrm_b), every=6, lead=2)
        gate_group(hT, HALO, 8)
        gate_tile(hT, HALO + 8 * 128, 64, 16, 16, sample=True)

        def state_update(h, ktile, vtile, slot, kw, n=128):
            S.ts("dve", kw[0:n], ktile, gst[0:n, slot, 4 + h:5 + h], None, ALU.mult)
            un = bank(2)[:, 2 * h:2 * h + 2]
            for dc in range(2):
                S.mm(un[:, dc:dc + 1], kw[0:n, dc * 128:(dc + 1) * 128], onesB[0:n, 0:1], True, True)
            for dc in range(2):
                U = bank(4 + h)
                S.mm(U, kw[0:n, dc * 128:(dc + 1) * 128], vtile, True, True)
                S.stt("dve", Zst[:, h, dc, :], Zst[:, h, dc, :], gst[:, slot, 12 + h:13 + h], U, ALU.mult, ALU.add)
            S.stt("dve", nZ[:, h, :], nZ[:, h, :], gst[:, slot, 12 + h:13 + h], un, ALU.mult, ALU.add)

        def prec_gen():
            for tt in range(8):
                for h in range(4):
                    state_update(h, kP[:, h, tt, :], vP[:, h, tt, :], tt, kwp[h])
                    yield

        B.p = m_hT
        bmT = B.alloc([128, 16, T], BF16)
        m_h = B.p
        for t_ in range(16):
            tmp = sc(4)
            S.tt("dve", tmp[0:1], mrun, mxrow[0:1, t_ * 4:(t_ + 1) * 4], ALU.max)
            S.tt("dve", mrun, tmp[0:1], bLrow[0:1, t_ * 4:(t_ + 1) * 4], ALU.add)
        mrep = bank(7)[:, 320:324]
        S.mm(mrep, onesF[0:1, :], mrun, True, True)
        S.act(emf, mrep, AF.Exp, scale=-1.0)
        S.dma("sp", mP.rearrange("h o -> o h"), mrun)

        if stop_after == "pP":
            dbg = dout("dbg", [128, 4096])
            S.dma("sp", dbg, Zst.rearrange("p a b c -> p (a b c)"))
            dbg2 = dout("dbg2", [128, 17 * 16])
            S.dma("sp", dbg2, gst.rearrange("p a b -> p (a b)"))
            S.emit()
            return nc

        B.p = m_h
        B.q = q_1c
        S.dma("pool", maskq.rearrange("p a b -> p (a b)"), cMq.partition_broadcast(128).rearrange("p a b -> p (a b)"))
        emR = B.alloc([128, 64], F32)
        S.dma("sp", emR, sm.partition_broadcast(128).rearrange("p a b -> p (a b)"))
        S.act(emR, emR, AF.Exp)
        mprevT = B.alloc([4, 16], F32)
        S.dma("sp", mprevT, sm.rearrange("o (s h) -> (o h) s", h=4), allow_slow_non_contiguous=True)
        wslots = [B.alloc([128, 16, 512], BF16) for _ in range(2)]
        qT = B.alloc([128, 2, T], BF16)
        kT = B.alloc([128, 2, T], BF16)
        ktok = B.alloc([128, 9, 256], BF16)
        vtok = B.alloc([128, 9, 512], BF16)
        assert B.p <= q_P, ("head-0 projection buffers overlap the prefix k/v", B.p, q_P)
        wslots.append(B.alloc([128, 16, 512], BF16))
        ghbc = B.alloc([128, 512], F32)
        Zbf = B.alloc([128, 2, 512], BF16)
        nbf = B.alloc([128, 2], BF16)
        kw = B.alloc([128, 256], BF16)
        Sp = B.alloc([128, 128], BF16)
        so = B.alloc([128, 512], F32)
        bmb = B.alloc([128, 512], BF16)
        junk = bmb
        Cs = [B.alloc([128, 2, 512], F32) for _ in range(3)]
        Cb = [B.alloc([128, 2, 512], BF16) for _ in range(1)]
        Cn = [B.alloc([128, 2, 512], F32) for _ in range(2)]
        qm = B.alloc([128, 2, 16, 64], BF16)
        nsT = B.alloc([128, 2, 16], F32)
        nsb = B.alloc([128, 2, 16], BF16)
        wkm = B.alloc([64, 16], F32)
        wkmb = B.alloc([64, 16], BF16)
        kwm = [B.alloc([64, 256], BF16)] * 2
        decR = B.alloc([128, 64], F32)
        nsq = B.alloc([16, 256], F32)
        small = B.alloc([64, 64], F32)
        m_1c_end = B.p

        mnew_sm = B.alloc([16, 4], F32)
        dec_sm = B.alloc([16, 4], F32)
        wkS = B.alloc([64, 4], F32)
        BD = B.alloc([4, 4, 16], F32)

        def sample_scalars():
            tps = bank(7)
            S.tr(tps[0:4, 128:192], dmbS[0:64, 0:4], identF[0:64, 0:64])
            mxT = small[0:4, 0:16]
            S.add("dve", lambda e_: e_.reduce_max(mxT, tps[0:4, 128:192].rearrange("p (s j) -> p s j", j=4), AX.X),
                  w=[mxT], r=[tps[0:4, 128:192]])
            S.tr(tps[0:4, 256:320], bLS[0:64, 0:4], identF[0:64, 0:64])
            bLT = small[0:4, 16:32]
            S.copy("dve", bLT, tps[0:4, 256:320].rearrange("p (s j) -> p s j", j=4)[:, :, 0])
            mnewT = small[0:4, 32:48]
            S.tt("dve", mnewT, mprevT, mxT, ALU.max)
            S.tt("dve", mnewT, mnewT, bLT, ALU.add)
            S.dma("sp", mS.rearrange("s h -> h s"), mnewT, allow_slow_non_contiguous=True)
            decT = small[0:4, 48:64]
            S.tt("dve", decT, bLT, mprevT, ALU.add)
            S.tt("dve", decT, decT, mnewT, ALU.subtract)
            S.act(decT, decT, AF.Exp)
            S.tr(tps[0:16, 384:388], mnewT, identF[0:4, 0:4])
            S.copy("dve", mnew_sm, tps[0:16, 384:388])
            S.tr(tps[0:16, 400:404], decT, identF[0:4, 0:4])
            S.copy("dve", dec_sm, tps[0:16, 400:404])
            mtok = tps[0:64, 416:420]
            S.mm(mtok, selT, mnew_sm, True, True)
            S.tt("dve", wkS, dmbS, bLS, ALU.add)
            S.tt("dve", wkS, wkS, mtok, ALU.subtract)
            S.act(wkS, wkS, AF.Exp)
            S.tt("dve", BD, decT.unsqueeze(1).broadcast_to([4, 4, 16]),
                 identF[0:4, 0:4].unsqueeze(2).broadcast_to([4, 4, 16]), ALU.mult)
            drep = bank(6)[:, 0:64]
            S.mm(drep, onesF[0:4, :], BD.rearrange("p a b -> p (a b)"), True, True)
            S.copy("dve", decR, drep)

        ghb = [ghbc, ghbc]
        so2 = [so, B.alloc([128, 512], F32)]
        Sp2 = [Sp, B.alloc([128, 128], BF16)]
        kw2 = [kw, B.alloc([128, 256], BF16)]
        bmb2 = [bmb, B.alloc([128, 512], BF16)]
        bmb3 = bmb2 + [B.alloc([128, 512], BF16)]
        qTs = B.alloc([128, 2, 64], BF16)
        ktoks = B.alloc([64, 256], BF16)
        vtoks = B.alloc([64, 512], BF16)
        sgo_s = B.alloc([64, 512], F32)
        Sp_s = B.alloc([64, 64], BF16)

        pb2 = itertools.cycle(range(2))

        def qkv_loads(h):
            wqk, wv = wslots[0], wslots[1]
            wload(wqk[:, :, 0:256], w_in[:, COL_Q + h * 256: COL_Q + (h + 1) * 256])
            wload(wqk[:, :, 256:512], w_in[:, COL_K + h * 256: COL_K + (h + 1) * 256])
            wload(wv, w_in[:, COL_V + h * 512: COL_V + (h + 1) * 512])

        def proj_gen(h):
            wqk, wv, wo = wslots[0], wslots[1], wslots[2]
            if h == 0:
                qkv_loads(0)
            if h > 0:
                wload(wo, w_in[:, COL_O + h * 512: COL_O + (h + 1) * 512])
            for which, dstT, mul in ((0, qT, None), (1, kT, 1.0 / 16)):
                for dc in range(2):
                    for (t0, n) in ((0, 512), (512, 512), (1024, 64)):
                        b_ = bank(next(pb2))
                        for kc in range(16):
                            S.mm(b_[:, 0:n], wqk[:, kc, which * 256 + dc * 128: which * 256 + (dc + 1) * 128],
                                 hT[:, kc, HALO + t0: HALO + t0 + n], kc == 0, kc == 15)
                        evac(dstT[:, dc, t0:t0 + n], b_[:, 0:n], mul=mul)
                        yield
            for tt in range(9):
                n = 128 if tt < 8 else 64
                c0 = HALO + tt * 128
                b_ = bank(next(pb2))
                for kc in range(16):
                    S.mm(b_[0:n, 0:256], hT[:, kc, c0:c0 + n], wqk[:, kc, 256:512], kc == 0, kc == 15)
                evac(ktok[0:n, tt, :], b_[0:n, 0:256], mul=1.0 / 16)
                yield
                b_ = bank(next(pb2))
                for kc in range(16):
                    S.mm(b_[0:n, :], hT[:, kc, c0:c0 + n], wv[:, kc, :], kc == 0, kc == 15)
                evac(vtok[0:n, tt, :], b_[0:n, :])
                yield

        def sample_prep(h):
            wo = wslots[2]
            c0 = HALO + NM
            ops = bank(6)
            for kc in range(16):
                S.mm(ops[0:64, :], hT[:, kc, c0:c0 + 64], wo[:, kc, :], kc == 0, kc == 15)
            S.act(sgo_s, ops[0:64, :], AF.Sigmoid)
            STp = bank(0)[0:64, 0:64]
            for dc in range(2):
                S.mm(STp, kT[:, dc, NM:NM + 64], qT[:, dc, NM:NM + 64], dc == 0, dc == 1)
            S.stt("dve", Sp_s, STp, gst[0:64, 16, h:h + 1], triS, ALU.mult, ALU.mult)
            S.copy("dve", qTs, qT[:, :, NM:NM + 64])
            S.copy("dve", ktoks, ktok[0:64, 8, :])
            S.copy("dve", vtoks, vtok[0:64, 8, :])
            for dc in range(2):
                S.tt("dve", qm[:, dc], qTs[:, dc, :].unsqueeze(1).broadcast_to([128, 16, 64]), maskq, ALU.mult)
            S.dma("sp", nsq, sn[:, h, :])
            for dc in range(2):
                tpn = bank(7)[:, 64 + dc * 16: 80 + dc * 16]
                S.tr(tpn, nsq[0:16, dc * 128:(dc + 1) * 128], identF[0:16, 0:16])
                S.copy("dve", nsT[:, dc, :], tpn)
            emh = emR.rearrange("p (s hh) -> p s hh", hh=4)[:, :, h]
            S.tt("dve", nsb, nsT, emh.unsqueeze(1).broadcast_to([128, 2, 16]), ALU.mult)

        def sample_gen(h):
            Np = bank(2)[0:64, :]
            Dp = bank(3)[0:64, 0:1]

            def kwm_for(s_):
                S.ts("dve", wkm[:, s_:s_ + 1], sel[:, s_:s_ + 1], wkS[:, h:h + 1], None, ALU.mult)
                S.ts("dve", kwm[0], ktoks, wkm[:, s_:s_ + 1], None, ALU.mult)

            kwm_for(0)
            for s_ in range(NSEQ):
                cs = Cs[s_ % 3]
                cb = Cb[0]
                S.dma("sp", cs, sC[s_, h].rearrange("(dc p) e -> p dc e", p=128))
                S.add("act", lambda e_, cb=cb, cs=cs, s_=s_: e_.mul(
                    cb.rearrange("p a b -> p (a b)"), cs.rearrange("p a b -> p (a b)"),
                    emR[:, s_ * 4 + h: s_ * 4 + h + 1]),
                    w=[cb], r=[cs, emR[:, s_ * 4 + h: s_ * 4 + h + 1]])
                yield
                for dc in range(2):
                    S.mm(Np, qm[:, dc, s_, :], cb[:, dc, :], (s_ == 0 and dc == 0), False)
                for dc in range(2):
                    S.mm(Dp, qm[:, dc, s_, :], nsb[:, dc, s_:s_ + 1], (s_ == 0 and dc == 0), False)
                kwm_ = kwm[0]
                cn = Cn[s_ % 2]
                for dc in range(2):
                    U = bank(4 + dc)
                    S.mm(U, kwm_[:, dc * 128:(dc + 1) * 128], vtoks, True, True)
                    S.stt("dve", cn[:, dc, :], cs[:, dc, :], decR[:, h * 16 + s_: h * 16 + s_ + 1], U,
                          ALU.mult, ALU.add)
                S.dma("pool", CS[s_, h].rearrange("(dc p) e -> p dc e", p=128), cn)
                if s_ + 1 < NSEQ:
                    kwm_for(s_ + 1)
                yield

        def finish_tile(h, n, Np, Dp, so_, ebcol, bmb_):
            d2, d3, d4 = sc(), sc(), sc()
            S.act(d2[0:n], Dp, AF.Abs)
            S.tt("dve", d3[0:n], d2[0:n], ebcol, ALU.max)
            S.add("dve", lambda e_, d3=d3, d4=d4, n=n: e_.reciprocal(d4[0:n], d3[0:n]), w=[d4[0:n]], r=[d3[0:n]])
            S.add("act", lambda e_, d4=d4, n=n: e_.mul(Np, Np, d4[0:n]), w=[Np], r=[Np, d4[0:n]])
            ss = sumsq(bmb_[0:n], Np, n)
            rstd = rsqrt_pow(ss, 1.0 / 512, n)
            S.stt("dve", Np, Np, rstd[0:n], ghb[h % 2][0:n, :], ALU.mult, ALU.mult)
            S.tt("dve", bmb_[0:n], Np, so_, ALU.mult)

        def bm_transpose(h, n, t0, bmb_):
            bps = bank(7).bitcast(BF16).rearrange("p (a b) -> p a b", b=128)
            for e4 in range(4):
                S.tr(bps[:, e4, 0:n], bmb_[0:n, e4 * 128:(e4 + 1) * 128], identB[0:n, 0:n])
            evac(bmT[:, h * 4:(h + 1) * 4, t0:t0 + n], bps[:, 0:4, 0:n], eng="act")

        def sample_fin(h):
            Np = bank(2)[0:64, :]
            Dp = bank(3)[0:64, 0:1]
            S.mm(Np, Sp_s, vtoks, False, True)
            S.mm(Dp, Sp_s, onesB[0:64, 0:1], False, True)
            S.copy("dve", wkmb, wkm)
            npn = bank(6)[0:16, 0:256]
            S.mm(npn, wkmb, ktoks, True, True)
            S.stt("dve", nsq, nsq, dec_sm[:, h:h + 1], npn, ALU.mult, ALU.add)
            S.dma("sp", nS[:, h, :], nsq)
            finish_tile(h, 64, Np, Dp, sgo_s, gst[0:64, 16, 8 + h:9 + h], bmb2[0])
            bm_transpose(h, 64, NM, bmb2[0])

        def prompt_loop(h):
            wo = wslots[2]
            S.dma("sp", ghbc, g_head[:, h * 512:(h + 1) * 512].partition_broadcast(128).rearrange("p a b -> p (a b)"))
            S.copy("act", Zbf.rearrange("p a b -> p (a b)"), Zst[:, h].rearrange("p a b -> p (a b)"))
            S.copy("act", nbf, nZ[:, h, :])

            def stA(tt):
                c0 = HALO + tt * 128
                t0 = tt * 128
                slot = 8 + tt
                ops = bank(6)
                for kc in range(16):
                    S.mm(ops, hT[:, kc, c0:c0 + 128], wo[:, kc, :], kc == 0, kc == 15)
                S.act(so2[tt % 2], ops, AF.Sigmoid)
                STp = bank(0)[:, 0:128]
                for dc in range(2):
                    S.mm(STp, kT[:, dc, t0:t0 + 128], qT[:, dc, t0:t0 + 128], dc == 0, dc == 1)
                S.stt("dve", Sp2[tt % 2], STp, gst[:, slot, h:h + 1], tri, ALU.mult, ALU.mult)
                S.ts("dve", kw2[tt % 2], ktok[:, tt, :], gst[:, slot, 4 + h:5 + h], None, ALU.mult)

            def stB(tt):
                t0 = tt * 128
                slot = 8 + tt
                Np = bank(2 + tt % 2)
                Dp = bank(1)[:, 0:1]
                for dc in range(2):
                    S.mm(Np, qT[:, dc, t0:t0 + 128], Zbf[:, dc, :], dc == 0, False)
                S.mm(Np, Sp2[tt % 2], vtok[:, tt, :], False, True)
                for dc in range(2):
                    S.mm(Dp, qT[:, dc, t0:t0 + 128], nbf[:, dc:dc + 1], dc == 0, False)
                S.mm(Dp, Sp2[tt % 2], onesB[:, 0:1], False, True)
                kw_ = kw2[tt % 2]
                un = bank(1)[:, 8:10]
                for dc in range(2):
                    S.mm(un[:, dc:dc + 1], kw_[:, dc * 128:(dc + 1) * 128], onesB[:, 0:1], True, True)
                for dc in range(2):
                    U = bank(4 + dc)
                    S.mm(U, kw_[:, dc * 128:(dc + 1) * 128], vtok[:, tt, :], True, True)
                    S.stt("dve", Zst[:, h, dc, :], Zst[:, h, dc, :], gst[:, slot, 12 + h:13 + h], U, ALU.mult, ALU.add)
                S.stt("dve", nZ[:, h, :], nZ[:, h, :], gst[:, slot, 12 + h:13 + h], un, ALU.mult, ALU.add)
                if tt < 7:
                    S.copy("act", Zbf.rearrange("p a b -> p (a b)"), Zst[:, h].rearrange("p a b -> p (a b)"))
                    S.copy("act", nbf, nZ[:, h, :])
                else:
                    co = Cn[0]
                    S.ts("dve", co.rearrange("p a b -> p (a b)"), Zst[:, h].rearrange("p a b -> p (a b)"),
                         emf[:, h:h + 1], None, ALU.mult)
                    S.dma("sp", CP[h].rearrange("(dc p) e -> p dc e", p=128), co)
                    no = sc(2)
                    S.ts("dve", no, nZ[:, h, :], emf[:, h:h + 1], None, ALU.mult)
                    S.dma("sp", nP[h:h + 1, :].rearrange("o (dc p) -> p (o dc)", p=128), no,
                          allow_slow_non_contiguous=True)

            def stC1(tt):
                slot = 8 + tt
                finish_tile(h, 128, bank(2 + tt % 2), bank(1)[:, 0:1], so2[tt % 2],
                            gst[:, slot, 8 + h:9 + h], bmb3[tt % 3])

            stA(0)
            for tt in range(8):
                if tt + 1 < 8:
                    stA(tt + 1)
                if tt >= 2:
                    bm_transpose(h, 128, (tt - 2) * 128, bmb3[(tt - 2) % 3])
                stB(tt)
                stC1(tt)
            bm_transpose(h, 128, 6 * 128, bmb3[6 % 3])
            bm_transpose(h, 128, 7 * 128, bmb3[7 % 3])

        for h in range(4):
            if h == 0:
                interleave(proj_gen(0), prec_gen(), every=1, lead=1)
                wload(wslots[2], w_in[:, COL_O:COL_O + 512])
                sample_scalars()
            else:
                interleave(proj_gen(h), sample_gen(h - 1), every=1, lead=2)
            if h > 0:
                sample_fin(h - 1)
            if h + 1 < 4:
                qkv_loads(h + 1)
            sample_prep(h)
            prompt_loop(h)
        for _ in sample_gen(3):
            pass
        sample_fin(3)

        if stop_after == "p1c":
            dbg = dout("dbg", [128, 16 * T // 2])
            S.dma("sp", dbg, bmT.rearrange("p a b -> p (a b)").bitcast(F32))
            S.emit()
            return nc

        B.p = m_h
        aT = B.alloc([128, 8, T], BF16)
        m_a = B.p
        uT = B.alloc([128, 8, HALO + T], F32)
        zT = B.alloc([128, 8, T], BF16)
        B.q = TOP
        wpg = B.alloc([128, 8, 256], BF16)
        posb = B.alloc([128, T], F32)
        ext = B.alloc([128, 8, 16, 20], F32)
        exa = B.alloc([128, 16, 20], F32)
        exb = B.alloc([128, 16, 20], F32)
        wslots = [B.alloc([128, 16, 256], BF16) for _ in range(2)]
        icnts = [B.alloc([128, NM], F32) for _ in range(2)]
        sa = B.alloc([128, HALO + NM], F32)
        sb = B.alloc([128, HALO + NM], F32)
        pso2 = sa[:, 0:1024]
        pso = sb[:, 0:1024]
        bufTc = B.alloc([128, 8, 240], F32)
        S.dma("sp", posb, pos.partition_broadcast(128).rearrange("p a b -> p (a b)"))
        S.dma("pool", wpg.rearrange("p (g k) n -> p g k n", k=2),
              w_pool.rearrange("g (k p) n -> p g k n", p=128))
        spf = spool.rearrange("s r c -> (s r) c")
        for blk, (r0, rn) in enumerate(((0, 128), (128, 112))):
            S.dma("sp", pso[0:rn, :], spf[r0:r0 + rn, :])
            for c in range(8):
                tp = bank(2 + (c % 2))
                S.tr(tp[:, 0:rn], pso[0:rn, c * 128:(c + 1) * 128], identF[0:rn, 0:rn])
                evac(bufTc[:, c, r0:r0 + rn], tp[:, 0:rn], eng="act")
        S.memset("dve", ext.rearrange("p a b c -> p (a b c)"), 0.0)

        def pool_chunk(c):
            g = c // 2
            wdw = (2, 4, 8, 16)[g]
            icnt = icnts[g % 2]
            if c % 2 == 0:
                S.ts("dve", icnt, posb[:, 0:NM], 1.0, float(wdw), ALU.add, ALU.min)
                S.add("dve", lambda e_, ic=icnt: e_.reciprocal(ic, ic), w=[icnt], r=[icnt])
            S.copy("dve", ext[:, c, :, 0:15], bufTc[:, c, :].rearrange("p (s r) -> p s r", r=15))
            S.copy("dve", ext[:, c, :, 15:19],
                   uT[:, c, HALO + NM: HALO + T].rearrange("p (s j) -> p s j", j=4))
            cur = uT[:, c, 0:HALO + NM]
            step = 1
            bufs = [sa, sb]
            bi = 0
            while step < wdw:
                nxt = bufs[bi]
                bi ^= 1
                S.tt("dve", nxt[:, step:], cur[:, step:], cur[:, 0:HALO + NM - step], ALU.add)
                cur = nxt
                step *= 2
            mean = bufs[bi]
            S.tt("dve", mean[:, 0:NM], cur[:, HALO:], icnt, ALU.mult)
            S.tt("dve", zT[:, c, 0:NM], mean[:, 0:NM], uT[:, c, HALO:HALO + NM], ALU.subtract)
            cur = ext[:, c]
            step = 1
            bufs = [exa, exb]
            bi = 0
            while step < wdw:
                nxt = bufs[bi]
                bi ^= 1
                S.tt("dve", nxt[:, :, step:19], cur[:, :, step:19], cur[:, :, 0:19 - step], ALU.add)
                cur = nxt
                step *= 2
            S.stt("dve", zT[:, c, NM:T].rearrange("p (s j) -> p s j", j=4), cur[:, :, 15:19], 1.0 / wdw,
                  ext[:, c, :, 15:19], ALU.mult, ALU.subtract)

        for half in range(4):
            wu = wslots[half % 2]
            wload(wu, w_in[:, COL_U + half * 256: COL_U + (half + 1) * 256])
            for c4 in range(2):
                c = half * 2 + c4
                for (t0, n) in ((0, 512), (512, 512), (1024, HALO + T - 1024)):
                    b_ = bank(next(pb2))
                    for kc in range(16):
                        S.mm(b_[:, 0:n], wu[:, kc, c4 * 128:(c4 + 1) * 128], hT[:, kc, t0:t0 + n], kc == 0, kc == 15)
                    evac(uT[:, c, t0:t0 + n], b_[:, 0:n], eng="act")
                pool_chunk(c)
        for c in range(8):
            g = c // 2
            for (t0, n) in ((0, 512), (512, 512), (1024, 64)):
                b_ = bank(next(pb))
                for k in range(2):
                    S.mm(b_[:, 0:n], wpg[:, 2 * g + k, (c % 2) * 128:(c % 2) * 128 + 128], zT[:, 2 * g + k, t0:t0 + n],
                         k == 0, k == 1)
                S.add("act", lambda e_, c=c, t0=t0, n=n, b_=b_: e_.mul(aT[:, c, t0:t0 + n], b_[:, 0:n], spT[:, c:c + 1]),
                      w=[aT[:, c, t0:t0 + n]], r=[b_[:, 0:n], spT[:, c:c + 1]])
        for c in range(8):
            tp = bank(next(pb))
            S.tr(tp[0:15, 0:128], uT[:, c, HALO + NM - 15: HALO + NM], identF)
            evac(pso[0:15, c * 128:(c + 1) * 128], tp[0:15, 0:128])
        S.dma("sp", poolP, pso[0:15, :])
        extr = bufTc.rearrange("p c (s r) -> p c s r", r=15)
        for c in range(8):
            S.copy("pool", extr[:, c], ext[:, c, :, 4:19])
        for blk, (r0, rn) in enumerate(((0, 128), (128, 112))):
            dst = pso if blk == 0 else pso2
            for c in range(8):
                tp = bank(next(pb))
                S.tr(tp[0:rn, 0:128], extr[:, c].rearrange("p s r -> p (s r)")[:, r0:r0 + rn], identF)
                evac(dst[0:rn, c * 128:(c + 1) * 128], tp[0:rn, 0:128])
            S.dma("sp", poolS[r0:r0 + rn, :], dst[0:rn, :])

        if stop_after == "p1b":
            dbg = dout("dbg", [128, 8 * T // 2])
            S.dma("sp", dbg, aT.rearrange("p a b -> p (a b)").bitcast(F32))
            S.emit()
            return nc

        B.p = m_a
        B.q = TOP
        mgT = B.hi([128, 16, T], BF16)
        wo_ = [B.hi([128, 16, 512], BF16) for _ in range(2)]
        wsl = [[B.alloc([128, 16, 128], BF16), B.alloc([128, 16, 128], BF16), B.alloc([128, 8, 128], BF16),
                B.alloc([128, 16, 128], BF16)] for _ in range(2)]
        sga = [B.alloc([128, 512], F32) for _ in range(2)]
        sgb = [B.alloc([128, 512], F32) for _ in range(2)]
        t1 = [B.alloc([128, 512], F32) for _ in range(2)]
        it = 0
        for c in range(16):
            wga, wgb, wpa_, wpb_ = wsl[c % 2]
            wload(wga, w_in[:, COL_GA + c * 128: COL_GA + (c + 1) * 128])
            wload(wgb, w_in[:, COL_GB + c * 128: COL_GB + (c + 1) * 128])
            wload(wpa_, w_pa[:, c * 128:(c + 1) * 128])
            wload(wpb_, w_pb[:, c * 128:(c + 1) * 128])
            if c == 1:
                for nn in range(2):
                    wload(wo_[nn], w_out[:, nn * 512:(nn + 1) * 512])
            for (t0, n) in ((0, 512), (512, 512), (1024, 64)):
                bb = (it % 2) * 4
                it += 1
                pga, pgb, pya, pyb = bank(bb), bank(bb + 1), bank(bb + 2), bank(bb + 3)
                for kc in range(16):
                    S.mm(pga[:, 0:n], wga[:, kc, :], hT[:, kc, HALO + t0:HALO + t0 + n], kc == 0, kc == 15)
                for kc in range(16):
                    S.mm(pgb[:, 0:n], wgb[:, kc, :], hT[:, kc, HALO + t0:HALO + t0 + n], kc == 0, kc == 15)
                for kc in range(8):
                    S.mm(pya[:, 0:n], wpa_[:, kc, :], aT[:, kc, t0:t0 + n], kc == 0, kc == 7)
                for kc in range(16):
                    S.mm(pyb[:, 0:n], wpb_[:, kc, :], bmT[:, kc, t0:t0 + n], kc == 0, kc == 15)
                j = it % 2
                S.act(sga[j][:, 0:n], pga[:, 0:n], AF.Sigmoid)
                S.act(sgb[j][:, 0:n], pgb[:, 0:n], AF.Sigmoid)
                S.tt("dve", sga[j][:, 0:n], sga[j][:, 0:n], pya[:, 0:n], ALU.mult)
                S.tt("dve", t1[j][:, 0:n], sgb[j][:, 0:n], pyb[:, 0:n], ALU.mult)
                S.tt("dve", mgT[:, c, t0:t0 + n], sga[j][:, 0:n], t1[j][:, 0:n], ALU.add)

        B.p = m_phase
        h2T = B.alloc([128, 16, T], BF16)
        m_h2 = B.p
        wo_ = wo_ + [B.alloc([128, 16, 512], BF16) for _ in range(2)]
        for nn in range(2, 4):
            wload(wo_[nn], w_out[:, nn * 512:(nn + 1) * 512])
        g1 = B.alloc([128, DM], F32)
        g2 = B.alloc([128, DM], F32)
        S.dma("sp", g1, g_post_mix.partition_broadcast(128).rearrange("p a b -> p (a b)"))
        S.dma("sp", g2, g_pre_ffn.partition_broadcast(128).rearrange("p a b -> p (a b)"))
        xt2 = [B.alloc([128, DM], F32) for _ in range(2)]
        tb = B.alloc([128, DM], F32)
        hb = [B.alloc([128, DM], BF16) for _ in range(2)]
        jk = B.alloc([128, 512], BF16)
        def mm2(tt):
            n = 128 if tt < 8 else 64
            t0 = tt * 128
            S.dma("sp", xt2[tt % 2][0:n], x[NPFX + t0: NPFX + t0 + n, :])
            for nn in range(4):
                b_ = bank((tt % 2) * 4 + nn)
                for kc in range(16):
                    S.mm(b_[0:n, :], mgT[:, kc, t0:t0 + n], wo_[nn][:, kc, :], kc == 0, kc == 15)

        def post2(tt):
            n = 128 if tt < 8 else 64
            t0 = tt * 128
            xt = xt2[tt % 2]
            bb = (tt % 2) * 4
            sst = sc(4)
            S.memset("dve", sst[0:n], 0.0)
            for nn in range(4):
                S.act(jk[0:n], bank(bb + nn)[0:n, :], AF.Square, accum_out=sst[0:n, nn:nn + 1])
            ss = sc()
            S.add("dve", lambda e_, ss=ss, sst=sst, n=n: e_.reduce_sum(ss[0:n], sst[0:n], AX.X), w=[ss[0:n]], r=[sst[0:n]])
            rstd = rsqrt(ss, 1.0 / DM, n)
            for nn in range(4):
                S.stt("dve", tb[0:n, nn * 512:(nn + 1) * 512], bank(bb + nn)[0:n, :], rstd[0:n],
                      g1[0:n, nn * 512:(nn + 1) * 512], ALU.mult, ALU.mult)
            S.tt("dve", xt[0:n], xt[0:n], tb[0:n], ALU.add)
            S.dma("pool", x1s[t0:t0 + n, :], xt[0:n])
            ss2 = sumsq(hb[tt % 2][0:n], xt[0:n], n)
            rstd2 = rsqrt(ss2, 1.0 / DM, n)
            S.stt("dve", hb[tt % 2][0:n], xt[0:n], rstd2[0:n], g2[0:n], ALU.mult, ALU.mult)
            pT3 = ps[:, bb * 512: bb * 512 + 1024].bitcast(BF16).rearrange("p (a b) -> p a b", a=16)
            for kc in range(16):
                S.tr(pT3[:, kc, 0:n], hb[tt % 2][0:n, kc * 128:(kc + 1) * 128], identB[0:n, 0:n])
            evac(h2T[:, :, t0:t0 + n], pT3[:, :, 0:n], eng="act")

        mm2(0)
        for tt in range(9):
            if tt + 1 < 9:
                mm2(tt + 1)
            post2(tt)

        B.p = m_h2
        B.q = TOP
        actT = B.hi([128, 44, T], BF16)
        wgu = [[B.alloc([128, 16, 128], BF16), B.alloc([128, 16, 128], BF16)] for _ in range(3)]
        sil = [B.alloc([128, 512], F32) for _ in range(2)]
        it = 0
        for fc in range(44):
            wg_, wu_ = wgu[fc % 3]
            wload(wg_, w_gate[:, fc * 128:(fc + 1) * 128])
            wload(wu_, w_up[:, fc * 128:(fc + 1) * 128])
            for (t0, n) in ((0, 512), (512, 512), (1024, 64)):
                bb = (it % 4) * 2
                it += 1
                pg, pu = bank(bb), bank(bb + 1)
                for kc in range(16):
                    S.mm(pg[:, 0:n], wg_[:, kc, :], h2T[:, kc, t0:t0 + n], kc == 0, kc == 15)
                for kc in range(16):
                    S.mm(pu[:, 0:n], wu_[:, kc, :], h2T[:, kc, t0:t0 + n], kc == 0, kc == 15)
                sl = sil[it % 2]
                S.act(sl[:, 0:n], pg[:, 0:n], AF.Silu)
                S.tt("dve", actT[:, fc, t0:t0 + n], sl[:, 0:n], pu[:, 0:n], ALU.mult)

        B.p = m_phase
        wd = [B.alloc([128, 44, 256], BF16) for _ in range(2)]
        fe = [B.alloc([128, 256], F32) for _ in range(4)]
        jkf = B.alloc([128, 256], BF16)
        ssf = B.alloc([128, 9, 8], F32)
        rstd5 = B.alloc([128, 16], F32)
        S.memset("dve", ssf.rearrange("p a b -> p (a b)"), 0.0)
        m_p4 = B.p
        wpg_ = [B.alloc([128, 16, 512], BF16) for _ in range(3)]
        m_p4e = B.p
        it = 0
        for nn in range(8):
            w_ = wd[nn % 2]
            v = w_down[:, nn * 256:(nn + 1) * 256].rearrange("(k p) n -> p k n", p=128)
            for k0 in range(0, 44, 11):
                S.dma("pool", w_[:, k0:k0 + 11, :], v[:, k0:k0 + 11, :])
            if nn == 1:
                for j in range(3):
                    wload(wpg_[j], w_ple_gate[:, j * 512:(j + 1) * 512])
            for tt in range(9):
                n = 128 if tt < 8 else 64
                t0 = tt * 128
                b_ = bank(it % 8)
                f_ = fe[it % 4]
                it += 1
                for kc in range(44):
                    S.mm(b_[0:n, 0:256], actT[:, kc, t0:t0 + n], w_[:, kc, :], kc == 0, kc == 43)
                S.act(jkf[0:n], b_[0:n, 0:256], AF.Square, accum_out=ssf[0:n, tt, nn:nn + 1])
                evac(f_[0:n], b_[0:n, 0:256], eng="dve")
                S.dma("sp", fsc[t0:t0 + n, nn * 256:(nn + 1) * 256], f_[0:n])

        B.p = m_phase
        B.q = TOP
        ft = [B.alloc([128, DM], F32) for _ in range(3)]
        x1t = [B.alloc([128, DM], F32) for _ in range(3)]
        assert B.p <= m_p4
        B.p = m_p4e
        wpl = B.alloc([128, 2, DM], BF16)
        wload(wpl, w_ple)
        wpg_ = wpg_ + [B.alloc([128, 16, 512], BF16)]
        wload(wpg_[3], w_ple_gate[:, 3 * 512:4 * 512])
        g3 = B.alloc([128, DM], F32)
        S.dma("sp", g3, g_post_ffn.partition_broadcast(128).rearrange("p a b -> p (a b)"))
        x2b = [B.alloc([128, DM], BF16) for _ in range(2)]
        x2T = [B.alloc([128, 16, 128], BF16) for _ in range(2)]
        pt = [B.alloc([128, 256], F32) for _ in range(2)]
        ptb = [B.alloc([128, 256], BF16) for _ in range(2)]
        pT = [B.alloc([128, 2, 128], BF16) for _ in range(2)]
        sg = [B.alloc([128, 1024], F32) for _ in range(2)]
        yt = [B.alloc([128, DM], F32) for _ in range(2)]
        jk5 = B.alloc([128, DM], BF16)

        ss5 = sc(9)
        S.add("dve", lambda e_: e_.reduce_sum(ss5, ssf, AX.X), w=[ss5], r=[ssf])
        t5a, t5b = sc(9), sc(9)
        S.ts("dve", t5a, ss5, 1.0 / DM, EPS, ALU.mult, ALU.add)
        S.act(t5b, t5a, AF.Ln)
        S.act(rstd5[:, 0:9], t5b, AF.Exp, scale=-0.5)

        def pre5(tt):
            n = 128 if tt < 8 else 64
            t0 = tt * 128
            f_, x1_ = ft[tt % 3], x1t[tt % 3]
            S.dma("sp", f_[0:n], fsc[t0:t0 + n, :])
            S.dma("sp", x1_[0:n], x1s[t0:t0 + n, :])
            S.dma("sp", pt[tt % 2][0:n], p_in[t0:t0 + n, :])
            S.stt("dve", f_[0:n], f_[0:n], rstd5[0:n, tt:tt + 1], g3[0:n], ALU.mult, ALU.mult)
            S.tt("dve", x1_[0:n], x1_[0:n], f_[0:n], ALU.add)
            S.copy("pool", x2b[tt % 2][0:n], x1_[0:n])
            S.copy("pool", ptb[tt % 2][0:n], pt[tt % 2][0:n])

        def preT5(tt):
            n = 128 if tt < 8 else 64
            pT3 = ps[:, 0:1024].bitcast(BF16).rearrange("p (a b) -> p a b", a=16)
            for kc in range(16):
                S.tr(pT3[:, kc, 0:n], x2b[tt % 2][0:n, kc * 128:(kc + 1) * 128], identB[0:n, 0:n])
            evac(x2T[tt % 2][:, :, 0:n], pT3[:, :, 0:n], eng="act")
            pP3 = ps[:, 1024:1536].bitcast(BF16).rearrange("p (a b) -> p a b", b=128)
            for k in range(2):
                S.tr(pP3[:, k, 0:n], ptb[tt % 2][0:n, k * 128:(k + 1) * 128], identB[0:n, 0:n])
            evac(pT[tt % 2][:, :, 0:n], pP3[:, 0:2, 0:n], eng="dve")

        def main5(tt, between=None):
            n = 128 if tt < 8 else 64
            t0 = tt * 128
            x1_ = x1t[tt % 3]
            y_ = yt[tt % 2]
            for hh in range(2):
                sg_ = sg[hh]
                for q in range(2):
                    nn = hh * 2 + q
                    bg, bp = bank(4 + q), bank(6 + q)
                    for kc in range(16):
                        S.mm(bg[0:n, :], x2T[tt % 2][:, kc, 0:n], wpg_[nn][:, kc, :], kc == 0, kc == 15)
                    for k in range(2):
                        S.mm(bp[0:n, :], pT[tt % 2][:, k, 0:n], wpl[:, k, nn * 512:(nn + 1) * 512], k == 0, k == 1)
                if hh == 1 and between is not None:
                    between()
                for q in range(2):
                    bg, bp = bank(4 + q), bank(6 + q)
                    S.act(sg_[0:n, q * 512:(q + 1) * 512], bg[0:n, :], AF.Sigmoid)
                    S.tt("dve", sg_[0:n, q * 512:(q + 1) * 512], sg_[0:n, q * 512:(q + 1) * 512], bp[0:n, :], ALU.mult)
                S.tt("dve", y_[0:n, hh * 1024:(hh + 1) * 1024], sg_[0:n], x1_[0:n, hh * 1024:(hh + 1) * 1024], ALU.add)
            S.dma("pool", y_out[t0:t0 + n, :], y_[0:n])

        pre5(0)
        preT5(0)
        for tt in range(9):
            if tt + 1 < 9:
                pre5(tt + 1)
                main5(tt, between=lambda tt=tt: preT5(tt + 1))
            else:
                main5(tt)

        S.emit()
    return nc


_NC_CACHE = {}


def _consts():
    i = np.arange(128)
    tri = (i[:, None] <= i[None, :]).astype(np.float32)
    j = np.arange(64)
    same = (j[:, None] // 4) == (j[None, :] // 4)
    triS = (same & (j[:, None] <= j[None, :])).astype(np.float32)
    onesS = same.astype(np.float32)
    sel = ((j[:, None] // 4) == np.arange(16)[None, :]).astype(np.float32)
    return {
        "cI": np.eye(128, dtype=np.float32), "cTri": tri, "cTriS": triS, "cOnesS": onesS,
        "cSel": sel, "cSelT": np.ascontiguousarray(sel.T),
        "cMq": np.ascontiguousarray(sel.T).reshape(1, 1024),
    }


def kernel(x_prompt, x_sample, p_prompt, p_sample, state_pool, state_C, state_n, state_m,
           g_pre_mix, w_in, b_i, b_f, w_pool_grp, s_pool, g_head, w_pa, w_pb, w_out,
           g_post_mix, g_pre_ffn, w_gate, w_up, w_down, g_post_ffn, w_ple, w_ple_gate):
    f32 = lambda a: np.ascontiguousarray(np.asarray(a, dtype=np.float32))
    x_prompt, x_sample = f32(x_prompt), f32(x_sample)
    p_prompt, p_sample = f32(p_prompt)[0], f32(p_sample)[0]
    state_pool, state_C, state_n, state_m = f32(state_pool)[0], f32(state_C)[0], f32(state_n)[0], f32(state_m)[0]
    shared = {
        "g_pre_mix": f32(g_pre_mix), "w_in": f32(w_in)[0], "b_i": f32(b_i), "b_f": f32(b_f),
        "w_pool_grp": f32(w_pool_grp)[0], "s_pool": f32(s_pool), "g_head": f32(g_head).reshape(1, 2048),
        "w_pa": f32(w_pa)[0], "w_pb": f32(w_pb)[0], "w_out": f32(w_out)[0], "g_post_mix": f32(g_post_mix),
        "g_pre_ffn": f32(g_pre_ffn), "w_gate": f32(w_gate)[0], "w_up": f32(w_up)[0], "w_down": f32(w_down)[0],
        "g_post_ffn": f32(g_post_ffn), "w_ple": f32(w_ple)[0], "w_ple_gate": f32(w_ple_gate)[0],
    }
    shared.update(_consts())
    in_maps = []
    for c in range(8):
        b, j = c // 2, c % 2
        xs = x_sample[16 * c:16 * c + 16].reshape(64, DM)
        xm = x_prompt[b, j * 1024:(j + 1) * 1024]
        xp = x_prompt[b, 0:1024] if j == 1 else np.zeros((1024, DM), np.float32)
        posv = np.concatenate([np.arange(j * 1024, (j + 1) * 1024), 16384 + np.tile(np.arange(4), 16)])
        m = dict(shared)
        m.update({
            "x": np.ascontiguousarray(np.concatenate([xp, xm, xs], 0)),
            "p": np.ascontiguousarray(np.concatenate([p_prompt[b, j * 1024:(j + 1) * 1024],
                                                      p_sample[16 * c:16 * c + 16].reshape(64, 256)], 0)),
            "pos": posv.astype(np.float32).reshape(1, T),
            "spool": np.ascontiguousarray(state_pool[16 * c:16 * c + 16]),
            "sC": np.ascontiguousarray(state_C[16 * c:16 * c + 16]),
            "sn": np.ascontiguousarray(state_n[16 * c:16 * c + 16]),
            "sm": np.ascontiguousarray(state_m[16 * c:16 * c + 16]).reshape(1, 64),
        })
        in_maps.append(m)
    if "nc" not in _NC_CACHE:
        _NC_CACHE["nc"] = build()
    res = run_bass_kernel_spmd(_NC_CACHE["nc"], in_maps, core_ids=list(range(8)))
    R = res.results
    yp = np.zeros((4, 2048, DM), np.float32)
    ys = np.zeros((128, 4, DM), np.float32)
    pool_p = np.zeros((1, 4, 15, 1024), np.float32)
    C_p = np.zeros((1, 4, 4, 256, 512), np.float32)
    n_p = np.zeros((1, 4, 4, 256), np.float32)
    m_p = np.zeros((1, 4, 4), np.float32)
    pool_s = np.zeros((1, 128, 15, 1024), np.float32)
    C_s = np.zeros((1, 128, 4, 256, 512), np.float32)
    n_s = np.zeros((1, 128, 4, 256), np.float32)
    m_s = np.zeros((1, 128, 4), np.float32)
    for c in range(8):
        b, j = c // 2, c % 2
        r = R[c]
        yp[b, j * 1024:(j + 1) * 1024] = r["y"][0:1024]
        ys[16 * c:16 * c + 16] = r["y"][1024:1088].reshape(16, 4, DM)
        if j == 1:
            pool_p[0, b] = r["poolP"]
            C_p[0, b] = r["CP"]
            n_p[0, b] = r["nP"]
            m_p[0, b] = r["mP"].reshape(4)
        pool_s[0, 16 * c:16 * c + 16] = r["poolS"].reshape(16, 15, 1024)
        C_s[0, 16 * c:16 * c + 16] = r["CS"]
        n_s[0, 16 * c:16 * c + 16] = r["nS"]
        m_s[0, 16 * c:16 * c + 16] = r["mS"]
    return (yp, ys, pool_p, C_p, n_p, m_p, pool_s, C_s, n_s, m_s)
```

```python
import concourse.bass as bass
import concourse.mybir as mybir

F32 = mybir.dt.float32
BF16 = mybir.dt.bfloat16
ALU = mybir.AluOpType
AF = mybir.ActivationFunctionType
AX = mybir.AxisListType

ENGS = ("pe", "act", "dve", "pool", "sp")
NDMASEM = 12
NDMA_Q = {"pool": 4, "sp": 12, "act": 4, "pe": 4, "dve": 4}
_DSZ = {F32: 4, BF16: 2, mybir.dt.int32: 4, mybir.dt.uint8: 1}


def _dsize(dt):
    return _DSZ[dt]


def ap_range(ap):
    t = ap.tensor
    name = t.name
    esz = _dsize(ap.dtype)
    dims = list(ap.ap)
    space = str(ap.space)
    if "DRAM" in space.upper() or "Dram" in space or "dram" in space:
        lo = ap.offset
        hi = lo + sum((c - 1) * abs(s) for s, c in dims) + 1
        return ("d:" + name, lo * esz, hi * esz, False)
    L = dims[0][0]
    fo = ap.offset % L if L > 0 else ap.offset
    hi = fo + sum((c - 1) * abs(s) for s, c in dims[1:]) + 1
    lo_b, hi_b = fo * esz, hi * esz
    if "PSUM" in space.upper() or "Psum" in space:
        b0, b1 = lo_b // 2048, (hi_b - 1) // 2048
        return ("p:" + name, b0 * 2048, (b1 + 1) * 2048, True)
    return ("s:" + name, lo_b, hi_b, False)


class Op:
    __slots__ = ("eng", "fn", "deps", "is_dma", "sem", "semval", "seq", "signal", "idx", "prev_dma")

    def __init__(self, eng, fn, is_dma):
        self.eng = eng
        self.fn = fn
        self.deps = set()
        self.is_dma = is_dma
        self.sem = None
        self.semval = 0
        self.seq = 0
        self.signal = False
        self.prev_dma = None


class Sched:
    def __init__(self, nc):
        self.nc = nc
        self.streams = {e: [] for e in ENGS}
        self.recs = {}
        self.dma_hist = {e: [] for e in ENGS}
        self.nops = 0

    def _touch(self, op, ap, is_write):
        key, lo, hi, excl = ap_range(ap)
        if excl:
            is_write = True
        lst = self.recs.setdefault(key, [])
        keep = []
        for rec in lst:
            rlo, rhi, rop, rw = rec
            if rlo < hi and lo < rhi:
                if is_write or rw:
                    if rop is not op:
                        op.deps.add(rop)
                if is_write and lo <= rlo and rhi <= hi:
                    continue
            keep.append(rec)
        if not is_write:
            keep = [rc for rc in keep if rc[3] or rc[2].eng != op.eng or rc[2].is_dma or op.is_dma
                    or not (lo <= rc[0] and rc[1] <= hi)]
        keep.append([lo, hi, op, is_write])
        self.recs[key] = keep

    def add(self, eng, fn, w=(), r=(), dma=False):
        op = Op(eng, fn, dma)
        op.idx = self.nops
        self.nops += 1
        for ap in r:
            self._touch(op, ap, False)
        for ap in w:
            self._touch(op, ap, True)
        if dma:
            h = self.dma_hist[eng]
            i = len(h)
            nq = NDMA_Q[eng]
            op.sem = (eng, i % nq)
            op.semval = 16 * (i // nq + 1)
            if i >= nq:
                op.prev_dma = h[i - nq]
            h.append(op)
        self.streams[eng].append(op)
        return op

    def dma(self, q, out, in_, **kw):
        return self.add(q, lambda e: e.dma_start(out=out, in_=in_, **kw), w=[out], r=[in_], dma=True)

    def mm(self, out, lhsT, rhs, start, stop, **kw):
        return self.add("pe", lambda e: e.matmul(out, lhsT, rhs, start=start, stop=stop, **kw),
                        w=[out], r=[lhsT, rhs])

    def tr(self, out, in_, ident):
        return self.add("pe", lambda e: e.transpose(out, in_, ident), w=[out], r=[in_, ident])

    def act(self, out, in_, func, bias=None, scale=None, accum_out=None, eng="act"):
        kw = {}
        r = [in_]
        w = [out]
        if bias is not None:
            kw["bias"] = bias
            if not isinstance(bias, (int, float)):
                r.append(bias)
        if scale is not None:
            kw["scale"] = scale
            if not isinstance(scale, (int, float)):
                r.append(scale)
        if accum_out is not None:
            kw["accum_out"] = accum_out
            w.append(accum_out)
        return self.add(eng, lambda e: e.activation(out, in_, func, **kw), w=w, r=r)

    def tt(self, eng, out, in0, in1, op):
        return self.add(eng, lambda e: e.tensor_tensor(out, in0, in1, op), w=[out], r=[in0, in1])

    def ts(self, eng, out, in0, s1, s2, op0, op1=None, accum_out=None):
        r = [in0]
        if not isinstance(s1, (int, float)):
            r.append(s1)
        if s2 is not None and not isinstance(s2, (int, float)):
            r.append(s2)
        w = [out]
        kw = {}
        if accum_out is not None:
            kw["accum_out"] = accum_out
            w.append(accum_out)
        if op1 is None:
            return self.add(eng, lambda e: e.tensor_scalar(out, in0, s1, None, op0, **kw), w=w, r=r)
        return self.add(eng, lambda e: e.tensor_scalar(out, in0, s1, s2, op0, op1, **kw), w=w, r=r)

    def stt(self, eng, out, in0, scalar, in1, op0, op1):
        r = [in0, in1]
        if not isinstance(scalar, (int, float)):
            r.append(scalar)
        return self.add(eng, lambda e: e.scalar_tensor_tensor(out, in0, scalar, in1, op0, op1), w=[out], r=r)

    def copy(self, eng, out, in_):
        if eng == "act":
            return self.add(eng, lambda e: e.copy(out, in_), w=[out], r=[in_])
        return self.add(eng, lambda e: e.tensor_copy(out, in_), w=[out], r=[in_])

    def memset(self, eng, ap, val):
        return self.add(eng, lambda e: e.memset(ap, val), w=[ap])

    def finalize(self):
        for e in ENGS:
            for op in self.streams[e]:
                for d in op.deps:
                    if d.is_dma:
                        continue
                    if d.eng == op.eng and d.eng == "pe":
                        continue
                    d.signal = True
        for e in ENGS:
            n = 0
            for op in self.streams[e]:
                if op.signal and not op.is_dma:
                    n += 1
                    op.seq = n

    def emit(self, final_waits=True):
        nc = self.nc
        self.finalize()
        import contextlib
        with contextlib.ExitStack() as st:
            esem = {e: st.enter_context(nc.semaphore("es_" + e)) for e in ENGS}
            dsem = {}
            for e in ENGS:
                if self.dma_hist[e]:
                    for i in range(NDMASEM):
                        dsem[(e, i)] = st.enter_context(nc.semaphore("ds_%s_%d" % (e, i)))
            block = st.enter_context(nc.Block())
            streams = self.streams
            dma_hist = self.dma_hist

            def run(ename, eng):
                seen = {}

                def wait(sem_key, sem, val):
                    if seen.get(sem_key, 0) >= val:
                        return
                    seen[sem_key] = val
                    eng.wait_ge(sem, val)

                for op in streams[ename]:
                    if op.prev_dma is not None:
                        p = op.prev_dma
                        wait(("d",) + p.sem, dsem[p.sem], p.semval)
                    for d in sorted(op.deps, key=lambda o: o.idx):
                        if d.is_dma:
                            wait(("d",) + d.sem, dsem[d.sem], d.semval)
                        else:
                            if d.eng == ename and ename == "pe":
                                continue
                            wait(("e", d.eng), esem[d.eng], d.seq)
                    ins = op.fn(eng)
                    if op.is_dma:
                        ins.then_inc(dsem[op.sem], 16)
                    elif op.signal:
                        ins.then_inc(esem[ename], 1)
                if final_waits:
                    h = dma_hist[ename]
                    last = {}
                    for p in h:
                        last[p.sem] = p.semval
                    for k, v in last.items():
                        wait(("d",) + k, dsem[k], v)

            @block.tensor
            def _(eng):
                run("pe", eng)

            @block.scalar
            def _(eng):
                run("act", eng)

            @block.vector
            def _(eng):
                run("dve", eng)

            @block.gpsimd
            def _(eng):
                run("pool", eng)

            @block.sync
            def _(eng):
                run("sp", eng)


class Arena:
    def __init__(self, big_f32, nbytes):
        self.big = big_f32
        self.nbytes = nbytes

    def view(self, off, shape, dt):
        esz = _dsize(dt)
        n = 1
        for s in shape[1:]:
            n *= s
        nb = n * esz
        assert off % 4 == 0 and off + nb <= self.nbytes, (off, nb, self.nbytes)
        nb4 = (nb + 3) // 4
        v = self.big[0:shape[0], off // 4: off // 4 + nb4]
        if dt != F32:
            v = v.bitcast(dt)
            v = v[:, 0:n]
        if len(shape) == 3:
            v = v.rearrange("p (a b) -> p a b", a=shape[1])
        elif len(shape) == 4:
            v = v.rearrange("p (a b c) -> p a b c", a=shape[1], b=shape[2])
        return v


import contextlib
import itertools
import numpy as np
from concourse.bass_utils import run_bass_kernel_spmd

DM = 2048
NPFX = 1024
NM = 1024
NS = 64
T = NM + NS
NSEQ = 16
DFF = 5632
COL_U, COL_Q, COL_K, COL_V, COL_O, COL_IG, COL_GA, COL_GB = 0, 1024, 2048, 3072, 5120, 7168, 7176, 9224
EPS = 1e-6
HALO = 16
SBUF_F32 = 53200


class Bump:
    def __init__(self, arena, start, limit):
        self.A, self.p, self.q = arena, start, limit
        self.peak = 0

    def _nb(self, shape, dt):
        n = 1
        for s in shape[1:]:
            n *= s
        return (n * _dsize(dt) + 31) // 32 * 32

    def alloc(self, shape, dt):
        nb = self._nb(shape, dt)
        v = self.A.view(self.p, shape, dt)
        self.p += nb
        assert self.p <= self.q, ("SBUF overflow", self.p, self.q)
        return v

    def hi(self, shape, dt):
        nb = self._nb(shape, dt)
        self.q -= nb
        assert self.p <= self.q, ("SBUF overflow", self.p, self.q)
        return self.A.view(self.q, shape, dt)


def build(stop_after=None):
    nc = bass.Bass("TRN2", target_bir_lowering=False)

    def din(name, shape):
        return nc.dram_tensor(name, shape, F32, kind="ExternalInput").ap()

    def dout(name, shape):
        return nc.dram_tensor(name, shape, F32, kind="ExternalOutput").ap()

    x = din("x", [NPFX + T, DM])
    p_in = din("p", [T, 256])
    pos = din("pos", [1, T])
    spool = din("spool", [NSEQ, 15, 1024])
    sC = din("sC", [NSEQ, 4, 256, 512])
    sn = din("sn", [NSEQ, 4, 256])
    sm = din("sm", [1, NSEQ * 4])
    g_pre_mix = din("g_pre_mix", [1, DM])
    w_in = din("w_in", [DM, 11272])
    b_i = din("b_i", [1, 4])
    b_f = din("b_f", [1, 4])
    w_pool = din("w_pool_grp", [4, 256, 256])
    s_pool = din("s_pool", [1, 1024])
    g_head = din("g_head", [1, 2048])
    w_pa = din("w_pa", [1024, DM])
    w_pb = din("w_pb", [DM, DM])
    w_out = din("w_out", [DM, DM])
    g_post_mix = din("g_post_mix", [1, DM])
    g_pre_ffn = din("g_pre_ffn", [1, DM])
    w_gate = din("w_gate", [DM, DFF])
    w_up = din("w_up", [DM, DFF])
    w_down = din("w_down", [DFF, DM])
    g_post_ffn = din("g_post_ffn", [1, DM])
    w_ple = din("w_ple", [256, DM])
    w_ple_gate = din("w_ple_gate", [DM, DM])
    cI = din("cI", [128, 128])
    cTri = din("cTri", [128, 128])
    cTriS = din("cTriS", [64, 64])
    cOnesS = din("cOnesS", [64, 64])
    cSel = din("cSel", [64, 16])
    cSelT = din("cSelT", [16, 64])
    cMq = din("cMq", [1, 1024])

    y_out = dout("y", [T, DM])
    poolP = dout("poolP", [15, 1024])
    CP = dout("CP", [4, 256, 512])
    nP = dout("nP", [4, 256])
    mP = dout("mP", [4, 1])
    poolS = dout("poolS", [NSEQ * 15, 1024])
    CS = dout("CS", [NSEQ, 4, 256, 512])
    nS = dout("nS", [NSEQ, 4, 256])
    mS = dout("mS", [NSEQ, 4])

    x1s = nc.dram_tensor("x1s", [T, DM], F32).ap()
    fsc = nc.dram_tensor("fsc", [T, DM], F32).ap()

    with contextlib.ExitStack() as st:
        big = st.enter_context(nc.sbuf_tensor("big", [128, SBUF_F32], F32))
        ps = st.enter_context(nc.psum_tensor("ps", [128, 4096], F32))
        A = Arena(big, SBUF_F32 * 4)
        S = Sched(nc)
        B = Bump(A, 0, SBUF_F32 * 4)

        def bank(i, n=512):
            return ps[:, i * 512: i * 512 + n]

        identF = B.alloc([128, 128], F32)
        identB = B.alloc([128, 128], BF16)
        tri = B.alloc([128, 128], F32)
        onesF = B.alloc([128, 128], F32)
        onesB = B.alloc([128, 8], BF16)
        triS = B.alloc([64, 64], F32)
        onesS = B.alloc([64, 64], F32)
        sel = B.alloc([64, 16], F32)
        selT = B.alloc([16, 64], F32)
        off_maskq = B.p
        maskq = B.alloc([128, 16, 64], BF16)
        gbuf = A.view(off_maskq, [128, 8, 32], F32)
        gT = B.alloc([128, 16], F32)
        spT = B.alloc([128, 8], F32)
        off_bibc = B.p
        bibc = B.alloc([128, 4], F32)
        mhalf = A.view(off_bibc + 16, [128, 1], F32)
        bfbc = B.alloc([128, 4], F32)
        gst = B.alloc([128, 17, 16], F32)
        dmbS = B.alloc([64, 4], F32)
        bLS = B.alloc([64, 4], F32)
        mxrow = B.alloc([1, 64], F32)
        bLrow = B.alloc([1, 64], F32)
        mrun = B.alloc([1, 4], F32)
        emf = B.alloc([128, 4], F32)
        scr = B.alloc([128, 64], F32)
        wg = B.alloc([128, 16, 8], BF16)
        TOP = B.q
        Zst = B.hi([128, 4, 2, 512], F32)
        nZ = B.hi([128, 4, 2], F32)
        _sc = [0]

        def sc(n=1):
            if _sc[0] + n > 64:
                _sc[0] = 0
            v = scr[:, _sc[0]: _sc[0] + n]
            _sc[0] += n
            return v

        S.dma("sp", identF, cI)
        S.dma("sp", tri, cTri)
        S.dma("sp", triS, cTriS)
        S.dma("sp", onesS, cOnesS)
        S.dma("sp", sel, cSel)
        S.dma("sp", selT, cSelT)
        S.dma("sp", gT, g_pre_mix.rearrange("o (k p) -> p (o k)", p=128), allow_slow_non_contiguous=True)
        S.dma("sp", spT, s_pool.rearrange("o (k p) -> p (o k)", p=128), allow_slow_non_contiguous=True)
        S.dma("sp", bibc, b_i.partition_broadcast(128).rearrange("p a b -> p (a b)"))
        S.dma("sp", bfbc, b_f.partition_broadcast(128).rearrange("p a b -> p (a b)"))
        S.copy("dve", identB, identF)
        S.memset("dve", onesF, 1.0)
        S.memset("dve", mhalf, -0.5)
        S.memset("dve", onesB, 1.0)
        S.memset("dve", Zst.rearrange("p a b c -> p (a b c)"), 0.0)
        S.memset("dve", nZ.rearrange("p a b -> p (a b)"), 0.0)
        S.memset("dve", mrun, 0.0)
        S.memset("dve", gst.rearrange("p a b -> p (a b)"), 0.0)

        def interleave(main, side, every, lead):
            cnt = 0
            side_done = side is None
            for _ in main:
                cnt += 1
                if not side_done and cnt >= lead and (cnt - lead) % every == 0:
                    try:
                        next(side)
                    except StopIteration:
                        side_done = True
            if not side_done:
                for _ in side:
                    pass

        evc = itertools.cycle(["act", "dve"])

        def evac(out, in_, mul=None, eng=None):
            eng = eng or next(evc)
            if eng == "act":
                if mul is None:
                    S.add("act", lambda e: e.copy(out, in_), w=[out], r=[in_])
                else:
                    S.add("act", lambda e: e.mul(out, in_, mul), w=[out], r=[in_])
            else:
                if mul is None:
                    S.copy("dve", out, in_)
                else:
                    S.ts("dve", out, in_, mul, None, ALU.mult)

        def rsqrt(ss, scale, n):
            t1, t2, t3 = sc(), sc(), sc()
            S.ts("dve", t1[0:n], ss[0:n], scale, EPS, ALU.mult, ALU.add)
            S.act(t2[0:n], t1[0:n], AF.Ln)
            S.act(t3[0:n], t2[0:n], AF.Exp, scale=-0.5)
            return t3

        def rsqrt_pow(ss, scale, n):
            t1, t3 = sc(), sc()
            S.ts("dve", t1[0:n], ss[0:n], scale, EPS, ALU.mult, ALU.add)
            S.tt("pool", t3[0:n], t1[0:n], mhalf[0:n], ALU.pow)
            return t3

        def sumsq(junk, src, n):
            ss = sc()
            S.memset("dve", ss[0:n], 0.0)
            S.act(junk, src, AF.Square, accum_out=ss[0:n])
            return ss

        def wload(dst, src_rows_cols):
            kc = dst.shape[1]
            half = max(1, kc // 2)
            v = src_rows_cols.rearrange("(k p) n -> p k n", p=128)
            for k0 in range(0, kc, half):
                k1 = min(kc, k0 + half)
                S.dma("pool", dst[:, k0:k1, :], v[:, k0:k1, :])

        m_phase = B.p

        def norm_stats(r0, n, xt, xbf):
            S.dma("sp", xt[0:n, :], x[r0:r0 + n, :])
            ss = sumsq(xbf[0:n, :], xt[0:n, :], n)
            return rsqrt(ss, 1.0 / DM, n)

        def norm_apply(rstd, n, dst3, c0, xt, xbf, psT3):
            S.ts("dve", xbf[0:n, :], xt[0:n, :], rstd[0:n], None, ALU.mult)
            for kc in range(16):
                S.tr(psT3[:, kc, 0:n], xbf[0:n, kc * 128:(kc + 1) * 128], identB[0:n, 0:n])
            S.tt("dve", dst3[:, :, c0:c0 + n], psT3[:, :, 0:n],
                 gT.unsqueeze(2).broadcast_to([128, 16, n]), ALU.mult)

        def norm_gen(tiles, dst3, xs, xb):
            nx, nb = len(xs), len(xb)
            nt = len(tiles)

            def load(i):
                S.dma("sp", xs[i % nx][0:tiles[i][1], :], x[tiles[i][0]:tiles[i][0] + tiles[i][1], :])

            for i in range(min(nx - 1, nt)):
                load(i)
            ss = sumsq(xb[0][0:tiles[0][1], :], xs[0][0:tiles[0][1], :], tiles[0][1])
            r = rsqrt(ss, 1.0 / DM, tiles[0][1])
            for i, (r0, n, c0) in enumerate(tiles):
                if i + nx - 1 < nt:
                    load(i + nx - 1)
                ssn = None
                if i + 1 < nt:
                    n1 = tiles[i + 1][1]
                    ssn = sumsq(xb[(i + 1) % nb][0:n1, :], xs[(i + 1) % nx][0:n1, :], n1)
                xt, xbf, psT3 = xs[i % nx], xb[i % nb], psTs[i % 2]
                S.ts("dve", xbf[0:n, :], xt[0:n, :], r[0:n], None, ALU.mult)
                for kc in range(16):
                    S.tr(psT3[:, kc, 0:n], xbf[0:n, kc * 128:(kc + 1) * 128], identB[0:n, 0:n])
                rn = rsqrt(ssn, 1.0 / DM, tiles[i + 1][1]) if ssn is not None else None
                S.tt("dve", dst3[:, :, c0:c0 + n], psT3[:, :, 0:n],
                     gT.unsqueeze(2).broadcast_to([128, 16, n]), ALU.mult)
                r = rn
                yield

        def norm_tiles(tiles, dst3, xs, xb):
            for _ in norm_gen(tiles, dst3, xs, xb):
                pass

        hT = B.alloc([128, 16, HALO + T], BF16)
        m_hT = B.p
        hTP = B.alloc([128, 16, NPFX], BF16)
        nrm_x = [B.alloc([128, DM], F32) for _ in range(2)]
        nrm_b = [B.alloc([128, DM], BF16) for _ in range(2)]
        m_afterP = B.p
        nrm0_x = [B.alloc([128, DM], F32) for _ in range(3)]
        nrm0_b = [B.alloc([128, DM], BF16) for _ in range(2)]
        B.p = m_afterP
        psTs = [ps[:, i * 1024:(i + 1) * 1024].bitcast(BF16).rearrange("p (a b) -> p a b", a=16) for i in range(2)]
        norm_tiles([(tt * 128, 128, tt * 128) for tt in range(8)], hTP, nrm0_x, nrm0_b)

        wload(wg, w_in[:, COL_IG:COL_IG + 8])

        def gate_group(hT3, c0, slot0):
            gps = bank(7)
            G = gps[:, 0:64].rearrange("p (t g) -> p t g", g=8)
            for i in range(8):
                for kc in range(16):
                    S.mm(gps[:, i * 8:(i + 1) * 8], hT3[:, kc, c0 + i * 128: c0 + (i + 1) * 128], wg[:, kc, :],
                         kc == 0, kc == 15)
            gb = gbuf
            v3 = lambda k: gb[:, k, :].rearrange("p (t h) -> p t h", h=4)
            zf, e, sp, lf, ig, dmb, dl, bLs = (gb[:, k, :] for k in range(8))
            bc = lambda t: t.unsqueeze(1).broadcast_to([128, 8, 4])
            S.tt("dve", v3(0), G[:, :, 4:8], bc(bfbc), ALU.add)
            S.act(e, zf, AF.Exp, scale=-1.0)
            S.act(sp, e, AF.Ln, bias=1.0)
            S.ts("dve", lf, sp, -1.0, None, ALU.mult)
            S.tt("dve", v3(4), G[:, :, 0:4], bc(bibc), ALU.add)
            b_ps, bL_ps = gps[:, 64:96], gps[:, 96:128]
            S.mm(b_ps, tri, lf, True, True)
            S.mm(bL_ps, onesF, lf, True, True)
            S.tt("dve", dmb, ig, b_ps, ALU.subtract)
            gsl = lambda a: gst[:, slot0:slot0 + 8, a:a + 4]
            S.act(gsl(0), v3(5), AF.Exp)
            S.tt("dve", dl, dmb, bL_ps, ALU.add)
            S.act(gsl(4), v3(6), AF.Exp)
            S.act(gsl(8), b_ps.rearrange("p (t h) -> p t h", h=4), AF.Exp, scale=-1.0)
            S.act(gsl(12), bL_ps.rearrange("p (t h) -> p t h", h=4), AF.Exp)
            S.copy("dve", bLs, bL_ps)
            S.copy("dve", bLrow[0:1, slot0 * 4: slot0 * 4 + 32], bLs[0:1, :])
            t1 = gps[0:32, 128:256]
            S.tr(t1, dmb, identF)
            mxc = gb[0:32, 0, 0:1]
            S.add("dve", lambda e_: e_.reduce_max(mxc, t1, AX.X), w=[mxc], r=[t1])
            t2 = gps[0:1, 256:288]
            S.tr(t2, mxc, identF[0:32, 0:32])
            S.copy("dve", mxrow[0:1, slot0 * 4: slot0 * 4 + 32], t2)

        def gate_tile(hT3, c0, n, slot, mcol, sample=False):
            gps = bank(7)
            for kc in range(16):
                S.mm(gps[0:n, 0:8], hT3[:, kc, c0:c0 + n], wg[:, kc, :], kc == 0, kc == 15)
            zf, e, sp, lf, ig = sc(4), sc(4), sc(4), sc(4), sc(4)
            S.tt("dve", zf[0:n], gps[0:n, 4:8], bfbc[0:n], ALU.add)
            S.act(e[0:n], zf[0:n], AF.Exp, scale=-1.0)
            S.act(sp[0:n], e[0:n], AF.Ln, bias=1.0)
            S.ts("dve", lf[0:n], sp[0:n], -1.0, None, ALU.mult)
            S.tt("dve", ig[0:n], gps[0:n, 0:4], bibc[0:n], ALU.add)
            triM, onesM = (triS, onesS) if sample else (tri, onesF)
            b_ps, bL_ps = gps[:, 16:20], gps[:, 32:36]
            S.mm(b_ps[0:n], triM[0:n, 0:n], lf[0:n], True, True)
            S.mm(bL_ps[0:n], onesM[0:n, 0:n], lf[0:n], True, True)
            dmb = dmbS if sample else sc(4)
            bLs = bLS if sample else sc(4)
            dl = sc(4)
            S.tt("dve", dmb[0:n], ig[0:n], b_ps[0:n], ALU.subtract)
            S.act(gst[0:n, slot, 0:4], dmb[0:n], AF.Exp)
            S.tt("dve", dl[0:n], dmb[0:n], bL_ps[0:n], ALU.add)
            S.act(gst[0:n, slot, 4:8], dl[0:n], AF.Exp)
            S.act(gst[0:n, slot, 8:12], b_ps[0:n], AF.Exp, scale=-1.0)
            S.act(gst[0:n, slot, 12:16], bL_ps[0:n], AF.Exp)
            S.copy("dve", bLs[0:n], bL_ps[0:n])

        gate_group(hTP, 0, 0)

        if stop_after == "p0":
            dbg = dout("dbg", [128, 16 * NPFX // 2])
            S.dma("sp", dbg, hTP.rearrange("p a b -> p (a b)").bitcast(F32))
            dbg2 = dout("dbg2", [128, 17 * 16])
            S.dma("sp", dbg2, gst.rearrange("p a b -> p (a b)"))
            S.emit()
            return nc

        q_1c = B.q
        kP = B.hi([128, 4, 8, 256], BF16)
        vP = B.hi([128, 4, 8, 512], BF16)
        kwp = [B.hi([128, 256], BF16) for _ in range(2)] * 2
        q_P = B.q
        wkP = [B.alloc([128, 16, 256], BF16)] * 2
        wvP = [B.alloc([128, 16, 512], BF16) for _ in range(2)]
        pb = itertools.cycle(range(4))
        pbP = itertools.cycle((4, 5, 6))

        def pproj_gen():
            for h in range(4):
                wk = wkP[h % 2]
                wv = wvP[h % 2]
                wload(wk, w_in[:, COL_K + h * 256: COL_K + (h + 1) * 256])
                wload(wv, w_in[:, COL_V + h * 512: COL_V + (h + 1) * 512])
                for tt in range(8):
                    b_ = bank(next(pbP))
                    for kc in range(16):
                        S.mm(b_[:, 0:256], hTP[:, kc, tt * 128:(tt + 1) * 128], wk[:, kc, :], kc == 0, kc == 15)
                    evac(kP[:, h, tt, :], b_[:, 0:256], mul=1.0 / 16)
                    yield
                for tt in range(8):
                    b_ = bank(next(pbP))
                    for kc in range(16):
                        S.mm(b_, hTP[:, kc, tt * 128:(tt + 1) * 128], wv[:, kc, :], kc == 0, kc == 15)
                    evac(vP[:, h, tt, :], b_)
                    yield

        S.copy("dve", hT[:, :, 0:HALO], hTP[:, :, NPFX - HALO:NPFX])
        interleave(pproj_gen(), norm_gen([(NPFX + tt * 128, 128 if tt < 8 else 64, HALO + tt * 128)
                                          for tt in range(9)], hT, nrm_x, nrm_b), every=6, lead=2)
        gate_group(hT, HALO, 8)
        gate_tile(hT, HALO + 8 * 128, 64, 16, 16, sample=True)

        def kw_prep(h, ktile, slot, kw, n=128):
            S.ts("dve", kw[0:n], ktile, gst[0:n, slot, 4 + h:5 + h], None, ALU.mult)

        def state_update(h, ktile, vtile, slot, kw, n=128):
            un = bank(2)[:, 2 * h:2 * h + 2]
            for dc in range(2):
                S.mm(un[:, dc:dc + 1], kw[0:n, dc * 128:(dc + 1) * 128], onesB[0:n, 0:1], True, True)
            for dc in range(2):
                U = bank(4 + h)
                S.mm(U, kw[0:n, dc * 128:(dc + 1) * 128], vtile, True, True)
                S.stt("dve", Zst[:, h, dc, :], Zst[:, h, dc, :], gst[:, slot, 12 + h:13 + h], U, ALU.mult, ALU.add)
            S.stt("dve", nZ[:, h, :], nZ[:, h, :], gst[:, slot, 12 + h:13 + h], un, ALU.mult, ALU.add)

        def prec_gen():
            steps = [(tt, h) for tt in range(8) for h in range(4)]
            kw_prep(0, kP[:, 0, 0, :], 0, kwp[0])
            for i, (tt, h) in enumerate(steps):
                state_update(h, kP[:, h, tt, :], vP[:, h, tt, :], tt, kwp[h])
                if i + 1 < len(steps):
                    t2, h2 = steps[i + 1]
                    kw_prep(h2, kP[:, h2, t2, :], t2, kwp[h2])
                yield

        B.p = m_hT
        bmT = B.alloc([128, 16, T], BF16)
        m_h = B.p
        for t_ in range(16):
            tmp = sc(4)
            S.tt("dve", tmp[0:1], mrun, mxrow[0:1, t_ * 4:(t_ + 1) * 4], ALU.max)
            S.tt("dve", mrun, tmp[0:1], bLrow[0:1, t_ * 4:(t_ + 1) * 4], ALU.add)
        mrep = bank(7)[:, 320:324]
        S.mm(mrep, onesF[0:1, :], mrun, True, True)
        S.act(emf, mrep, AF.Exp, scale=-1.0)
        S.dma("sp", mP.rearrange("h o -> o h"), mrun)

        if stop_after == "pP":
            dbg = dout("dbg", [128, 4096])
            S.dma("sp", dbg, Zst.rearrange("p a b c -> p (a b c)"))
            dbg2 = dout("dbg2", [128, 17 * 16])
            S.dma("sp", dbg2, gst.rearrange("p a b -> p (a b)"))
            S.emit()
            return nc

        B.p = m_h
        B.q = q_1c
        S.dma("pool", maskq.rearrange("p a b -> p (a b)"), cMq.partition_broadcast(128).rearrange("p a b -> p (a b)"))
        emR = B.alloc([128, 64], F32)
        S.dma("sp", emR, sm.partition_broadcast(128).rearrange("p a b -> p (a b)"))
        S.act(emR, emR, AF.Exp)
        mprevT = B.alloc([4, 16], F32)
        S.dma("sp", mprevT, sm.rearrange("o (s h) -> (o h) s", h=4), allow_slow_non_contiguous=True)
        wslots = [B.alloc([128, 16, 512], BF16) for _ in range(2)]
        qT = B.alloc([128, 2, T], BF16)
        kT = B.alloc([128, 2, T], BF16)
        ktok = B.alloc([128, 9, 256], BF16)
        vtok = B.alloc([128, 9, 512], BF16)
        assert B.p <= q_P, ("head-0 projection buffers overlap the prefix k/v", B.p, q_P)
        wslots.append(B.alloc([128, 16, 512], BF16))
        ghbc = B.alloc([128, 512], F32)
        Zbf = B.alloc([128, 2, 512], BF16)
        nbf = B.alloc([128, 2], BF16)
        kw = B.alloc([128, 256], BF16)
        Sp = B.alloc([128, 128], BF16)
        so = B.alloc([128, 512], F32)
        bmb = B.alloc([128, 512], BF16)
        junk = bmb
        Cs = [B.alloc([128, 2, 512], F32) for _ in range(3)]
        Cb = [B.alloc([128, 2, 512], BF16) for _ in range(1)]
        Cn = [B.alloc([128, 2, 512], F32) for _ in range(2)]
        qm = B.alloc([128, 2, 16, 64], BF16)
        nsT = B.alloc([128, 2, 16], F32)
        nsb = B.alloc([128, 2, 16], BF16)
        wkm = B.alloc([64, 16], F32)
        wkmb = B.alloc([64, 16], BF16)
        kwm = [B.alloc([64, 256], BF16)] * 2
        decR = B.alloc([128, 64], F32)
        nsq = B.alloc([16, 256], F32)
        small = B.alloc([64, 64], F32)
        m_1c_end = B.p

        mnew_sm = B.alloc([16, 4], F32)
        dec_sm = B.alloc([16, 4], F32)
        wkS = B.alloc([64, 4], F32)
        BD = B.alloc([4, 4, 16], F32)

        def sample_scalars():
            tps = bank(7)
            S.tr(tps[0:4, 128:192], dmbS[0:64, 0:4], identF[0:64, 0:64])
            mxT = small[0:4, 0:16]
            S.add("dve", lambda e_: e_.reduce_max(mxT, tps[0:4, 128:192].rearrange("p (s j) -> p s j", j=4), AX.X),
                  w=[mxT], r=[tps[0:4, 128:192]])
            S.tr(tps[0:4, 256:320], bLS[0:64, 0:4], identF[0:64, 0:64])
            bLT = small[0:4, 16:32]
            S.copy("dve", bLT, tps[0:4, 256:320].rearrange("p (s j) -> p s j", j=4)[:, :, 0])
            mnewT = small[0:4, 32:48]
            S.tt("dve", mnewT, mprevT, mxT, ALU.max)
            S.tt("dve", mnewT, mnewT, bLT, ALU.add)
            S.dma("sp", mS.rearrange("s h -> h s"), mnewT, allow_slow_non_contiguous=True)
            decT = small[0:4, 48:64]
            S.tt("dve", decT, bLT, mprevT, ALU.add)
            S.tt("dve", decT, decT, mnewT, ALU.subtract)
            S.act(decT, decT, AF.Exp)
            S.tr(tps[0:16, 384:388], mnewT, identF[0:4, 0:4])
            S.copy("dve", mnew_sm, tps[0:16, 384:388])
            S.tr(tps[0:16, 400:404], decT, identF[0:4, 0:4])
            S.copy("dve", dec_sm, tps[0:16, 400:404])
            mtok = tps[0:64, 416:420]
            S.mm(mtok, selT, mnew_sm, True, True)
            S.tt("dve", wkS, dmbS, bLS, ALU.add)
            S.tt("dve", wkS, wkS, mtok, ALU.subtract)
            S.act(wkS, wkS, AF.Exp)
            S.tt("dve", BD, decT.unsqueeze(1).broadcast_to([4, 4, 16]),
                 identF[0:4, 0:4].unsqueeze(2).broadcast_to([4, 4, 16]), ALU.mult)
            drep = bank(6)[:, 0:64]
            S.mm(drep, onesF[0:4, :], BD.rearrange("p a b -> p (a b)"), True, True)
            S.copy("dve", decR, drep)

        ghb = [ghbc, ghbc]
        so2 = [so, B.alloc([128, 512], F32)]
        Sp2 = [Sp, B.alloc([128, 128], BF16)]
        kw2 = [kw, B.alloc([128, 256], BF16)]
        bmb2 = [bmb, B.alloc([128, 512], BF16)]
        bmb3 = bmb2 + [B.alloc([128, 512], BF16)]
        qTs = B.alloc([128, 2, 64], BF16)
        ktoks = B.alloc([64, 256], BF16)
        vtoks = B.alloc([64, 512], BF16)
        sgo_s = B.alloc([64, 512], F32)
        Sp_s = B.alloc([64, 64], BF16)

        pb2 = itertools.cycle(range(2))

        def qkv_loads(h):
            wqk, wv = wslots[0], wslots[1]
            wload(wqk[:, :, 0:256], w_in[:, COL_Q + h * 256: COL_Q + (h + 1) * 256])
            wload(wqk[:, :, 256:512], w_in[:, COL_K + h * 256: COL_K + (h + 1) * 256])
            wload(wv, w_in[:, COL_V + h * 512: COL_V + (h + 1) * 512])

        def proj_gen(h):
            wqk, wv, wo = wslots[0], wslots[1], wslots[2]
            if h == 0:
                qkv_loads(0)
            if h > 0:
                wload(wo, w_in[:, COL_O + h * 512: COL_O + (h + 1) * 512])
            for which, dstT, mul in ((0, qT, None), (1, kT, 1.0 / 16)):
                for dc in range(2):
                    for (t0, n) in ((0, 512), (512, 512), (1024, 64)):
                        b_ = bank(next(pb2))
                        for kc in range(16):
                            S.mm(b_[:, 0:n], wqk[:, kc, which * 256 + dc * 128: which * 256 + (dc + 1) * 128],
                                 hT[:, kc, HALO + t0: HALO + t0 + n], kc == 0, kc == 15)
                        evac(dstT[:, dc, t0:t0 + n], b_[:, 0:n], mul=mul)
                        yield
            for tt in range(9):
                n = 128 if tt < 8 else 64
                c0 = HALO + tt * 128
                b_ = bank(next(pb2))
                for kc in range(16):
                    S.mm(b_[0:n, 0:256], hT[:, kc, c0:c0 + n], wqk[:, kc, 256:512], kc == 0, kc == 15)
                evac(ktok[0:n, tt, :], b_[0:n, 0:256], mul=1.0 / 16)
                yield
                b_ = bank(next(pb2))
                for kc in range(16):
                    S.mm(b_[0:n, :], hT[:, kc, c0:c0 + n], wv[:, kc, :], kc == 0, kc == 15)
                evac(vtok[0:n, tt, :], b_[0:n, :])
                yield

        def sample_prep(h):
            wo = wslots[2]
            c0 = HALO + NM
            ops = bank(6)
            for kc in range(16):
                S.mm(ops[0:64, :], hT[:, kc, c0:c0 + 64], wo[:, kc, :], kc == 0, kc == 15)
            S.act(sgo_s, ops[0:64, :], AF.Sigmoid)
            STp = bank(0)[0:64, 0:64]
            for dc in range(2):
                S.mm(STp, kT[:, dc, NM:NM + 64], qT[:, dc, NM:NM + 64], dc == 0, dc == 1)
            S.stt("dve", Sp_s, STp, gst[0:64, 16, h:h + 1], triS, ALU.mult, ALU.mult)
            S.copy("dve", qTs, qT[:, :, NM:NM + 64])
            S.copy("dve", ktoks, ktok[0:64, 8, :])
            S.copy("dve", vtoks, vtok[0:64, 8, :])
            for dc in range(2):
                S.tt("dve", qm[:, dc], qTs[:, dc, :].unsqueeze(1).broadcast_to([128, 16, 64]), maskq, ALU.mult)
            S.dma("sp", nsq, sn[:, h, :])
            for dc in range(2):
                tpn = bank(7)[:, 64 + dc * 16: 80 + dc * 16]
                S.tr(tpn, nsq[0:16, dc * 128:(dc + 1) * 128], identF[0:16, 0:16])
                S.copy("dve", nsT[:, dc, :], tpn)
            emh = emR.rearrange("p (s hh) -> p s hh", hh=4)[:, :, h]
            S.tt("dve", nsb, nsT, emh.unsqueeze(1).broadcast_to([128, 2, 16]), ALU.mult)

        def sample_gen(h):
            Np = bank(2)[0:64, :]
            Dp = bank(3)[0:64, 0:1]

            def kwm_for(s_):
                S.ts("dve", wkm[:, s_:s_ + 1], sel[:, s_:s_ + 1], wkS[:, h:h + 1], None, ALU.mult)
                S.ts("dve", kwm[0], ktoks, wkm[:, s_:s_ + 1], None, ALU.mult)

            kwm_for(0)
            for s_ in range(NSEQ):
                cs = Cs[s_ % 3]
                cb = Cb[0]
                S.dma("sp", cs, sC[s_, h].rearrange("(dc p) e -> p dc e", p=128))
                S.add("act", lambda e_, cb=cb, cs=cs, s_=s_: e_.mul(
                    cb.rearrange("p a b -> p (a b)"), cs.rearrange("p a b -> p (a b)"),
                    emR[:, s_ * 4 + h: s_ * 4 + h + 1]),
                    w=[cb], r=[cs, emR[:, s_ * 4 + h: s_ * 4 + h + 1]])
                yield
                for dc in range(2):
                    S.mm(Np, qm[:, dc, s_, :], cb[:, dc, :], (s_ == 0 and dc == 0), False)
                for dc in range(2):
                    S.mm(Dp, qm[:, dc, s_, :], nsb[:, dc, s_:s_ + 1], (s_ == 0 and dc == 0), False)
                kwm_ = kwm[0]
                cn = Cn[s_ % 2]
                for dc in range(2):
                    U = bank(4 + dc)
                    S.mm(U, kwm_[:, dc * 128:(dc + 1) * 128], vtoks, True, True)
                    S.stt("dve", cn[:, dc, :], cs[:, dc, :], decR[:, h * 16 + s_: h * 16 + s_ + 1], U,
                          ALU.mult, ALU.add)
                S.dma("pool", CS[s_, h].rearrange("(dc p) e -> p dc e", p=128), cn)
                if s_ + 1 < NSEQ:
                    kwm_for(s_ + 1)
                yield

        def finish_tile(h, n, Np, Dp, so_, ebcol, bmb_):
            d2, d3, d4 = sc(), sc(), sc()
            S.act(d2[0:n], Dp, AF.Abs)
            S.tt("dve", d3[0:n], d2[0:n], ebcol, ALU.max)
            S.add("dve", lambda e_, d3=d3, d4=d4, n=n: e_.reciprocal(d4[0:n], d3[0:n]), w=[d4[0:n]], r=[d3[0:n]])
            S.add("act", lambda e_, d4=d4, n=n: e_.mul(Np, Np, d4[0:n]), w=[Np], r=[Np, d4[0:n]])
            ss = sumsq(bmb_[0:n], Np, n)
            rstd = rsqrt_pow(ss, 1.0 / 512, n)
            S.stt("dve", Np, Np, rstd[0:n], ghb[h % 2][0:n, :], ALU.mult, ALU.mult)
            S.tt("dve", bmb_[0:n], Np, so_, ALU.mult)

        def bm_transpose(h, n, t0, bmb_):
            bps = bank(7).bitcast(BF16).rearrange("p (a b) -> p a b", b=128)
            for e4 in range(4):
                S.tr(bps[:, e4, 0:n], bmb_[0:n, e4 * 128:(e4 + 1) * 128], identB[0:n, 0:n])
            evac(bmT[:, h * 4:(h + 1) * 4, t0:t0 + n], bps[:, 0:4, 0:n], eng="act")

        def sample_fin(h):
            Np = bank(2)[0:64, :]
            Dp = bank(3)[0:64, 0:1]
            S.mm(Np, Sp_s, vtoks, False, True)
            S.mm(Dp, Sp_s, onesB[0:64, 0:1], False, True)
            S.copy("dve", wkmb, wkm)
            npn = bank(6)[0:16, 0:256]
            S.mm(npn, wkmb, ktoks, True, True)
            S.stt("dve", nsq, nsq, dec_sm[:, h:h + 1], npn, ALU.mult, ALU.add)
            S.dma("sp", nS[:, h, :], nsq)
            finish_tile(h, 64, Np, Dp, sgo_s, gst[0:64, 16, 8 + h:9 + h], bmb2[0])
            bm_transpose(h, 64, NM, bmb2[0])

        def prompt_loop(h):
            wo = wslots[2]
            S.dma("sp", ghbc, g_head[:, h * 512:(h + 1) * 512].partition_broadcast(128).rearrange("p a b -> p (a b)"))
            S.copy("act", Zbf.rearrange("p a b -> p (a b)"), Zst[:, h].rearrange("p a b -> p (a b)"))
            S.copy("act", nbf, nZ[:, h, :])

            def stA(tt):
                c0 = HALO + tt * 128
                t0 = tt * 128
                slot = 8 + tt
                ops = bank(6)
                for kc in range(16):
                    S.mm(ops, hT[:, kc, c0:c0 + 128], wo[:, kc, :], kc == 0, kc == 15)
                S.act(so2[tt % 2], ops, AF.Sigmoid)
                STp = bank(0)[:, 0:128]
                for dc in range(2):
                    S.mm(STp, kT[:, dc, t0:t0 + 128], qT[:, dc, t0:t0 + 128], dc == 0, dc == 1)
                S.stt("dve", Sp2[tt % 2], STp, gst[:, slot, h:h + 1], tri, ALU.mult, ALU.mult)
                S.ts("dve", kw2[tt % 2], ktok[:, tt, :], gst[:, slot, 4 + h:5 + h], None, ALU.mult)

            def stB(tt):
                t0 = tt * 128
                slot = 8 + tt
                Np = bank(2 + tt % 2)
                Dp = bank(1)[:, 0:1]
                for dc in range(2):
                    S.mm(Np, qT[:, dc, t0:t0 + 128], Zbf[:, dc, :], dc == 0, False)
                S.mm(Np, Sp2[tt % 2], vtok[:, tt, :], False, True)
                for dc in range(2):
                    S.mm(Dp, qT[:, dc, t0:t0 + 128], nbf[:, dc:dc + 1], dc == 0, False)
                S.mm(Dp, Sp2[tt % 2], onesB[:, 0:1], False, True)
                kw_ = kw2[tt % 2]
                un = bank(1)[:, 8:10]
                for dc in range(2):
                    S.mm(un[:, dc:dc + 1], kw_[:, dc * 128:(dc + 1) * 128], onesB[:, 0:1], True, True)
                for dc in range(2):
                    U = bank(4 + dc)
                    S.mm(U, kw_[:, dc * 128:(dc + 1) * 128], vtok[:, tt, :], True, True)
                    S.stt("dve", Zst[:, h, dc, :], Zst[:, h, dc, :], gst[:, slot, 12 + h:13 + h], U, ALU.mult, ALU.add)
                S.stt("dve", nZ[:, h, :], nZ[:, h, :], gst[:, slot, 12 + h:13 + h], un, ALU.mult, ALU.add)
                if tt < 7:
                    S.copy("act", Zbf.rearrange("p a b -> p (a b)"), Zst[:, h].rearrange("p a b -> p (a b)"))
                    S.copy("act", nbf, nZ[:, h, :])
                else:
                    co = Cn[0]
                    S.ts("dve", co.rearrange("p a b -> p (a b)"), Zst[:, h].rearrange("p a b -> p (a b)"),
                         emf[:, h:h + 1], None, ALU.mult)
                    S.dma("sp", CP[h].rearrange("(dc p) e -> p dc e", p=128), co)
                    no = sc(2)
                    S.ts("dve", no, nZ[:, h, :], emf[:, h:h + 1], None, ALU.mult)
                    S.dma("sp", nP[h:h + 1, :].rearrange("o (dc p) -> p (o dc)", p=128), no,
                          allow_slow_non_contiguous=True)

            def stC1(tt):
                slot = 8 + tt
                finish_tile(h, 128, bank(2 + tt % 2), bank(1)[:, 0:1], so2[tt % 2],
                            gst[:, slot, 8 + h:9 + h], bmb3[tt % 3])

            stA(0)
            for tt in range(8):
                if tt + 1 < 8:
                    stA(tt + 1)
                if tt >= 2:
                    bm_transpose(h, 128, (tt - 2) * 128, bmb3[(tt - 2) % 3])
                stB(tt)
                stC1(tt)
            bm_transpose(h, 128, 6 * 128, bmb3[6 % 3])
            bm_transpose(h, 128, 7 * 128, bmb3[7 % 3])

        for h in range(4):
            if h == 0:
                interleave(proj_gen(0), prec_gen(), every=1, lead=1)
                wload(wslots[2], w_in[:, COL_O:COL_O + 512])
                sample_scalars()
            else:
                interleave(proj_gen(h), sample_gen(h - 1), every=1, lead=2)
            if h > 0:
                sample_fin(h - 1)
            if h + 1 < 4:
                qkv_loads(h + 1)
            sample_prep(h)
            prompt_loop(h)
        for _ in sample_gen(3):
            pass
        sample_fin(3)

        if stop_after == "p1c":
            dbg = dout("dbg", [128, 16 * T // 2])
            S.dma("sp", dbg, bmT.rearrange("p a b -> p (a b)").bitcast(F32))
            S.emit()
            return nc

        B.p = m_h
        aT = B.alloc([128, 8, T], BF16)
        m_a = B.p
        uT = B.alloc([128, 8, HALO + T], F32)
        zT = B.alloc([128, 8, T], BF16)
        B.q = TOP
        wpg = B.alloc([128, 8, 256], BF16)
        posb = B.alloc([128, T], F32)
        ext = B.alloc([128, 8, 16, 20], F32)
        exa = B.alloc([128, 16, 20], F32)
        exb = B.alloc([128, 16, 20], F32)
        wslots = [B.alloc([128, 16, 256], BF16) for _ in range(2)]
        icnts = [B.alloc([128, NM], F32) for _ in range(2)]
        sa = B.alloc([128, HALO + NM], F32)
        sb = B.alloc([128, HALO + NM], F32)
        pso2 = sa[:, 0:1024]
        pso = sb[:, 0:1024]
        bufTc = B.alloc([128, 8, 240], F32)
        S.dma("sp", posb, pos.partition_broadcast(128).rearrange("p a b -> p (a b)"))
        S.dma("pool", wpg.rearrange("p (g k) n -> p g k n", k=2),
              w_pool.rearrange("g (k p) n -> p g k n", p=128))
        spf = spool.rearrange("s r c -> (s r) c")
        for blk, (r0, rn) in enumerate(((0, 128), (128, 112))):
            S.dma("sp", pso[0:rn, :], spf[r0:r0 + rn, :])
            for c in range(8):
                tp = bank(2 + (c % 2))
                S.tr(tp[:, 0:rn], pso[0:rn, c * 128:(c + 1) * 128], identF[0:rn, 0:rn])
                evac(bufTc[:, c, r0:r0 + rn], tp[:, 0:rn], eng="act")
        S.memset("dve", ext.rearrange("p a b c -> p (a b c)"), 0.0)

        def pool_chunk(c):
            g = c // 2
            wdw = (2, 4, 8, 16)[g]
            icnt = icnts[g % 2]
            if c % 2 == 0:
                S.ts("dve", icnt, posb[:, 0:NM], 1.0, float(wdw), ALU.add, ALU.min)
                S.add("dve", lambda e_, ic=icnt: e_.reciprocal(ic, ic), w=[icnt], r=[icnt])
            S.copy("dve", ext[:, c, :, 0:15], bufTc[:, c, :].rearrange("p (s r) -> p s r", r=15))
            S.copy("dve", ext[:, c, :, 15:19],
                   uT[:, c, HALO + NM: HALO + T].rearrange("p (s j) -> p s j", j=4))
            cur = uT[:, c, 0:HALO + NM]
            step = 1
            bufs = [sa, sb]
            bi = 0
            while step < wdw:
                nxt = bufs[bi]
                bi ^= 1
                S.tt("dve", nxt[:, step:], cur[:, step:], cur[:, 0:HALO + NM - step], ALU.add)
                cur = nxt
                step *= 2
            mean = bufs[bi]
            S.tt("dve", mean[:, 0:NM], cur[:, HALO:], icnt, ALU.mult)
            S.tt("dve", zT[:, c, 0:NM], mean[:, 0:NM], uT[:, c, HALO:HALO + NM], ALU.subtract)
            cur = ext[:, c]
            step = 1
            bufs = [exa, exb]
            bi = 0
            while step < wdw:
                nxt = bufs[bi]
                bi ^= 1
                S.tt("dve", nxt[:, :, step:19], cur[:, :, step:19], cur[:, :, 0:19 - step], ALU.add)
                cur = nxt
                step *= 2
            S.stt("dve", zT[:, c, NM:T].rearrange("p (s j) -> p s j", j=4), cur[:, :, 15:19], 1.0 / wdw,
                  ext[:, c, :, 15:19], ALU.mult, ALU.subtract)

        for half in range(4):
            wu = wslots[half % 2]
            wload(wu, w_in[:, COL_U + half * 256: COL_U + (half + 1) * 256])
            for c4 in range(2):
                c = half * 2 + c4
                for (t0, n) in ((0, 512), (512, 512), (1024, HALO + T - 1024)):
                    b_ = bank(next(pb2))
                    for kc in range(16):
                        S.mm(b_[:, 0:n], wu[:, kc, c4 * 128:(c4 + 1) * 128], hT[:, kc, t0:t0 + n], kc == 0, kc == 15)
                    evac(uT[:, c, t0:t0 + n], b_[:, 0:n], eng="act")
                pool_chunk(c)
        for c in range(8):
            g = c // 2
            for (t0, n) in ((0, 512), (512, 512), (1024, 64)):
                b_ = bank(next(pb))
                for k in range(2):
                    S.mm(b_[:, 0:n], wpg[:, 2 * g + k, (c % 2) * 128:(c % 2) * 128 + 128], zT[:, 2 * g + k, t0:t0 + n],
                         k == 0, k == 1)
                S.add("act", lambda e_, c=c, t0=t0, n=n, b_=b_: e_.mul(aT[:, c, t0:t0 + n], b_[:, 0:n], spT[:, c:c + 1]),
                      w=[aT[:, c, t0:t0 + n]], r=[b_[:, 0:n], spT[:, c:c + 1]])
        for c in range(8):
            tp = bank(next(pb))
            S.tr(tp[0:15, 0:128], uT[:, c, HALO + NM - 15: HALO + NM], identF)
            evac(pso[0:15, c * 128:(c + 1) * 128], tp[0:15, 0:128])
        S.dma("sp", poolP, pso[0:15, :])
        extr = bufTc.rearrange("p c (s r) -> p c s r", r=15)
        for c in range(8):
            S.copy("pool", extr[:, c], ext[:, c, :, 4:19])
        for blk, (r0, rn) in enumerate(((0, 128), (128, 112))):
            dst = pso if blk == 0 else pso2
            for c in range(8):
                tp = bank(next(pb))
                S.tr(tp[0:rn, 0:128], extr[:, c].rearrange("p s r -> p (s r)")[:, r0:r0 + rn], identF)
                evac(dst[0:rn, c * 128:(c + 1) * 128], tp[0:rn, 0:128])
            S.dma("sp", poolS[r0:r0 + rn, :], dst[0:rn, :])

        if stop_after == "p1b":
            dbg = dout("dbg", [128, 8 * T // 2])
            S.dma("sp", dbg, aT.rearrange("p a b -> p (a b)").bitcast(F32))
            S.emit()
            return nc

        B.p = m_a
        B.q = TOP
        mgT = B.hi([128, 16, T], BF16)
        wo_ = [B.hi([128, 16, 512], BF16) for _ in range(2)]
        wsl = [[B.alloc([128, 16, 128], BF16), B.alloc([128, 16, 128], BF16), B.alloc([128, 8, 128], BF16),
                B.alloc([128, 16, 128], BF16)] for _ in range(2)]
        sga = [B.alloc([128, 512], F32) for _ in range(2)]
        sgb = [B.alloc([128, 512], F32) for _ in range(2)]
        t1 = [B.alloc([128, 512], F32) for _ in range(2)]
        it = 0
        for c in range(16):
            wga, wgb, wpa_, wpb_ = wsl[c % 2]
            wload(wga, w_in[:, COL_GA + c * 128: COL_GA + (c + 1) * 128])
            wload(wgb, w_in[:, COL_GB + c * 128: COL_GB + (c + 1) * 128])
            wload(wpa_, w_pa[:, c * 128:(c + 1) * 128])
            wload(wpb_, w_pb[:, c * 128:(c + 1) * 128])
            if c == 1:
                for nn in range(2):
                    wload(wo_[nn], w_out[:, nn * 512:(nn + 1) * 512])
            for (t0, n) in ((0, 512), (512, 512), (1024, 64)):
                bb = (it % 2) * 4
                it += 1
                pga, pgb, pya, pyb = bank(bb), bank(bb + 1), bank(bb + 2), bank(bb + 3)
                for kc in range(16):
                    S.mm(pga[:, 0:n], wga[:, kc, :], hT[:, kc, HALO + t0:HALO + t0 + n], kc == 0, kc == 15)
                for kc in range(16):
                    S.mm(pgb[:, 0:n], wgb[:, kc, :], hT[:, kc, HALO + t0:HALO + t0 + n], kc == 0, kc == 15)
                for kc in range(8):
                    S.mm(pya[:, 0:n], wpa_[:, kc, :], aT[:, kc, t0:t0 + n], kc == 0, kc == 7)
                for kc in range(16):
                    S.mm(pyb[:, 0:n], wpb_[:, kc, :], bmT[:, kc, t0:t0 + n], kc == 0, kc == 15)
                j = it % 2
                S.act(sga[j][:, 0:n], pga[:, 0:n], AF.Sigmoid)
                S.act(sgb[j][:, 0:n], pgb[:, 0:n], AF.Sigmoid)
                S.tt("dve", sga[j][:, 0:n], sga[j][:, 0:n], pya[:, 0:n], ALU.mult)
                S.tt("dve", t1[j][:, 0:n], sgb[j][:, 0:n], pyb[:, 0:n], ALU.mult)
                S.tt("dve", mgT[:, c, t0:t0 + n], sga[j][:, 0:n], t1[j][:, 0:n], ALU.add)

        B.p = m_phase
        h2T = B.alloc([128, 16, T], BF16)
        m_h2 = B.p
        wo_ = wo_ + [B.alloc([128, 16, 512], BF16) for _ in range(2)]
        for nn in range(2, 4):
            wload(wo_[nn], w_out[:, nn * 512:(nn + 1) * 512])
        g1 = B.alloc([128, DM], F32)
        g2 = B.alloc([128, DM], F32)
        S.dma("sp", g1, g_post_mix.partition_broadcast(128).rearrange("p a b -> p (a b)"))
        S.dma("sp", g2, g_pre_ffn.partition_broadcast(128).rearrange("p a b -> p (a b)"))
        xt2 = [B.alloc([128, DM], F32) for _ in range(2)]
        tb = B.alloc([128, DM], F32)
        hb = [B.alloc([128, DM], BF16) for _ in range(2)]
        jk = B.alloc([128, 512], BF16)
        def mm2(tt):
            n = 128 if tt < 8 else 64
            t0 = tt * 128
            S.dma("sp", xt2[tt % 2][0:n], x[NPFX + t0: NPFX + t0 + n, :])
            for nn in range(4):
                b_ = bank((tt % 2) * 4 + nn)
                for kc in range(16):
                    S.mm(b_[0:n, :], mgT[:, kc, t0:t0 + n], wo_[nn][:, kc, :], kc == 0, kc == 15)

        def post2(tt):
            n = 128 if tt < 8 else 64
            t0 = tt * 128
            xt = xt2[tt % 2]
            bb = (tt % 2) * 4
            sst = sc(4)
            S.memset("dve", sst[0:n], 0.0)
            for nn in range(4):
                S.act(jk[0:n], bank(bb + nn)[0:n, :], AF.Square, accum_out=sst[0:n, nn:nn + 1])
            ss = sc()
            S.add("dve", lambda e_, ss=ss, sst=sst, n=n: e_.reduce_sum(ss[0:n], sst[0:n], AX.X), w=[ss[0:n]], r=[sst[0:n]])
            rstd = rsqrt(ss, 1.0 / DM, n)
            for nn in range(4):
                S.stt("dve", tb[0:n, nn * 512:(nn + 1) * 512], bank(bb + nn)[0:n, :], rstd[0:n],
                      g1[0:n, nn * 512:(nn + 1) * 512], ALU.mult, ALU.mult)
            S.tt("dve", xt[0:n], xt[0:n], tb[0:n], ALU.add)
            S.dma("pool", x1s[t0:t0 + n, :], xt[0:n])
            ss2 = sumsq(hb[tt % 2][0:n], xt[0:n], n)
            rstd2 = rsqrt(ss2, 1.0 / DM, n)
            S.stt("dve", hb[tt % 2][0:n], xt[0:n], rstd2[0:n], g2[0:n], ALU.mult, ALU.mult)
            pT3 = ps[:, bb * 512: bb * 512 + 1024].bitcast(BF16).rearrange("p (a b) -> p a b", a=16)
            for kc in range(16):
                S.tr(pT3[:, kc, 0:n], hb[tt % 2][0:n, kc * 128:(kc + 1) * 128], identB[0:n, 0:n])
            evac(h2T[:, :, t0:t0 + n], pT3[:, :, 0:n], eng="act")

        mm2(0)
        for tt in range(9):
            if tt + 1 < 9:
                mm2(tt + 1)
            post2(tt)

        B.p = m_h2
        B.q = TOP
        actT = B.hi([128, 44, T], BF16)
        wgu = [[B.alloc([128, 16, 128], BF16), B.alloc([128, 16, 128], BF16)] for _ in range(3)]
        sil = [B.alloc([128, 512], F32) for _ in range(2)]
        it = 0
        for fc in range(44):
            wg_, wu_ = wgu[fc % 3]
            wload(wg_, w_gate[:, fc * 128:(fc + 1) * 128])
            wload(wu_, w_up[:, fc * 128:(fc + 1) * 128])
            for (t0, n) in ((0, 512), (512, 512), (1024, 64)):
                bb = (it % 4) * 2
                it += 1
                pg, pu = bank(bb), bank(bb + 1)
                for kc in range(16):
                    S.mm(pg[:, 0:n], wg_[:, kc, :], h2T[:, kc, t0:t0 + n], kc == 0, kc == 15)
                for kc in range(16):
                    S.mm(pu[:, 0:n], wu_[:, kc, :], h2T[:, kc, t0:t0 + n], kc == 0, kc == 15)
                sl = sil[it % 2]
                S.act(sl[:, 0:n], pg[:, 0:n], AF.Silu)
                S.tt("dve", actT[:, fc, t0:t0 + n], sl[:, 0:n], pu[:, 0:n], ALU.mult)

        B.p = m_phase
        wd = [B.alloc([128, 44, 256], BF16) for _ in range(2)]
        fe = [B.alloc([128, 256], F32) for _ in range(4)]
        jkf = B.alloc([128, 256], BF16)
        ssf = B.alloc([128, 9, 8], F32)
        rstd5 = B.alloc([128, 16], F32)
        S.memset("dve", ssf.rearrange("p a b -> p (a b)"), 0.0)
        m_p4 = B.p
        wpg_ = [B.alloc([128, 16, 512], BF16) for _ in range(3)]
        m_p4e = B.p
        it = 0
        for nn in range(8):
            w_ = wd[nn % 2]
            v = w_down[:, nn * 256:(nn + 1) * 256].rearrange("(k p) n -> p k n", p=128)
            for k0 in range(0, 44, 11):
                S.dma("pool", w_[:, k0:k0 + 11, :], v[:, k0:k0 + 11, :])
            if nn == 1:
                for j in range(3):
                    wload(wpg_[j], w_ple_gate[:, j * 512:(j + 1) * 512])
            for tt in range(9):
                n = 128 if tt < 8 else 64
                t0 = tt * 128
                b_ = bank(it % 8)
                f_ = fe[it % 4]
                it += 1
                for kc in range(44):
                    S.mm(b_[0:n, 0:256], actT[:, kc, t0:t0 + n], w_[:, kc, :], kc == 0, kc == 43)
                S.act(jkf[0:n], b_[0:n, 0:256], AF.Square, accum_out=ssf[0:n, tt, nn:nn + 1])
                evac(f_[0:n], b_[0:n, 0:256], eng="dve")
                S.dma("sp", fsc[t0:t0 + n, nn * 256:(nn + 1) * 256], f_[0:n])

        B.p = m_phase
        B.q = TOP
        ft = [B.alloc([128, DM], F32) for _ in range(3)]
        x1t = [B.alloc([128, DM], F32) for _ in range(3)]
        assert B.p <= m_p4
        B.p = m_p4e
        wpl = B.alloc([128, 2, DM], BF16)
        wload(wpl, w_ple)
        wpg_ = wpg_ + [B.alloc([128, 16, 512], BF16)]
        wload(wpg_[3], w_ple_gate[:, 3 * 512:4 * 512])
        g3 = B.alloc([128, DM], F32)
        S.dma("sp", g3, g_post_ffn.partition_broadcast(128).rearrange("p a b -> p (a b)"))
        x2b = [B.alloc([128, DM], BF16) for _ in range(2)]
        x2T = [B.alloc([128, 16, 128], BF16) for _ in range(2)]
        pt = [B.alloc([128, 256], F32) for _ in range(2)]
        ptb = [B.alloc([128, 256], BF16) for _ in range(2)]
        pT = [B.alloc([128, 2, 128], BF16) for _ in range(2)]
        sg = [B.alloc([128, 1024], F32) for _ in range(2)]
        yt = [B.alloc([128, DM], F32) for _ in range(2)]
        jk5 = B.alloc([128, DM], BF16)

        ss5 = sc(9)
        S.add("dve", lambda e_: e_.reduce_sum(ss5, ssf, AX.X), w=[ss5], r=[ssf])
        t5a, t5b = sc(9), sc(9)
        S.ts("dve", t5a, ss5, 1.0 / DM, EPS, ALU.mult, ALU.add)
        S.act(t5b, t5a, AF.Ln)
        S.act(rstd5[:, 0:9], t5b, AF.Exp, scale=-0.5)

        def pre5(tt):
            n = 128 if tt < 8 else 64
            t0 = tt * 128
            f_, x1_ = ft[tt % 3], x1t[tt % 3]
            S.dma("sp", f_[0:n], fsc[t0:t0 + n, :])
            S.dma("sp", x1_[0:n], x1s[t0:t0 + n, :])
            S.dma("sp", pt[tt % 2][0:n], p_in[t0:t0 + n, :])
            S.stt("dve", f_[0:n], f_[0:n], rstd5[0:n, tt:tt + 1], g3[0:n], ALU.mult, ALU.mult)
            S.tt("dve", x1_[0:n], x1_[0:n], f_[0:n], ALU.add)
            S.copy("pool", x2b[tt % 2][0:n], x1_[0:n])
            S.copy("pool", ptb[tt % 2][0:n], pt[tt % 2][0:n])

        def preT5(tt):
            n = 128 if tt < 8 else 64
            pT3 = ps[:, 0:1024].bitcast(BF16).rearrange("p (a b) -> p a b", a=16)
            for kc in range(16):
                S.tr(pT3[:, kc, 0:n], x2b[tt % 2][0:n, kc * 128:(kc + 1) * 128], identB[0:n, 0:n])
            evac(x2T[tt % 2][:, :, 0:n], pT3[:, :, 0:n], eng="act")
            pP3 = ps[:, 1024:1536].bitcast(BF16).rearrange("p (a b) -> p a b", b=128)
            for k in range(2):
                S.tr(pP3[:, k, 0:n], ptb[tt % 2][0:n, k * 128:(k + 1) * 128], identB[0:n, 0:n])
            evac(pT[tt % 2][:, :, 0:n], pP3[:, 0:2, 0:n], eng="dve")

        def main5(tt, between=None):
            n = 128 if tt < 8 else 64
            t0 = tt * 128
            x1_ = x1t[tt % 3]
            y_ = yt[tt % 2]
            for hh in range(2):
                sg_ = sg[hh]
                for q in range(2):
                    nn = hh * 2 + q
                    bg, bp = bank(4 + q), bank(6 + q)
                    for kc in range(16):
                        S.mm(bg[0:n, :], x2T[tt % 2][:, kc, 0:n], wpg_[nn][:, kc, :], kc == 0, kc == 15)
                    for k in range(2):
                        S.mm(bp[0:n, :], pT[tt % 2][:, k, 0:n], wpl[:, k, nn * 512:(nn + 1) * 512], k == 0, k == 1)
                if hh == 1 and between is not None:
                    between()
                for q in range(2):
                    bg, bp = bank(4 + q), bank(6 + q)
                    S.act(sg_[0:n, q * 512:(q + 1) * 512], bg[0:n, :], AF.Sigmoid)
                    S.tt("dve", sg_[0:n, q * 512:(q + 1) * 512], sg_[0:n, q * 512:(q + 1) * 512], bp[0:n, :], ALU.mult)
                S.tt("dve", y_[0:n, hh * 1024:(hh + 1) * 1024], sg_[0:n], x1_[0:n, hh * 1024:(hh + 1) * 1024], ALU.add)
            S.dma("pool", y_out[t0:t0 + n, :], y_[0:n])

        pre5(0)
        preT5(0)
        for tt in range(9):
            if tt + 1 < 9:
                pre5(tt + 1)
                main5(tt, between=lambda tt=tt: preT5(tt + 1))
            else:
                main5(tt)

        S.emit()
    return nc


_NC_CACHE = {}


def _consts():
    i = np.arange(128)
    tri = (i[:, None] <= i[None, :]).astype(np.float32)
    j = np.arange(64)
    same = (j[:, None] // 4) == (j[None, :] // 4)
    triS = (same & (j[:, None] <= j[None, :])).astype(np.float32)
    onesS = same.astype(np.float32)
    sel = ((j[:, None] // 4) == np.arange(16)[None, :]).astype(np.float32)
    return {
        "cI": np.eye(128, dtype=np.float32), "cTri": tri, "cTriS": triS, "cOnesS": onesS,
        "cSel": sel, "cSelT": np.ascontiguousarray(sel.T),
        "cMq": np.ascontiguousarray(sel.T).reshape(1, 1024),
    }


def kernel(x_prompt, x_sample, p_prompt, p_sample, state_pool, state_C, state_n, state_m,
           g_pre_mix, w_in, b_i, b_f, w_pool_grp, s_pool, g_head, w_pa, w_pb, w_out,
           g_post_mix, g_pre_ffn, w_gate, w_up, w_down, g_post_ffn, w_ple, w_ple_gate):
    f32 = lambda a: np.ascontiguousarray(np.asarray(a, dtype=np.float32))
    x_prompt, x_sample = f32(x_prompt), f32(x_sample)
    p_prompt, p_sample = f32(p_prompt)[0], f32(p_sample)[0]
    state_pool, state_C, state_n, state_m = f32(state_pool)[0], f32(state_C)[0], f32(state_n)[0], f32(state_m)[0]
    shared = {
        "g_pre_mix": f32(g_pre_mix), "w_in": f32(w_in)[0], "b_i": f32(b_i), "b_f": f32(b_f),
        "w_pool_grp": f32(w_pool_grp)[0], "s_pool": f32(s_pool), "g_head": f32(g_head).reshape(1, 2048),
        "w_pa": f32(w_pa)[0], "w_pb": f32(w_pb)[0], "w_out": f32(w_out)[0], "g_post_mix": f32(g_post_mix),
        "g_pre_ffn": f32(g_pre_ffn), "w_gate": f32(w_gate)[0], "w_up": f32(w_up)[0], "w_down": f32(w_down)[0],
        "g_post_ffn": f32(g_post_ffn), "w_ple": f32(w_ple)[0], "w_ple_gate": f32(w_ple_gate)[0],
    }
    shared.update(_consts())
    in_maps = []
    for c in range(8):
        b, j = c // 2, c % 2
        xs = x_sample[16 * c:16 * c + 16].reshape(64, DM)
        xm = x_prompt[b, j * 1024:(j + 1) * 1024]
        xp = x_prompt[b, 0:1024] if j == 1 else np.zeros((1024, DM), np.float32)
        posv = np.concatenate([np.arange(j * 1024, (j + 1) * 1024), 16384 + np.tile(np.arange(4), 16)])
        m = dict(shared)
        m.update({
            "x": np.ascontiguousarray(np.concatenate([xp, xm, xs], 0)),
            "p": np.ascontiguousarray(np.concatenate([p_prompt[b, j * 1024:(j + 1) * 1024],
                                                      p_sample[16 * c:16 * c + 16].reshape(64, 256)], 0)),
            "pos": posv.astype(np.float32).reshape(1, T),
            "spool": np.ascontiguousarray(state_pool[16 * c:16 * c + 16]),
            "sC": np.ascontiguousarray(state_C[16 * c:16 * c + 16]),
            "sn": np.ascontiguousarray(state_n[16 * c:16 * c + 16]),
            "sm": np.ascontiguousarray(state_m[16 * c:16 * c + 16]).reshape(1, 64),
        })
        in_maps.append(m)
    if "nc" not in _NC_CACHE:
        _NC_CACHE["nc"] = build()
    res = run_bass_kernel_spmd(_NC_CACHE["nc"], in_maps, core_ids=list(range(8)))
    R = res.results
    yp = np.zeros((4, 2048, DM), np.float32)
    ys = np.zeros((128, 4, DM), np.float32)
    pool_p = np.zeros((1, 4, 15, 1024), np.float32)
    C_p = np.zeros((1, 4, 4, 256, 512), np.float32)
    n_p = np.zeros((1, 4, 4, 256), np.float32)
    m_p = np.zeros((1, 4, 4), np.float32)
    pool_s = np.zeros((1, 128, 15, 1024), np.float32)
    C_s = np.zeros((1, 128, 4, 256, 512), np.float32)
    n_s = np.zeros((1, 128, 4, 256), np.float32)
    m_s = np.zeros((1, 128, 4), np.float32)
    for c in range(8):
        b, j = c // 2, c % 2
        r = R[c]
        yp[b, j * 1024:(j + 1) * 1024] = r["y"][0:1024]
        ys[16 * c:16 * c + 16] = r["y"][1024:1088].reshape(16, 4, DM)
        if j == 1:
            pool_p[0, b] = r["poolP"]
            C_p[0, b] = r["CP"]
            n_p[0, b] = r["nP"]
            m_p[0, b] = r["mP"].reshape(4)
        pool_s[0, 16 * c:16 * c + 16] = r["poolS"].reshape(16, 15, 1024)
        C_s[0, 16 * c:16 * c + 16] = r["CS"]
        n_s[0, 16 * c:16 * c + 16] = r["nS"]
        m_s[0, 16 * c:16 * c + 16] = r["mS"]
    return (yp, ys, pool_p, C_p, n_p, m_p, pool_s, C_s, n_s, m_s)
```
